# Optimizing a Trainium2 kernel written in Bass

```python
import jax
import jax.numpy as jnp
from jax import lax
import numpy as np

D_MODEL = 2048
BATCH = 2
SEQ = 4096
DEPTH = 4
DEC_BATCH = 8
DEC_SEQ = 4
PAST_LEN = 16384
PAGE_SIZE = 128

N_HEADS = 16
HEAD_DIM = D_MODEL // N_HEADS
ATTN_WIDTH = N_HEADS * HEAD_DIM
DILATED_GROUPS = ((128, 1), (512, 4), (2048, 16))
N_GROUPS = len(DILATED_GROUPS)
BLK = 128
CONV_WIDTH = 31
CONV_CH = D_MODEL
N_ATTN_LAYERS = (DEPTH + 1) // 2
N_CONV_LAYERS = DEPTH // 2
RMS_EPS = 1e-6
LN_EPS = 1e-5
NEG_INF = -1e30
ALIBI_MAX_EXP = 8.0

kernel_name = 'dilated_alibi_conformer_hybrid_step'


def rms_norm(x, gain):
    xf = x.astype(jnp.float32)
    y = xf * lax.rsqrt(jnp.mean(xf * xf, axis=-1, keepdims=True) + RMS_EPS)
    return (y * gain.astype(jnp.float32)).astype(x.dtype)


def layer_norm(x, gain, bias):
    xf = x.astype(jnp.float32)
    xc = xf - jnp.mean(xf, axis=-1, keepdims=True)
    y = xc * lax.rsqrt(jnp.mean(xc * xc, axis=-1, keepdims=True) + LN_EPS)
    return (y * gain.astype(jnp.float32) + bias.astype(jnp.float32)).astype(x.dtype)


def alibi_slopes():
    n = N_GROUPS * N_HEADS
    s = np.exp2(-ALIBI_MAX_EXP * np.arange(1, n + 1) / n).astype(np.float32)
    return jnp.asarray(s.reshape(N_GROUPS, N_HEADS))


def attn_project(h, w_in, q_gain, k_gain):
    lead = h.shape[:2]
    proj = h @ w_in
    n_qkv = 3 * N_GROUPS * ATTN_WIDTH
    qkv = proj[..., :n_qkv].reshape(*lead, N_GROUPS, 3, N_HEADS, HEAD_DIM)
    z = proj[..., n_qkv:]
    scale = HEAD_DIM ** -0.5
    qs = [rms_norm(qkv[:, :, g, 0], q_gain[g]) * scale for g in range(N_GROUPS)]
    ks = [rms_norm(qkv[:, :, g, 1], k_gain[g]) for g in range(N_GROUPS)]
    vs = [qkv[:, :, g, 2] for g in range(N_GROUPS)]
    return qs, ks, vs, z


def dilated_group_prompt(q, k, v, dil, n_off, slopes):
    bsz, s_len, nh, hd = q.shape
    L = s_len // dil
    nb = -(-L // BLK)
    Lp = nb * BLK

    def to_blocks(t):
        t = t.reshape(bsz, L, dil, nh, hd).transpose(0, 2, 1, 3, 4)
        t = jnp.pad(t, ((0, 0), (0, 0), (0, Lp - L), (0, 0), (0, 0)))
        return t.reshape(bsz, dil, nb, BLK, nh, hd)

    def with_prev(t):
        prev = jnp.pad(t, ((0, 0), (0, 0), (1, 0), (0, 0), (0, 0), (0, 0)))[:, :, :nb]
        return jnp.concatenate([prev, t], axis=3)

    qb = to_blocks(q)
    kb = with_prev(to_blocks(k))
    vb = with_prev(to_blocks(v))
    qi = jnp.arange(BLK)[:, None]
    ci = jnp.arange(2 * BLK)[None, :]
    off = qi + BLK - ci
    in_band = (off >= 0) & (off <= n_off)
    has_prev = (jnp.arange(nb)[:, None, None] > 0) | (ci[None] >= BLK)
    mask = in_band[None] & has_prev
    dist = (dil * jnp.maximum(off, 0)).astype(jnp.float32)
    bias = -slopes[:, None, None] * dist[None]
    s = jnp.einsum('brnqhe,brnkhe->brnhqk', qb, kb, preferred_element_type=jnp.float32)
    s = jnp.where(mask[None, None, :, None], s + bias, NEG_INF)
    m = jnp.max(s, axis=-1)
    p = jnp.exp(s - m[..., None])
    l = jnp.sum(p, axis=-1)
    acc = jnp.einsum('brnhqk,brnkhe->brnqhe', p, vb.astype(jnp.float32))
    acc = acc.reshape(bsz, dil, Lp, nh, hd)[:, :, :L].transpose(0, 2, 1, 3, 4).reshape(bsz, s_len, nh, hd)

    def stat(t):
        t = t.transpose(0, 1, 2, 4, 3).reshape(bsz, dil, Lp, nh)[:, :, :L]
        return t.transpose(0, 2, 1, 3).reshape(bsz, s_len, nh)

    return acc, stat(m), stat(l)


def dilated_group_sample(q, k_new, v_new, kv_buf, dil, n_off, slopes):
    t_len = q.shape[1]
    lb = kv_buf.shape[1]
    kk = jnp.concatenate([kv_buf[:, :, 0], k_new], axis=1)
    vv = jnp.concatenate([kv_buf[:, :, 1], v_new], axis=1)
    offs = jnp.arange(n_off + 1)
    idx = lb + jnp.arange(t_len)[:, None] - dil * offs[None, :]
    valid = idx >= 0
    idx = jnp.maximum(idx, 0)
    kg = kk[:, idx]
    vg = vv[:, idx]
    bias = -slopes[:, None] * (dil * offs).astype(jnp.float32)[None, :]
    s = jnp.einsum('bthe,btkhe->bthk', q, kg, preferred_element_type=jnp.float32)
    s = jnp.where(valid[None, :, None, :], s + bias[None, None], NEG_INF)
    m = jnp.max(s, axis=-1)
    p = jnp.exp(s - m[..., None])
    l = jnp.sum(p, axis=-1)
    acc = jnp.einsum('bthk,btkhe->bthe', p, vg.astype(jnp.float32))
    new_buf = jnp.stack([kk[:, t_len:], vv[:, t_len:]], axis=2)
    return acc, m, l, new_buf


def merge_groups(accs, ms, ls, dtype):
    m_all = jnp.stack(ms)
    w = jnp.exp(m_all - jnp.max(m_all, axis=0, keepdims=True))
    num = jnp.sum(w[..., None] * jnp.stack(accs), axis=0)
    den = jnp.sum(w * jnp.stack(ls), axis=0)
    o = num / den[..., None]
    return o.reshape(*o.shape[:2], ATTN_WIDTH).astype(dtype)


def attn_mixer_prompt(h, w_in, q_gain, k_gain, w_out, slopes):
    s_len = h.shape[1]
    qs, ks, vs, z = attn_project(h, w_in, q_gain, k_gain)
    accs, ms, ls, new_kv = [], [], [], []
    for g, (win, dil) in enumerate(DILATED_GROUPS):
        acc, m, l = dilated_group_prompt(qs[g], ks[g], vs[g], dil, win // dil, slopes[g])
        accs.append(acc)
        ms.append(m)
        ls.append(l)
        keep = min(win, s_len)
        new_kv.append(jnp.stack([ks[g][:, s_len - keep:], vs[g][:, s_len - keep:]], axis=2))
    o = merge_groups(accs, ms, ls, h.dtype)
    return (o * jax.nn.silu(z)) @ w_out, new_kv


def attn_mixer_sample(h, kv_bufs, w_in, q_gain, k_gain, w_out, slopes):
    qs, ks, vs, z = attn_project(h, w_in, q_gain, k_gain)
    accs, ms, ls, new_kv = [], [], [], []
    for g, (win, dil) in enumerate(DILATED_GROUPS):
        acc, m, l, nb = dilated_group_sample(qs[g], ks[g], vs[g], kv_bufs[g], dil, win // dil, slopes[g])
        accs.append(acc)
        ms.append(m)
        ls.append(l)
        new_kv.append(nb)
    o = merge_groups(accs, ms, ls, h.dtype)
    return (o * jax.nn.silu(z)) @ w_out, new_kv


def conv_mixer(h, buf, w_in, dw_w, dw_b, ln_g, ln_b, w_out):
    a, b, z = jnp.split(h @ w_in, 3, axis=-1)
    u = a * jax.nn.sigmoid(b)
    up = jnp.concatenate([buf.astype(u.dtype), u], axis=1)
    c = lax.conv_general_dilated(up, dw_w[:, None, :], window_strides=(1,), padding='VALID',
                                 dimension_numbers=('NWC', 'WIO', 'NWC'),
                                 feature_group_count=CONV_CH) + dw_b
    c = jax.nn.silu(layer_norm(c, ln_g, ln_b))
    return (c * jax.nn.silu(z)) @ w_out, up[:, up.shape[1] - (CONV_WIDTH - 1):]


def setup_inputs(seed: int = 0) -> dict:
    key = jax.random.key(seed)
    ks = jax.random.split(key, 20)
    f32 = jnp.float32

    def nrm(k, shape, scale):
        return jax.random.normal(k, shape, f32) * scale

    n_in_attn = 3 * N_GROUPS * ATTN_WIDTH + ATTN_WIDTH
    return {
        'x_prompt': nrm(ks[0], (BATCH, SEQ, D_MODEL), 1.0),
        'x_sample': nrm(ks[1], (DEC_BATCH, DEC_SEQ, D_MODEL), 1.0),
        'cache_kv_w128': nrm(ks[2], (N_ATTN_LAYERS, DEC_BATCH, min(DILATED_GROUPS[0][0], PAST_LEN), 2, N_HEADS, HEAD_DIM), 1.0),
        'cache_kv_w512': nrm(ks[3], (N_ATTN_LAYERS, DEC_BATCH, min(DILATED_GROUPS[1][0], PAST_LEN), 2, N_HEADS, HEAD_DIM), 1.0),
        'cache_kv_w2048': nrm(ks[4], (N_ATTN_LAYERS, DEC_BATCH, min(DILATED_GROUPS[2][0], PAST_LEN), 2, N_HEADS, HEAD_DIM), 1.0),
        'state_conv': nrm(ks[5], (N_CONV_LAYERS, DEC_BATCH, CONV_WIDTH - 1, CONV_CH), 0.5),
        'attn_norm': 1.0 + nrm(ks[6], (N_ATTN_LAYERS, D_MODEL), 0.05),
        'attn_w_in': nrm(ks[7], (N_ATTN_LAYERS, D_MODEL, n_in_attn), D_MODEL ** -0.5),
        'attn_q_gain': 1.0 + nrm(ks[8], (N_ATTN_LAYERS, N_GROUPS, HEAD_DIM), 0.05),
        'attn_k_gain': 1.0 + nrm(ks[9], (N_ATTN_LAYERS, N_GROUPS, HEAD_DIM), 0.05),
        'attn_w_out': nrm(ks[10], (N_ATTN_LAYERS, ATTN_WIDTH, D_MODEL), 0.5 * ATTN_WIDTH ** -0.5),
        'conv_norm': 1.0 + nrm(ks[11], (N_CONV_LAYERS, D_MODEL), 0.05),
        'conv_w_in': nrm(ks[12], (N_CONV_LAYERS, D_MODEL, 3 * CONV_CH), D_MODEL ** -0.5),
        'conv_dw_w': nrm(ks[13], (N_CONV_LAYERS, CONV_WIDTH, CONV_CH), CONV_WIDTH ** -0.5),
        'conv_dw_b': nrm(ks[14], (N_CONV_LAYERS, CONV_CH), 0.02),
        'conv_ln_g': 1.0 + nrm(ks[15], (N_CONV_LAYERS, CONV_CH), 0.05),
        'conv_ln_b': nrm(ks[16], (N_CONV_LAYERS, CONV_CH), 0.02),
        'conv_w_out': nrm(ks[17], (N_CONV_LAYERS, CONV_CH, D_MODEL), 0.5 * CONV_CH ** -0.5),
    }


def reference(x_prompt, x_sample, cache_kv_w128, cache_kv_w512, cache_kv_w2048, state_conv,
              attn_norm, attn_w_in, attn_q_gain, attn_k_gain, attn_w_out,
              conv_norm, conv_w_in, conv_dw_w, conv_dw_b, conv_ln_g, conv_ln_b, conv_w_out):
    slopes = alibi_slopes()
    caches = (cache_kv_w128, cache_kv_w512, cache_kv_w2048)
    xp, xs = x_prompt, x_sample
    kv_p = [[] for _ in range(N_GROUPS)]
    kv_s = [[] for _ in range(N_GROUPS)]
    conv_p, conv_s = [], []
    for i in range(DEPTH):
        j = i // 2
        if i % 2 == 0:
            hp = rms_norm(xp, attn_norm[j])
            hs = rms_norm(xs, attn_norm[j])
            yp, new_p = attn_mixer_prompt(hp, attn_w_in[j], attn_q_gain[j], attn_k_gain[j], attn_w_out[j], slopes)
            ys, new_s = attn_mixer_sample(hs, [c[j] for c in caches], attn_w_in[j], attn_q_gain[j],
                                          attn_k_gain[j], attn_w_out[j], slopes)
            for g in range(N_GROUPS):
                kv_p[g].append(new_p[g])
                kv_s[g].append(new_s[g])
        else:
            hp = rms_norm(xp, conv_norm[j])
            hs = rms_norm(xs, conv_norm[j])
            zero_buf = jnp.zeros((xp.shape[0], CONV_WIDTH - 1, CONV_CH), xp.dtype)
            yp, cp = conv_mixer(hp, zero_buf, conv_w_in[j], conv_dw_w[j], conv_dw_b[j],
                                conv_ln_g[j], conv_ln_b[j], conv_w_out[j])
            ys, cs = conv_mixer(hs, state_conv[j], conv_w_in[j], conv_dw_w[j], conv_dw_b[j],
                                conv_ln_g[j], conv_ln_b[j], conv_w_out[j])
            conv_p.append(cp)
            conv_s.append(cs)
        xp = xp + yp
        xs = xs + ys
    kv_w128_prompt = jnp.stack(kv_p[0])
    kv_w512_prompt = jnp.stack(kv_p[1])
    kv_w2048_prompt = jnp.stack(kv_p[2])
    conv_prompt = jnp.stack(conv_p)
    kv_w128_sample = jnp.stack(kv_s[0])
    kv_w512_sample = jnp.stack(kv_s[1])
    kv_w2048_sample = jnp.stack(kv_s[2])
    conv_sample = jnp.stack(conv_s)
    return (xp, xs, kv_w128_prompt, kv_w512_prompt, kv_w2048_prompt, conv_prompt,
            kv_w128_sample, kv_w512_sample, kv_w2048_sample, conv_sample)
```

```python
import numpy as np
from contextlib import ExitStack
import concourse.bass as bass
import concourse.mybir as mybir
from concourse.bass_utils import run_bass_kernel_spmd

F32 = mybir.dt.float32
BF16 = mybir.dt.bfloat16
I32 = mybir.dt.int32
AF = mybir.ActivationFunctionType
ALU = mybir.AluOpType
AX = mybir.AxisListType

import os
STAGE = int(os.environ.get("KSTAGE", "7"))
NCORES = 8
D = 2048
KC = 16
NT = 1024
NS = 4
NTOK = NT + NS
TG = [(0, 512), (512, 512), (1024, 4)]
GROUPS = [(128, 1), (512, 4), (2048, 16)]
NTAIL = [128, 512, 1024]
NQKV = 3 * 3 * 2048
CW = 31
BIG = 1.0e6
NEG = -30000.0

PV_AN = 0
PV_CN = 32
PV_QG = 64
PV_KG = 70
PV_DWW = 76
PV_DWB = PV_DWW + 2 * 16 * CW
PV_LNG = PV_DWB + 32
PV_LNB = PV_LNG + 32
PV_EPS_RMS = PV_LNB + 32
PV_EPS_Q = PV_EPS_RMS + 1
PV_EPS_LN = PV_EPS_Q + 1
PV_LBM = PV_EPS_LN + 1
PV_HALO = PV_LBM + 3
PV_ZERO = PV_HALO + 1
NPV = PV_ZERO + 1
CF_D = 0
CF_DD = 256
CF_ID = 260
NCF = 260 + 128


def alibi_sd(g, h):
    n = 48
    i = g * 16 + h
    s = float(np.exp2(np.float32(-8.0) * np.float32(i + 1) / np.float32(n)).astype(np.float32))
    return s * GROUPS[g][1]


class Scope:
    def __init__(self, nc, specs):
        self.nc, self.specs = nc, specs

    _uid = [0]

    def __enter__(self):
        self.es = ExitStack()
        Scope._uid[0] += 1
        u = Scope._uid[0]
        return tuple(self.es.enter_context(self.nc.sbuf_tensor(f"{n}_{u}", list(sh), dt)) for (n, sh, dt) in self.specs)

    def __exit__(self, *a):
        self.es.close()
        return False


class Prog:
    ENGS = ("pe", "act", "dve", "pool", "sp")

    def __init__(self):
        self.q = {e: [] for e in self.ENGS}
        self.cnt = {e: 0 for e in self.ENGS}
        self.waited = {e: {} for e in self.ENGS}
        self.lastw = {}
        self.readers = {}
        self.dcnt = {}

    def _deps(self, eng, r, w):
        need = {}
        for b in r:
            for k, v in self.lastw.get(b, {}).items():
                if need.get(k, 0) < v:
                    need[k] = v
        for b in w:
            for k, v in self.lastw.get(b, {}).items():
                if need.get(k, 0) < v:
                    need[k] = v
            for k, v in self.readers.get(b, {}).items():
                if need.get(k, 0) < v:
                    need[k] = v
        waits = []
        wd = self.waited[eng]
        for k, v in need.items():
            if k == "E:pe" and eng == "pe":
                continue
            if k == "E:sp" and eng == "sp":
                continue
            if wd.get(k, 0) >= v:
                continue
            wd[k] = v
            waits.append((k, v))
        return waits

    def _record(self, tk, r, w):
        k, v = tk
        for b in r:
            d = self.readers.setdefault(b, {})
            if d.get(k, 0) < v:
                d[k] = v
        for b in w:
            self.lastw[b] = {k: v}
            self.readers[b] = {}

    def op(self, eng, fn, r=(), w=(), inc=True):
        waits = self._deps(eng, r, w)
        if inc:
            self.cnt[eng] += 1
            tk = ("E:" + eng, self.cnt[eng])
        else:
            tk = ("E:" + eng, self.cnt[eng] + 1)
        self._record(tk, r, w)
        self.q[eng].append((waits, fn, ("E:" + eng, 1) if inc else None))

    def dma(self, q, fn, r, w, sem, n=16):
        waits = self._deps(q, r, w)
        k = "D:" + q + ":" + sem
        self.dcnt[k] = self.dcnt.get(k, 0) + n
        self._record((k, self.dcnt[k]), r, w)
        self.q[q].append((waits, fn, (k, n)))

    def join(self, sem, keys, q="pool"):
        k = "D:" + q + ":" + sem
        for b in keys:
            self.lastw[b] = {k: self.dcnt[k]}

    def barrier(self):
        allk = {("E:" + e): self.cnt[e] for e in self.ENGS if self.cnt[e] > 0}
        allk.update(self.dcnt)
        for e in self.ENGS:
            waits = []
            for k, v in allk.items():
                if k == "E:" + e:
                    continue
                if self.waited[e].get(k, 0) >= v:
                    continue
                self.waited[e][k] = v
                waits.append((k, v))
            if waits:
                self.q[e].append((waits, None, None))

    def final_waits(self, eng):
        waits = []
        for k, v in self.dcnt.items():
            if self.waited[eng].get(k, 0) < v:
                waits.append((k, v))
        for e in self.ENGS:
            if e != eng and self.cnt[e] > 0 and self.waited[eng].get("E:" + e, 0) < self.cnt[e]:
                waits.append(("E:" + e, self.cnt[e]))
        self.q[eng].append((waits, None, None))


def build_nc():
    nc = bass.Bass("TRN2", target_bir_lowering=False)
    P = Prog()

    def din(name, shape, dt=F32):
        return nc.dram_tensor(name, list(shape), dt, kind="ExternalInput").ap()

    def dout(name, shape, dt=F32):
        return nc.dram_tensor(name, list(shape), dt, kind="ExternalOutput").ap()

    def dint(name, shape, dt=BF16):
        return nc.dram_tensor(name, list(shape), dt, kind="Internal").ap()

    FAKE = bool(os.environ.get("KFAKE"))

    class FakeW:
        def __init__(self, ap):
            self.ap = ap

        def __getitem__(self, idx):
            if not isinstance(idx, tuple):
                return self
            if len(idx) == 2:
                rows, cols = idx
                return self.ap[rows, 0:cols.stop - cols.start]
            return self.ap[idx[1], idx[2]]

    def dbig(name, shape):
        if not FAKE:
            return din(name, shape)
        if name.startswith("ck"):
            return FakeW(dint(name, shape[1:], F32))
        return FakeW(dint(name, [D, 512], F32))

    xp = din("xp", [NT, D])
    xs = din("xs", [NS, D])
    ck = [dbig("ck0", [2, 128, 2 * D]), dbig("ck1", [2, 512, 2 * D]), dbig("ck2", [2, 2048, 2 * D])]
    sconv = din("sconv", [2, 30, D])
    w_in_a = dbig("attn_w_in", [2, D, 20480])
    w_out_a = dbig("attn_w_out", [2, D, D])
    w_in_c = dbig("conv_w_in", [2, D, 3 * D])
    w_out_c = dbig("conv_w_out", [2, D, D])
    pvec_d = din("pvec", [128, NPV])
    cf_d = din("cft", [128, NCF])
    idx_d = din("idxt", [128, 112], I32)

    yp = dout("yp", [NT, D])
    ys = dout("ys", [NS, D])
    dobig = (lambda n, sh: dint(n, sh, F32)) if FAKE else dout
    okv = [dobig("okv0", [2, 128, 2 * D]), dobig("okv1", [2, 512, 2 * D]), dobig("okv2", [2, 1024, 2 * D])]
    oconv = dout("oconv", [2, 30, D])
    skv = [dobig("skv0", [2, 128, 2 * D]), dobig("skv1", [2, 512, 2 * D]), dobig("skv2", [2, 2048, 2 * D])]
    sconv_o = dout("sconv_o", [2, 30, D])

    AGC = [1, 2, 4]
    KTown = [[dint(f"KTown{j}_{g}", [2048, NT]) for g in range(2)] for j in range(2)]
    KTown2 = [[dint(f"KTown{j}_2_{c}", [512, NT]) for c in range(4)] for j in range(2)]
    KTt0 = [dint(f"KTt{j}_0", [2048, 128]) for j in range(2)]
    KTt1 = [[dint(f"KTt{j}_1_{c}", [1024, 512]) for c in range(2)] for j in range(2)]
    KTa0 = [dint(f"KTa{j}_0", [4 * 2048, 128]) for j in range(2)]
    KTa1 = [[dint(f"KTa{j}_1_{c}", [4 * 1024, 512]) for c in range(2)] for j in range(2)]
    KTa2 = [[dint(f"KTa{j}_2_{c}", [4 * 512, NT]) for c in range(4)] for j in range(2)]
    Vown = [[dint(f"Vown{j}_{g}", [16 * NT, 128]) for g in range(2)] for j in range(2)]
    Vown2 = [[dint(f"Vown{j}_2_{c}", [4 * NT, 128]) for c in range(4)] for j in range(2)]
    Vt0 = [dint(f"Vt{j}_0", [16 * 128, 128]) for j in range(2)]
    Vt1 = [[dint(f"Vt{j}_1_{c}", [8 * 512, 128]) for c in range(2)] for j in range(2)]
    Va0 = [dint(f"Va{j}_0", [4 * 16 * 128, 128]) for j in range(2)]
    Va1 = [[dint(f"Va{j}_1_{c}", [4 * 8 * 512, 128]) for c in range(2)] for j in range(2)]
    Va2 = [[dint(f"Va{j}_2_{c}", [4 * 4 * NT, 128]) for c in range(4)] for j in range(2)]
    Vsn = [dint(f"Vsn{j}", [NS, 3 * 2048]) for j in range(2)]
    Utl = [dint(f"Utl{j}", [2048, 32], F32) for j in range(2)]
    Ua = [dint(f"Ua{j}", [4 * 2048, 32], F32) for j in range(2)]

    es = ExitStack()

    def sb(name, shape, dt=F32):
        return es.enter_context(nc.sbuf_tensor(name, list(shape), dt))

    with es:
        xT = sb("xT", [128, KC, NTOK])
        hT = sb("hT", [128, KC, NTOK], BF16)
        pv = sb("pv", [128, NPV])
        cf = sb("cf", [128, NCF])
        idx = sb("idx", [128, 112], I32)
        ones_bf = sb("ones_bf", [128, 128], BF16)
        id_bf = sb("id_bf", [128, 128], BF16)
        banks = [es.enter_context(nc.psum_tensor(f"bank{i}", [128, 512], F32)) for i in range(8)]
        BK = [f"B{i}" for i in range(8)]
        ident = cf[:, CF_ID:CF_ID + 128]

        def pvc(col, n=1):
            return pv[:, col:col + n]

        rot = {}

        def nxt(name, n):
            rot[name] = (rot.get(name, -1) + 1) % n
            return rot[name]

        def mm(out, lhsT, rhs, start, stop, r, w, inc, skip=False):
            P.op("pe", lambda e, o=out, l=lhsT, rr=rhs, s=start, t=stop, sk=skip:
                 e.matmul(o, l, rr, start=s, stop=t, skip_group_check=sk), r, w, inc)

        def tr(out, in_, idn, r, w, inc=True):
            P.op("pe", lambda e, o=out, i=in_, d=idn: e.transpose(o, i, d), r, w, inc)

        def act(out, in_, func, r, w, bias=None, scale=1.0):
            if bias is None:
                P.op("act", lambda e, o=out, i=in_, f=func, s=scale: e.activation(o, i, f, scale=s), r, w)
            else:
                P.op("act", lambda e, o=out, i=in_, f=func, s=scale, b=bias: e.activation(o, i, f, bias=b, scale=s), r, w)

        def cpy(eng, out, in_, r, w):
            if eng == "act":
                P.op("act", lambda e, o=out, i=in_: e.activation(o, i, AF.Copy), r, w)
            else:
                P.op(eng, lambda e, o=out, i=in_: e.tensor_copy(o, i), r, w)

        def tt(eng, out, a, b, op, r, w):
            P.op(eng, lambda e, o=out, x=a, y=b, p=op: e.tensor_tensor(o, x, y, p), r, w)

        def ts(eng, out, a, s1, op0, r, w, s2=None, op1=None):
            if op1 is None:
                P.op(eng, lambda e, o=out, x=a, s=s1, p=op0: e.tensor_scalar(o, x, s, None, p), r, w)
            else:
                P.op(eng, lambda e, o=out, x=a, s=s1, p=op0, t=s2, q=op1: e.tensor_scalar(o, x, s, t, p, q), r, w)

        def stt(eng, out, a, sc, b, op0, op1, r, w):
            P.op(eng, lambda e, o=out, x=a, s=sc, y=b, p=op0, q=op1: e.scalar_tensor_tensor(o, x, s, y, p, q), r, w)

        def recip(out, r, w):
            P.op("dve", lambda e, o=out: e.reciprocal(o, o), r, w)

        def mset(eng, out, val, r, w):
            P.op(eng, lambda e, o=out, v=val: e.memset(o, v), r, w)

        def dma(q, out, in_, r, w, sem):
            P.dma(q, lambda e, o=out, i=in_: e.dma_start(out=o, in_=i), r, w, sem)

        def gather(out, in_, idx_ap, r, w, sem):
            P.dma("pool", lambda e, o=out, i=in_, x=idx_ap: e.indirect_dma_start(
                out=o, out_offset=None, in_=i, in_offset=bass.IndirectOffsetOnAxis(ap=x, axis=0)), r, w, sem)

        def allgather(src, dst, r, w, sem):
            if os.environ.get("KNOAG"):
                return
            P.dma("pool", lambda e, s=src, d=dst: e.collective_compute(
                "AllGather", ALU.bypass, replica_groups=[[0, 1, 2, 3], [4, 5, 6, 7]], ins=[s], outs=[d]),
                r, w, sem, n=1)

        def load_w(dst, wsrc, r0, nk, c0, n, key, sem):
            src = wsrc[r0:r0 + nk * 128, c0:c0 + n].rearrange("(k p) c -> p k c", p=128)
            dma("pool", dst, src, [], key if isinstance(key, list) else [key], sem)

        dma("sp", pv[:], pvec_d, [], ["pv"], "pv")
        dma("sp", cf[:], cf_d, [], ["cf"], "cf")
        dma("sp", idx[:], idx_d, [], ["idx"], "idx")
        mset("dve", ones_bf[:], 1.0, [], ["ones"])
        cpy("dve", id_bf[:], ident, ["cf"], ["idbf"])

        def xkeys(kc, gi):
            if gi == 0:
                return [f"xT{kc}_{t}" for t in range(4)]
            if gi == 1:
                return [f"xT{kc}_{t}" for t in range(4, 8)]
            return [f"xT{kc}_8"]

        with nc.sbuf_tensor("xin", [128, 2, D], F32) as xin:
            for t8 in range(9):
                s = nxt("xin", 2)
                n = 128 if t8 < 8 else NS
                src = xp[t8 * 128:(t8 + 1) * 128, :] if t8 < 8 else xs
                dma("sp", xin[0:n, s, :], src, [], [f"xin{s}"], f"xin{s}")
                for k4 in range(4):
                    b = 4 + nxt("ldb", 4)
                    for kk in range(4):
                        kc = k4 * 4 + kk
                        tr(banks[b][:, kk * 128:kk * 128 + n], xin[0:n, s, kc * 128:(kc + 1) * 128], ident[0:n, 0:n],
                           [f"xin{s}", "cf"], [BK[b]], inc=(kk == 3))
                    dst = xT[:, k4 * 4:(k4 + 1) * 4, t8 * 128:t8 * 128 + n]
                    srcp = banks[b][:, :].rearrange("p (k t) -> p k t", k=4)[:, :, 0:n]
                    cpy("dve" if (k4 % 2 == 0) else "act", dst, srcp, [BK[b]],
                        [f"xT{kc_}_{t8}" for kc_ in range(k4 * 4, k4 * 4 + 4)])
            P.barrier()

        def rmsnorm(gcol):
            with Scope(nc, [("sqb", [128, 2, 512], BF16), ("rstd", [128, NTOK], F32)]) as (sqb, rstd,):
                for gi, (t0, n) in enumerate(TG):
                    b = 4 + nxt("ldb", 4)
                    for kc in range(KC):
                        s = nxt("sqb", 2)
                        act(sqb[:, s, 0:n], xT[:, kc, t0:t0 + n], AF.Square, xkeys(kc, gi), [f"sqb{s}"])
                        mm(banks[b][:, 0:n], ones_bf[:], sqb[:, s, 0:n], kc == 0, kc == KC - 1,
                           ["ones", f"sqb{s}"], [BK[b]], inc=True)
                    act(rstd[:, t0:t0 + n], banks[b][:, 0:n], AF.Sqrt, [BK[b], "pv"], [f"rstd{gi}"],
                        bias=pvc(PV_EPS_RMS), scale=1.0 / D)
                    recip(rstd[:, t0:t0 + n], [f"rstd{gi}"], [f"rstd{gi}"])
                    for kc in range(KC):
                        stt("dve", hT[:, kc, t0:t0 + n], xT[:, kc, t0:t0 + n],
                            pvc(gcol + kc), rstd[:, t0:t0 + n], ALU.mult, ALU.mult,
                            xkeys(kc, gi) + [f"rstd{gi}", "pv"], [f"hT{kc}_{gi}"])
                P.barrier()

        def pnorm_rstd(acc_ap, n, sq_ap, sqkey, ssb, rt_ap, rtkey, eps_col, scale):
            act(sq_ap, acc_ap, AF.Square, [sqkey[0]], [sqkey[1]])
            mm(banks[ssb][:, 0:n], ones_bf[:], sq_ap, True, True, ["ones", sqkey[1]], [BK[ssb]], inc=True)
            act(rt_ap, banks[ssb][:, 0:n], AF.Sqrt, [BK[ssb], "pv"], [rtkey], bias=pvc(eps_col), scale=scale)
            recip(rt_ap, [rtkey], [rtkey])

        def attn_layer(j):
            win = w_in_a[j]
            wout = w_out_a[j]
            rmsnorm(PV_AN + 16 * j)
            with Scope(nc, [("gT", [128, 8, NTOK], BF16), ("KTs", [128, 3, 16, NS], BF16)]) as (gT, KTs,):
                if STAGE < 2:
                    return
                with Scope(nc, [("wtk", [128, 2, 3, KC, 128], BF16), ("sq1", [128, 2, 512], BF16), ("rt1", [128, 2, 512], F32), ("Kn", [128, 2, 512], F32), ("KTb", [128, 2, NTOK], BF16), ("Kst", [128, 4, 128], F32)]) as (wtk, sq1, rt1, Kn, KTb, Kst,):
                    def ldk(h):
                        s = h % 2
                        keys = [f"wtk{s}_{g}" for g in range(3)]
                        for g in range(3):
                            load_w(wtk[:, s, g], win, 0, KC, g * 6144 + 2048 + h * 128, 128, keys if g == 0 else [], f"wtk{s}")
                        P.join(f"wtk{s}", keys)
                    ldk(0)
                    ldk(1)
                    for h in range(16):
                        s = h % 2
                        for g in range(3):
                            win_g, d = GROUPS[g]
                            kb = nxt("KTb", 2)
                            for gi, (t0, n) in enumerate(TG):
                                b = nxt("pb", 4)
                                for kc in range(KC):
                                    mm(banks[b][:, 0:n], wtk[:, s, g, kc, :], hT[:, kc, t0:t0 + n], kc == 0, kc == KC - 1,
                                       [f"wtk{s}_{g}", f"hT{kc}_{gi}"], [BK[b]], inc=(kc == KC - 1))
                                q2 = nxt("sq1", 2)
                                pnorm_rstd(banks[b][:, 0:n], n, sq1[:, q2, 0:n], (BK[b], f"sq1{q2}"), 4 + nxt("ssb", 2),
                                           rt1[:, q2, 0:n], f"rt1{q2}", PV_EPS_RMS, 1.0 / 128)
                                stt("dve", Kn[:, q2, 0:n], banks[b][:, 0:n], pvc(PV_KG + 3 * j + g), rt1[:, q2, 0:n],
                                    ALU.mult, ALU.mult, [BK[b], f"rt1{q2}", "pv"], [f"Kn{q2}"])
                                if gi < 2:
                                    dstv = KTb[:, kb, 0:NT].rearrange("p (r i) -> p r i", r=d)[:, :, t0 // d:(t0 + n) // d]
                                    srcv = Kn[:, q2, 0:n].rearrange("p (i r) -> p r i", r=d)
                                    cpy("pool", dstv, srcv, [f"Kn{q2}"], [f"KTb{kb}_{gi}"])
                                else:
                                    cpy("pool", KTb[:, kb, NT:NTOK], Kn[:, q2, 0:n], [f"Kn{q2}"], [f"KTb{kb}_2"])
                                    cpy("pool", KTs[:, g, h, :], Kn[:, q2, 0:n], [f"Kn{q2}"], [f"KTs{g}_{h}"])
                                if gi < 2:
                                    for t4 in range(4):
                                        tt8 = gi * 4 + t4
                                        row0 = tt8 * 128 - (NT - NTAIL[g])
                                        if row0 < 0:
                                            continue
                                        tb = 6 + nxt("tb", 2)
                                        tr(banks[tb][:, 0:128], Kn[:, q2, t4 * 128:(t4 + 1) * 128], ident, [f"Kn{q2}", "cf"], [BK[tb]])
                                        ks = nxt("Kst", 4)
                                        cpy("act", Kst[:, ks, :], banks[tb][:, 0:128], [BK[tb]], [f"Kst{ks}"])
                                        dma("sp", okv[g][j, row0:row0 + 128, h * 128:(h + 1) * 128], Kst[:, ks, :],
                                            [f"Kst{ks}"], [], f"Kst{ks}")
                                else:
                                    tb = 6 + nxt("tb", 2)
                                    tr(banks[tb][0:NS, 0:128], Kn[:, q2, 0:NS], ident, [f"Kn{q2}", "cf"], [BK[tb]])
                                    ks = nxt("Kst", 4)
                                    cpy("act", Kst[0:NS, ks, :], banks[tb][0:NS, 0:128], [BK[tb]], [f"Kst{ks}"])
                                    dma("sp", skv[g][j, win_g - NS:win_g, h * 128:(h + 1) * 128], Kst[0:NS, ks, :],
                                        [f"Kst{ks}"], [], f"Kst{ks}")
                            kr = [f"KTb{kb}_0", f"KTb{kb}_1"]
                            if g < 2:
                                dma("sp", KTown[j][g][h * 128:(h + 1) * 128, :], KTb[:, kb, 0:NT], kr, [f"KTown{g}_{h}"], f"KTb{kb}")
                            else:
                                dma("sp", KTown2[j][h // 4][(h % 4) * 128:(h % 4 + 1) * 128, :], KTb[:, kb, 0:NT], kr, [f"KTown{g}_{h}"], f"KTb{kb}")
                            if g == 0:
                                dma("sp", KTt0[j][h * 128:(h + 1) * 128, :], KTb[:, kb, NT - 128:NT], kr, [f"KTt0_{h}"], f"KTb{kb}")
                            if g == 1:
                                dma("sp", KTt1[j][h // 8][(h % 8) * 128:(h % 8 + 1) * 128, :].rearrange("p (r i) -> p r i", r=4),
                                    KTb[:, kb, 0:NT].rearrange("p (r i) -> p r i", r=4)[:, :, 128:256], kr, [f"KTt1_{h}"], f"KTb{kb}")
                        if h + 2 < 16:
                            ldk(h + 2)
                    P.barrier()
                if STAGE < 3:
                    return
                with Scope(nc, [("wtv", [128, 2, KC, 512], BF16), ("Vsb", [128, 3, 512], BF16), ("Vsf", [128, 3, 512], F32)]) as (wtv, Vsb, Vsf,):
                    order = [(g, hq) for g in (2, 1, 0) for hq in range(4)]
                    KV = int(os.environ.get("KV", "0"))
                    realdma = dma

                    def dmaf(bit):
                        return (lambda *a, **k: None) if (KV >> bit) & 1 else realdma

                    def ldv(i):
                        g, hq = order[i]
                        load_w(wtv[:, i % 2], win, 0, KC, g * 6144 + 4096 + hq * 512, 512, f"wtv{i % 2}", f"wtv{i % 2}")
                    ldv(0)
                    ldv(1)
                    if (KV >> 5) & 1:
                        order = order[:2]
                    for i, (g, hq) in enumerate(order):
                        s = i % 2
                        win_g, d = GROUPS[g]
                        for t8 in range(8 if (KV >> 3) & 1 else 9):
                            n = 128 if t8 < 8 else NS
                            gi = 0 if t8 < 4 else (1 if t8 < 8 else 2)
                            b = nxt("pb", 4)
                            for kc in range(KC):
                                mm(banks[b][0:n, :], hT[:, kc, t8 * 128:t8 * 128 + n], wtv[:, s, kc, :], kc == 0, kc == KC - 1,
                                   [f"wtv{s}", f"hT{kc}_{gi}"], [BK[b]], inc=(kc == KC - 1))
                            vs = nxt("Vsb", 3)
                            if not (KV >> 6) & 1:
                                cpy("dve", Vsb[0:n, vs, :], banks[b][0:n, :], [BK[b]], [f"Vsb{vs}"])
                            row0 = t8 * 128 - (NT - NTAIL[g])
                            need_f = (t8 == 8) or row0 >= 0
                            if need_f and not (KV >> 4) & 1:
                                cpy("dve", Vsf[0:n, vs, :], banks[b][0:n, :], [BK[b]], [f"Vsf{vs}"])
                            if t8 < 8:
                                if g < 2:
                                    dst = Vown[j][g].rearrange("(h t) e -> t h e", h=16)[t8 * 128:(t8 + 1) * 128, hq * 4:hq * 4 + 4, :]
                                else:
                                    dst = Vown2[j][hq].rearrange("(h t) e -> t h e", h=4)[t8 * 128:(t8 + 1) * 128, :, :]
                                dmaf(0)("sp", dst, Vsb[:, vs, :].rearrange("p (h e) -> p h e", h=4), [f"Vsb{vs}"],
                                    [f"Vown{g}_{hq}_{t8}"], f"Vsb{vs}")
                                if g < 2 and row0 >= 0:
                                    if g == 0:
                                        dst = Vt0[j].rearrange("(h t) e -> t h e", h=16)[row0:row0 + 128, hq * 4:hq * 4 + 4, :]
                                    else:
                                        dst = Vt1[j][hq // 2].rearrange("(h t) e -> t h e", h=8)[row0:row0 + 128, (hq % 2) * 4:(hq % 2) * 4 + 4, :]
                                    dmaf(0)("sp", dst, Vsb[:, vs, :].rearrange("p (h e) -> p h e", h=4), [f"Vsb{vs}"],
                                        [f"Vt{g}_{hq}_{t8}"], f"Vsb{vs}")
                                if row0 >= 0:
                                    dmaf(1)("sp", okv[g][j, row0:row0 + 128, D + hq * 512:D + (hq + 1) * 512], Vsf[:, vs, :],
                                        [f"Vsf{vs}"], [], f"Vsf{vs}")
                            else:
                                dmaf(2)("sp", Vsn[j][:, g * 2048 + hq * 512:g * 2048 + (hq + 1) * 512], Vsb[0:NS, vs, :],
                                    [f"Vsb{vs}"], [f"Vsn{g}_{hq}"], f"Vsb{vs}")
                                dmaf(1)("sp", skv[g][j, win_g - NS:win_g, D + hq * 512:D + (hq + 1) * 512], Vsf[0:NS, vs, :],
                                    [f"Vsf{vs}"], [], f"Vsf{vs}")
                        if i + 2 < len(order):
                            ldv(i + 2)
                        if hq == 3:
                            if g == 0:
                                allgather(KTt0[j], KTa0[j], [f"KTt0_{h}" for h in range(16)], ["KTa0_0"], f"ag{j}")
                                allgather(Vt0[j], Va0[j], [f"Vt0_{q}_{t}" for q in range(4) for t in range(8)], ["Va0_0"], f"ag{j}")
                            elif g == 1:
                                for c in range(2):
                                    allgather(KTt1[j][c], KTa1[j][c], [f"KTt1_{h}" for h in range(8 * c, 8 * c + 8)], [f"KTa1_{c}"], f"ag{j}")
                                    allgather(Vt1[j][c], Va1[j][c], [f"Vt1_{q}_{t}" for q in (2 * c, 2 * c + 1) for t in range(8)],
                                              [f"Va1_{c}"], f"ag{j}")
                            else:
                                for c in range(4):
                                    allgather(KTown2[j][c], KTa2[j][c], [f"KTown2_{h}" for h in range(4 * c, 4 * c + 4)], [f"KTa2_{c}"], f"ag{j}")
                                    allgather(Vown2[j][c], Va2[j][c], [f"Vown2_{c}_{t}" for t in range(8)], [f"Va2_{c}"], f"ag{j}")
                    for g in range(3):
                        win_g = GROUPS[g][0]
                        for r0 in range(0, (win_g - NS) if not os.environ.get("KNOSHIFT") else 0, 128):
                            r1 = min(r0 + 128, win_g - NS)
                            dma("pool", skv[g][j, r0:r1, :], ck[g][j, NS + r0:NS + r1, :], [], [], "cshift")
                    P.barrier()
                if STAGE < 4:
                    return
                with Scope(nc, [("wq", [128, 4, KC, 128], BF16), ("gst", [128, 2048], BF16), ("QT", [128, 3, NT], BF16), ("QTs", [128, 3, NS], BF16), ("sz", [128, NTOK], F32), ("KTx", [128, 5760], BF16), ("Vx", [128, 53, 128], BF16), ("PT", [128, 3, 256], BF16), ("Sb", [128, 3, 256], F32), ("Ksc", [128, 9, 128], F32), ("KsT", [128, 9, 128], BF16), ("Vsc", [128, 9, 128], BF16), ("Vnh", [NS, 3, 128], BF16), ("og", [128, NTOK], F32), ("wo", [128, 2, 8, 128], BF16), ("sq2", [128, 2, 512], BF16), ("rt2", [128, 2, 512], F32)]) as (wq, gst, QT, QTs, sz, KTx, Vx, PT, Sb, Ksc, KsT, Vsc, Vnh, og, wo, sq2, rt2,):
                    KOFF = [0, 1152, 1152 + 1536]
                    VOFF = [0, 9, 21]
                    NCH = [9, 3, 2]

                    def ktx(g, r):
                        d = GROUPS[g][1]
                        ext = 128 + NT // d
                        return KTx[:, KOFF[g] + r * ext:KOFF[g] + (r + 1) * ext]

                    qcols = [(g * 6144 + 0) for g in range(3)] + [NQKV]

                    def ldq(hh, c):
                        k = hh * 4 + c
                        s = k % 4
                        load_w(wq[:, s], win, 0, KC, qcols[c] + hh * 128, 128, f"wq{s}", f"wq{s}")
                    for k in range(4):
                        ldq(k // 4, k % 4)

                    def unit(ktap, qtap, vap, dap, sd, o_ap, l_ap, nk, nq, rk, okey, lkey, bias=None):
                        b = 4 + nxt("sbk", 2)
                        mm(banks[b][0:nk, 0:nq], ktap, qtap, True, True, rk[0], [BK[b]], inc=True)
                        s = nxt("PT", 3)
                        stt("dve", Sb[0:nk, s, 0:nq], dap, -sd, banks[b][0:nk, 0:nq], ALU.mult, ALU.add,
                            [BK[b], "cf"], [f"Sb{s}"])
                        act(PT[0:nk, s, 0:nq], Sb[0:nk, s, 0:nq], AF.Exp, [f"Sb{s}", "pv"], [f"PT{s}"],
                            bias=(bias if bias is not None else pvc(PV_ZERO)[0:nk, :]))
                        mm(o_ap, vap, PT[0:nk, s, 0:nq], False, False, rk[1] + [f"PT{s}"], [okey], inc=False, skip=True)
                        mm(l_ap, ones_bf[0:nk, :], PT[0:nk, s, 0:nq], False, False, ["ones", f"PT{s}"], [lkey], inc=True, skip=True)

                    for h in range(16):
                        kxk = [f"KTx{g}" for g in range(3)]
                        vxk = [f"Vx{g}" for g in range(3)]
                        for g in range(3):
                            win_g, d = GROUPS[g]
                            ext = 128 + NT // d
                            dstk = KTx[:, KOFF[g]:KOFF[g] + d * ext].rearrange("p (r x) -> p r x", r=d)
                            ksrc = KTown[j][g][h * 128:(h + 1) * 128, :] if g < 2 else KTown2[j][h // 4][(h % 4) * 128:(h % 4 + 1) * 128, :]
                            dma("sp", dstk[:, :, 128:ext], ksrc.rearrange("p (r i) -> p r i", r=d),
                                [f"KTown{g}_{h}"], [kxk[g]], f"KTx{g}")
                            vv = Vx[:, VOFF[g]:VOFF[g] + d * NCH[g], :].rearrange("p (r c) e -> p r c e", r=d)
                            vsrc = Vown[j][g][h * NT:(h + 1) * NT, :] if g < 2 else Vown2[j][h // 4][(h % 4) * NT:(h % 4 + 1) * NT, :]
                            vr = [f"Vown{g}_{h // 4}_{t}" for t in range(8)]
                            if g == 0:
                                dma("sp", vv[:, 0, 1:9, :], vsrc.rearrange("(c p) e -> p c e", p=128), vr, [vxk[g]], f"Vx{g}")
                            elif g == 1:
                                for r in range(4):
                                    dma("sp", vv[:, r, 1:3, :], vsrc.rearrange("(c p r) e -> p r c e", p=128, r=4)[:, r],
                                        vr, [vxk[g]], f"Vx{g}")
                            else:
                                dma("sp", vv[0:64, :, 1, :], vsrc.rearrange("(p r) e -> p r e", r=16), vr, [vxk[g]], f"Vx{g}")
                        ia = idx[:, h:h + 1]
                        ik1 = idx[:, 16 + h:17 + h]
                        ik2a = idx[:, 32 + h:33 + h]
                        ik2b = idx[:, 48 + h:49 + h]
                        iv0 = idx[:, 64 + h:65 + h]
                        iv1 = idx[:, 80 + h:81 + h]
                        iv2 = idx[:, 96 + h:97 + h]
                        gather(KTx[:, 0:128], KTa0[j], ia, ["KTa0_0", "idx"], [kxk[0]], "KTx0")
                        d1 = KTx[:, KOFF[1]:KOFF[1] + 4 * 384].rearrange("p (r x) -> p r x", r=4)
                        gather(gst[:, 0:512], KTa1[j][h // 8], ik1, [f"KTa1_{h // 8}", "idx"], ["gst"], "gst")
                        cpy("pool", d1[:, :, 0:128], gst[:, 0:512].rearrange("p (r i) -> p r i", r=4), ["gst"], [kxk[1]])
                        d2 = KTx[:, KOFF[2]:KOFF[2] + 16 * 192].rearrange("p (r x) -> p r x", r=16)
                        gather(gst[:, 0:1024], KTa2[j][h // 4], ik2a, [f"KTa2_{h // 4}", "idx"], ["gst"], "gst")
                        cpy("pool", d2[:, :, 64:128], gst[:, 0:1024].rearrange("p (r i) -> p r i", r=16), ["gst"], [kxk[2]])
                        gather(gst[:, 0:1024], KTa2[j][h // 4], ik2b, [f"KTa2_{h // 4}", "idx"], ["gst"], "gst")
                        cpy("pool", d2[:, :, 0:64], gst[:, 0:1024].rearrange("p (r i) -> p r i", r=16), ["gst"], [kxk[2]])
                        gather(Vx[:, 0, :], Va0[j], iv0, ["Va0_0", "idx"], [vxk[0]], "Vx0")
                        v1 = Vx[:, VOFF[1]:VOFF[1] + 12, :].rearrange("p (r c) e -> p r c e", r=4)
                        gather(gst[:, 0:512], Va1[j][h // 8].rearrange("(n r) e -> n (r e)", r=4), iv1, [f"Va1_{h // 8}", "idx"], ["gst"], "gst")
                        cpy("pool", v1[:, :, 0, :], gst[:, 0:512].rearrange("p (r e) -> p r e", r=4), ["gst"], [vxk[1]])
                        v2 = Vx[:, VOFF[2]:VOFF[2] + 32, :].rearrange("p (r c) e -> p r c e", r=16)
                        gather(gst[:, 0:2048], Va2[j][h // 4].rearrange("(n r) e -> n (r e)", r=16), iv2, [f"Va2_{h // 4}", "idx"], ["gst"], "gst")
                        cpy("pool", v2[:, :, 0, :], gst[:, 0:2048].rearrange("p (r e) -> p r e", r=16), ["gst"], [vxk[2]])
                        ci = 0
                        for g in range(3):
                            win_g, d = GROUPS[g]
                            nres = 1 if g == 0 else NS
                            for r in range(nres):
                                rows = ck[g][j, r:win_g:d, :] if g > 0 else ck[g][j, 0:128, :]
                                dma("sp", Ksc[:, ci, :], rows[:, h * 128:(h + 1) * 128], [], [f"Ksc{ci}"], f"Ksc{ci}")
                                dma("pool", Vsc[:, ci, :], rows[:, D + h * 128:D + (h + 1) * 128], [], [f"Vsc{ci}"], f"Vsc{ci}")
                                ci += 1
                        dma("sp", Vnh[:, :, :], Vsn[j].rearrange("t (g c) -> t g c", g=3)[:, :, h * 128:(h + 1) * 128],
                            [f"Vsn{g}_{h // 4}" for g in range(3)], ["Vnh"], "Vnh")
                        for c in range(4):
                            k = h * 4 + c
                            s = k % 4
                            for gi, (t0, n) in enumerate(TG):
                                b = 6 + nxt("qb", 2)
                                for kc in range(KC):
                                    mm(banks[b][:, 0:n], wq[:, s, kc, :], hT[:, kc, t0:t0 + n], kc == 0, kc == KC - 1,
                                       [f"wq{s}", f"hT{kc}_{gi}"], [BK[b]], inc=(kc == KC - 1))
                                if c == 3:
                                    act(sz[:, t0:t0 + n], banks[b][:, 0:n], AF.Silu, [BK[b]], [f"sz{gi}"])
                                    continue
                                g = c
                                d = GROUPS[g][1]
                                q2 = nxt("sq2", 2)
                                pnorm_rstd(banks[b][:, 0:n], n, sq2[:, q2, 0:n], (BK[b], f"sq2{q2}"), 4 + nxt("sbk", 2),
                                           rt2[:, q2, 0:n], f"rt2{q2}", PV_EPS_Q, 1.0)
                                if gi < 2:
                                    dstv = QT[:, g, :].rearrange("p (r i) -> p r i", r=d)[:, :, t0 // d:(t0 + n) // d]
                                    a_v = banks[b][:, 0:n].rearrange("p (i r) -> p r i", r=d)
                                    r_v = rt2[:, q2, 0:n].rearrange("p (i r) -> p r i", r=d)
                                    stt("dve", dstv, a_v, pvc(PV_QG + 3 * j + g), r_v, ALU.mult, ALU.mult,
                                        [BK[b], f"rt2{q2}", "pv"], [f"QT{g}_{gi}"])
                                else:
                                    stt("dve", QTs[:, g, :], banks[b][:, 0:n], pvc(PV_QG + 3 * j + g), rt2[:, q2, 0:n],
                                        ALU.mult, ALU.mult, [BK[b], f"rt2{q2}", "pv"], [f"QTs{g}"])
                            if k + 4 < 64:
                                ldq((k + 4) // 4, (k + 4) % 4)
                        for ci in range(9):
                            tb = 6 + nxt("qb", 2)
                            tr(banks[tb][:, 0:128], Ksc[:, ci, :], ident, [f"Ksc{ci}", "cf"], [BK[tb]])
                            cpy("dve", KsT[:, ci, :], banks[tb][:, 0:128], [BK[tb]], [f"KsT{ci}"])
                        for b in range(4):
                            mset("dve", banks[b][:, :], 0.0, [], [BK[b]])
                        mset("dve", banks[7][:, 0:16], 0.0, [], [BK[7]])
                        for g in range(3):
                            win_g, d = GROUPS[g]
                            nown = NT // d
                            sd = alibi_sd(g, h)
                            for r in range(d):
                                kt = ktx(g, r)
                                for c in range(NCH[g]):
                                    nk = min(128, 128 + nown - c * 128)
                                    ilo = max(0, 128 * (c - 1))
                                    ihi = min(nown, 128 * (c - 1) + 256)
                                    vap = Vx[0:nk, VOFF[g] + r * NCH[g] + c, :]
                                    bias = pvc(PV_LBM + g) if c == 0 else None
                                    half_n = 512 // d
                                    for hf in range(2):
                                        a = max(ilo, hf * half_n)
                                        e_ = min(ihi, (hf + 1) * half_n)
                                        if a >= e_:
                                            continue
                                        nq = e_ - a
                                        j0 = a - 128 * (c - 1)
                                        col0 = (a - hf * half_n) * d + r
                                        cols = slice(col0, col0 + (nq - 1) * d + 1, d)
                                        unit(kt[:, c * 128:c * 128 + nk], QT[:, g, r * nown + a:r * nown + e_], vap,
                                             cf[0:nk, CF_D + j0:CF_D + j0 + nq], sd,
                                             banks[hf][:, cols], banks[2 + hf][:, cols], nk, nq,
                                             ([f"KTx{g}", f"QT{g}_{hf}"], [f"Vx{g}"]), BK[hf], BK[2 + hf], bias=bias)
                        ci = 0
                        for g in range(3):
                            win_g, d = GROUPS[g]
                            sd = alibi_sd(g, h)
                            nres = 1 if g == 0 else NS
                            for r in range(nres):
                                q0, nq = (0, NS) if g == 0 else (r, 1)
                                unit(KsT[:, ci, :], QTs[:, g, q0:q0 + nq], Vsc[:, ci, :],
                                     cf[:, CF_D + 128 + (0 if g == 0 else 0):CF_D + 128 + nq], sd,
                                     banks[7][:, q0:q0 + nq], banks[7][:, 8 + q0:8 + q0 + nq], 128, nq,
                                     ([f"KsT{ci}", f"QTs{g}"], [f"Vsc{ci}"]), BK[7], BK[7])
                                ci += 1
                            dd = cf[0:NS, CF_D:CF_D + NS] if g == 0 else cf[0:NS, CF_DD:CF_DD + NS]
                            unit(KTs[:, g, h, :], QTs[:, g, :], Vnh[:, g, :], dd, sd,
                                 banks[7][:, 0:NS], banks[7][:, 8:8 + NS], NS, NS,
                                 ([f"KTs{g}_{h}", f"QTs{g}"], ["Vnh"]), BK[7], BK[7])
                        for hf in range(2):
                            cpy("act", og[:, hf * 512:(hf + 1) * 512], banks[2 + hf][:, :], [BK[2 + hf]], [f"og{hf}"])
                            recip(og[:, hf * 512:(hf + 1) * 512], [f"og{hf}"], [f"og{hf}"])
                            tt("dve", og[:, hf * 512:(hf + 1) * 512], og[:, hf * 512:(hf + 1) * 512], banks[hf][:, :], ALU.mult,
                               [f"og{hf}", BK[hf]], [f"og{hf}"])
                            tt("pool", gT[:, h % 8, hf * 512:(hf + 1) * 512], og[:, hf * 512:(hf + 1) * 512],
                               sz[:, hf * 512:(hf + 1) * 512], ALU.mult, [f"og{hf}", f"sz{hf}"], [f"gT{h % 8}_{hf}"])
                        cpy("act", og[:, NT:NTOK], banks[7][:, 8:8 + NS], [BK[7]], ["og2"])
                        recip(og[:, NT:NTOK], ["og2"], ["og2"])
                        tt("dve", og[:, NT:NTOK], og[:, NT:NTOK], banks[7][:, 0:NS], ALU.mult, ["og2", BK[7]], ["og2"])
                        tt("pool", gT[:, h % 8, NT:NTOK], og[:, NT:NTOK], sz[:, NT:NTOK], ALU.mult, ["og2", "sz2"], [f"gT{h % 8}_2"])
                        if h % 8 == 7:
                            hb = h - 7
                            for dc in range(16):
                                s = nxt("wo", 2)
                                load_w(wo[:, s], wout, hb * 128, 8, dc * 128, 128, f"wo{s}", f"wo{s}")
                                for gi, (t0, n) in enumerate(TG):
                                    b = 6 + nxt("qb", 2)
                                    for hh in range(8):
                                        mm(banks[b][:, 0:n], wo[:, s, hh, :], gT[:, hh, t0:t0 + n], hh == 0, hh == 7,
                                           [f"wo{s}", f"gT{hh}_{gi}"], [BK[b]], inc=(hh == 7))
                                    tt("dve", xT[:, dc, t0:t0 + n], xT[:, dc, t0:t0 + n], banks[b][:, 0:n], ALU.add,
                                       xkeys(dc, gi) + [BK[b]], xkeys(dc, gi))
                    P.barrier()

        def conv_layer(j):
            win = w_in_c[j]
            wout = w_out_c[j]
            rmsnorm(PV_CN + 16 * j)
            dwc = PV_DWW + j * 16 * CW
            with Scope(nc, [("cT", [128, KC, NTOK], BF16), ("szc", [128, KC, NTOK], BF16), ("Us", [128, KC, 34], F32), ("Uh", [128, KC, 64], F32)]) as (cT, szc, Us, Uh,):
                with Scope(nc, [("wc", [128, 3, KC, 128], BF16), ("Ub", [128, NT], F32), ("sg", [128, 2, 512], F32), ("ac", [128, 2, NT], F32), ("prs", [128, NS * CW], F32), ("cs", [128, 32], F32)]) as (wc, Ub, sg, ac, prs, cs,):
                    st = bass.AP(ac, 0, [list(ac[0:32, 0, 0:1].ap[0]), [1, D]])

                    def ldc(k):
                        cc, a = k // 3, k % 3
                        load_w(wc[:, a], win, 0, KC, a * D + cc * 128, 128, f"wc{a}", f"wc{a}")
                    for k in range(3):
                        ldc(k)
                    dma("sp", st[0:30, :], sconv[j], [], ["st"], "st")
                    for cc in range(KC):
                        tb = 6 + nxt("qb", 2)
                        tr(banks[tb][:, 0:30], st[0:30, cc * 128:(cc + 1) * 128], ident[0:30, 0:30], ["st", "cf"], [BK[tb]])
                        cpy("act", Us[:, cc, 0:30], banks[tb][:, 0:30], [BK[tb]], [f"Us{cc}"])
                    dma("pool", sconv_o[j, 0:26, :], sconv[j, 4:30, :], [], [], "cshift")
                    P.barrier()
                    for cc in range(KC):
                        for a in range(3):
                            for gi, (t0, n) in enumerate(TG):
                                b = nxt("pb", 6)
                                for kc in range(KC):
                                    mm(banks[b][:, 0:n], wc[:, a, kc, :], hT[:, kc, t0:t0 + n], kc == 0, kc == KC - 1,
                                       [f"wc{a}", f"hT{kc}_{gi}"], [BK[b]], inc=(kc == KC - 1))
                                udst = Ub[:, t0:t0 + n] if gi < 2 else Us[:, cc, 30:34]
                                ukey = f"Ub_{gi}" if gi < 2 else f"Us{cc}"
                                if a == 0:
                                    cpy("act", udst, banks[b][:, 0:n], [BK[b]], [ukey])
                                elif a == 1:
                                    q2 = nxt("sg", 2)
                                    act(sg[:, q2, 0:n], banks[b][:, 0:n], AF.Sigmoid, [BK[b]], [f"sg{q2}"])
                                    tt("dve", udst, udst, sg[:, q2, 0:n], ALU.mult, [f"sg{q2}", ukey], [ukey])
                                else:
                                    act(szc[:, cc, t0:t0 + n], banks[b][:, 0:n], AF.Silu, [BK[b]], [f"szc{cc}_{gi}"])
                            if cc + 1 < KC:
                                ldc((cc + 1) * 3 + a)
                        uk = ["Ub_0", "Ub_1"]
                        cpy("pool", Uh[:, cc, 32:64], Ub[:, 0:32], uk, [f"Uh{cc}"])
                        dma("sp", Utl[j][cc * 128:(cc + 1) * 128, :], Ub[:, NT - 32:NT], uk, [f"Utl{cc}"], "Ub")
                        L = NT - 30
                        a0 = ac[:, 0, 0:L]
                        a1 = ac[:, 1, 0:L]
                        ts("dve", a0, Ub[:, 0:L], pvc(dwc + cc * CW + 0), ALU.mult, uk + ["pv"], ["ac0"],
                           s2=pvc(PV_DWB + 16 * j + cc), op1=ALU.add)
                        for tap in range(1, CW):
                            stt("dve", a0, Ub[:, tap:tap + L], pvc(dwc + cc * CW + tap), a0, ALU.mult, ALU.add, uk + ["ac0", "pv"], ["ac0"])
                        cpy("pool", cT[:, cc, 30:NT], a0, ["ac0"], [f"cT{cc}_m"])
                        base = Us[:, cc, 0:1]
                        win_ap = bass.AP(Us, base.offset, [list(base.ap[0]), [1, NS], [1, CW]])
                        wcol = pvc(dwc + cc * CW, CW)
                        w_ap = bass.AP(pv, wcol.offset, [list(wcol.ap[0]), [0, NS], [1, CW]])
                        pr = prs[:, :].rearrange("p (t k) -> p t k", k=CW)
                        tt("dve", pr, win_ap, w_ap, ALU.mult, [f"Us{cc}", "pv"], ["prs"])
                        P.op("dve", lambda e, o=cs[:, 0:NS], i=pr: e.tensor_reduce(o, i, AX.X, ALU.add), ["prs"], ["cs"])
                        ts("dve", cT[:, cc, NT:NTOK], cs[:, 0:NS], pvc(PV_DWB + 16 * j + cc), ALU.add, ["cs", "pv"], [f"cT{cc}_s"])
                    allgather(Utl[j], Ua[j], [f"Utl{cc}" for cc in range(KC)], ["Ua"], f"agu{j}")
                    for cc in range(KC):
                        gather(Uh[:, cc, 0:32], Ua[j], idx[:, cc:cc + 1], ["Ua", "idx"], [f"Uh{cc}"] if cc else [f"Uh{c_}" for c_ in range(KC)], "Uh")
                    P.join("Uh", [f"Uh{cc}" for cc in range(KC)])
                    prb = bass.AP(sg, 0, [list(sg[:, 0, 0:1].ap[0]), [CW, 30], [1, CW]])
                    for cc in range(KC):
                        ts("dve", Uh[:, cc, 0:32], Uh[:, cc, 0:32], pvc(PV_HALO), ALU.mult, [f"Uh{cc}", "pv"], [f"Uh{cc}"])
                        base = Uh[:, cc, 2:3]
                        win_ap = bass.AP(Uh, base.offset, [list(base.ap[0]), [1, 30], [1, CW]])
                        wcol = pvc(dwc + cc * CW, CW)
                        w_ap = bass.AP(pv, wcol.offset, [list(wcol.ap[0]), [0, 30], [1, CW]])
                        tt("dve", prb, win_ap, w_ap, ALU.mult, [f"Uh{cc}", "pv", "sg0", "sg1"], ["sg0", "sg1"])
                        P.op("dve", lambda e, o=cs[:, 0:30], i=prb: e.tensor_reduce(o, i, AX.X, ALU.add), ["sg0", "sg1"], ["cs"])
                        ts("dve", cT[:, cc, 0:30], cs[:, 0:30], pvc(PV_DWB + 16 * j + cc), ALU.add, ["cs", "pv"], [f"cT{cc}_h"])
                    P.barrier()
                    dma("sp", Uh[:, :, 0:32], Utl[j].rearrange("(c p) t -> p c t", p=128), [], [f"Uh{cc}" for cc in range(KC)], "Uh")
                    for cc in range(KC):
                        tb = 6 + nxt("qb", 2)
                        tr(banks[tb][0:32, 0:128], Uh[:, cc, 0:32], ident, [f"Uh{cc}", "cf"], [BK[tb]])
                        cpy("act", st[0:32, cc * 128:(cc + 1) * 128], banks[tb][0:32, 0:128], [BK[tb]], ["st"])
                    dma("sp", oconv[j], st[2:32, :], ["st"], [], "st")
                    for cc in range(KC):
                        tb = 6 + nxt("qb", 2)
                        tr(banks[tb][0:NS, 0:128], Us[:, cc, 30:34], ident, [f"Us{cc}", "cf"], [BK[tb]])
                        cpy("act", st[0:NS, cc * 128:(cc + 1) * 128], banks[tb][0:NS, 0:128], [BK[tb]], ["st"])
                    dma("sp", sconv_o[j, 26:30, :], st[0:NS, :], ["st"], [], "st")
                    P.barrier()
                with Scope(nc, [("Sq16", [128, 2, 512], BF16), ("sg2", [128, 2, 512], F32), ("mean", [128, NTOK], F32), ("rs", [128, NTOK], F32), ("wo2", [128, 2, KC, 128], BF16)]) as (Sq16, sg2, mean, rs, wo2,):
                    def ckeys(cc, gi):
                        return [f"cT{cc}_m", f"cT{cc}_h"] if gi == 0 else ([f"cT{cc}_m"] if gi == 1 else [f"cT{cc}_s"])
                    for gi, (t0, n) in enumerate(TG):
                        b1 = 0 + gi % 2
                        b2 = 2 + gi % 2
                        for cc in range(KC):
                            mm(banks[b1][:, 0:n], ones_bf[:], cT[:, cc, t0:t0 + n], cc == 0, cc == KC - 1,
                               ["ones"] + ckeys(cc, gi), [BK[b1]], inc=True)
                            q2 = nxt("sqc", 2)
                            act(Sq16[:, q2, 0:n], cT[:, cc, t0:t0 + n], AF.Square, ckeys(cc, gi), [f"sq16{q2}"])
                            mm(banks[b2][:, 0:n], ones_bf[:], Sq16[:, q2, 0:n], cc == 0, cc == KC - 1,
                               ["ones", f"sq16{q2}"], [BK[b2]], inc=True)
                        mk = f"mean{gi}"
                        rk = f"rs{gi}"
                        m_ap = mean[:, t0:t0 + n]
                        r_ap = rs[:, t0:t0 + n]
                        ts("dve", m_ap, banks[b1][:, 0:n], 1.0 / D, ALU.mult, [BK[b1]], [mk])
                        tt("dve", r_ap, m_ap, m_ap, ALU.mult, [mk], [rk])
                        stt("dve", r_ap, banks[b2][:, 0:n], 1.0 / D, r_ap, ALU.mult, ALU.subtract, [BK[b2], rk], [rk])
                        act(r_ap, r_ap, AF.Sqrt, [rk, "pv"], [rk], bias=pvc(PV_EPS_LN), scale=1.0)
                        recip(r_ap, [rk], [rk])
                        for cc in range(KC):
                            q2 = nxt("sg2", 2)
                            tmp = sg2[:, q2, 0:n]
                            tt("dve", tmp, cT[:, cc, t0:t0 + n], m_ap, ALU.subtract, ckeys(cc, gi) + [mk], [f"sg2{q2}"])
                            tt("pool", tmp, tmp, r_ap, ALU.mult, [f"sg2{q2}", rk], [f"sg2{q2}"])
                            act(tmp, tmp, AF.Silu, [f"sg2{q2}", "pv"], [f"sg2{q2}"], bias=pvc(PV_LNB + 16 * j + cc),
                                scale=pvc(PV_LNG + 16 * j + cc))
                            tt("dve", hT[:, cc, t0:t0 + n], tmp, szc[:, cc, t0:t0 + n], ALU.mult,
                               [f"sg2{q2}", f"szc{cc}_{gi}"], [f"hT{cc}_{gi}"])
                    for dc in range(16):
                        s = dc % 2
                        load_w(wo2[:, s], wout, 0, KC, dc * 128, 128, f"wo2{s}", f"wo2{s}")
                        for gi, (t0, n) in enumerate(TG):
                            b = 4 + nxt("ldb", 4)
                            for cc in range(KC):
                                mm(banks[b][:, 0:n], wo2[:, s, cc, :], hT[:, cc, t0:t0 + n], cc == 0, cc == KC - 1,
                                   [f"wo2{s}", f"hT{cc}_{gi}"], [BK[b]], inc=(cc == KC - 1))
                            tt("dve", xT[:, dc, t0:t0 + n], xT[:, dc, t0:t0 + n], banks[b][:, 0:n], ALU.add,
                               xkeys(dc, gi) + [BK[b]], xkeys(dc, gi))
                    P.barrier()


        if STAGE >= 1:
            attn_layer(0)
        if STAGE >= 5:
            conv_layer(0)
        if STAGE >= 6:
            attn_layer(1)
        if STAGE >= 7:
            conv_layer(1)

        with nc.sbuf_tensor("xo", [128, 2, D], F32) as xo:
            for t8 in range(9):
                s = nxt("xo", 2)
                n = 128 if t8 < 8 else NS
                gi = 0 if t8 < 4 else (1 if t8 < 8 else 2)
                for k4 in range(4):
                    b = 4 + nxt("ldb", 4)
                    for kk in range(4):
                        kc = k4 * 4 + kk
                        tr(banks[b][0:n, kk * 128:(kk + 1) * 128], xT[:, kc, t8 * 128:t8 * 128 + n], ident,
                           xkeys(kc, gi) + ["cf"], [BK[b]], inc=(kk == 3))
                    cpy("dve" if k4 % 2 == 0 else "act", xo[0:n, s, k4 * 512:(k4 + 1) * 512], banks[b][0:n, :], [BK[b]], [f"xo{s}_{k4}"])
                dst = yp[t8 * 128:(t8 + 1) * 128, :] if t8 < 8 else ys
                dma("sp", dst, xo[0:n, s, :], [f"xo{s}_{k}" for k in range(4)], [], f"xo{s}")
        P.final_waits("sp")

        sems = {}

        def sem_of(k):
            if k not in sems:
                sems[k] = es.enter_context(nc.semaphore(k.replace(":", "_")))
            return sems[k]
        for e in Prog.ENGS:
            sem_of("E:" + e)
        for k in P.dcnt:
            sem_of(k)

        with nc.Block() as block:
            def run(engname):
                def f(eng):
                    for waits, fn, inc in P.q[engname]:
                        for k, v in waits:
                            eng.wait_ge(sems[k], v)
                        if fn is not None:
                            ins = fn(eng)
                            if inc is not None:
                                ins.then_inc(sems[inc[0]], inc[1])
                return f
            block.tensor(run("pe"))
            block.scalar(run("act"))
            block.vector(run("dve"))
            block.gpsimd(run("pool"))
            block.sync(run("sp"))
    return nc


_NC_CACHE = {}


def _host_tables(c):
    pos = c % 4
    r1 = max(pos - 1, 0)
    r2 = max(pos - 2, 0)
    p = np.arange(128)
    idx = np.zeros((128, 112), np.int32)
    for h in range(16):
        idx[:, h] = r1 * 2048 + h * 128 + p
        idx[:, 16 + h] = r1 * 1024 + (h % 8) * 128 + p
        idx[:, 32 + h] = r1 * 512 + (h % 4) * 128 + p
        idx[:, 48 + h] = r2 * 512 + (h % 4) * 128 + p
        idx[:, 64 + h] = (r1 * 16 + h) * 128 + p
        idx[:, 80 + h] = (r1 * 8 + h % 8) * 128 + p
        idx[:, 96 + h] = np.where(p < 64, (r2 * 4 + h % 4) * 64 + p, (r1 * 4 + h % 4) * 64 + (p - 64))
    return idx


def _const_table():
    cf = np.zeros((128, NCF), np.float32)
    k = np.arange(128)[:, None]
    jj = np.arange(256)[None, :]
    dist = (jj - k).astype(np.float32)
    cf[:, CF_D:CF_D + 256] = np.where((dist >= 0) & (dist <= 128), dist, BIG)
    cf[:, CF_DD:CF_DD + 4] = np.where(np.arange(4)[None, :] == k, 0.0, BIG)
    cf[:, CF_ID:CF_ID + 128] = np.eye(128, dtype=np.float32)
    return cf


def _pvec(c, attn_norm, conv_norm, q_gain, k_gain, dw_w, dw_b, ln_g, ln_b):
    pos = c % 4
    pv = np.zeros((128, NPV), np.float32)

    def fm(v):
        return np.ascontiguousarray(v.reshape(16, 128).T)
    for l in range(2):
        pv[:, PV_AN + 16 * l:PV_AN + 16 * l + 16] = fm(attn_norm[l])
        pv[:, PV_CN + 16 * l:PV_CN + 16 * l + 16] = fm(conv_norm[l])
        for g in range(3):
            pv[:, PV_QG + 3 * l + g] = q_gain[l, g]
            pv[:, PV_KG + 3 * l + g] = k_gain[l, g]
        pv[:, PV_DWW + l * 16 * CW:PV_DWW + (l + 1) * 16 * CW] = dw_w[l].T.reshape(16, 128, CW).transpose(1, 0, 2).reshape(128, 16 * CW)
        pv[:, PV_DWB + 16 * l:PV_DWB + 16 * l + 16] = fm(dw_b[l])
        pv[:, PV_LNG + 16 * l:PV_LNG + 16 * l + 16] = fm(ln_g[l])
        pv[:, PV_LNB + 16 * l:PV_LNB + 16 * l + 16] = fm(ln_b[l])
    pv[:, PV_EPS_RMS] = 1e-6
    pv[:, PV_EPS_Q] = 128 * 1e-6
    pv[:, PV_EPS_LN] = 1e-5
    if pos == 0:
        pv[:, PV_LBM:PV_LBM + 3] = NEG
    elif pos == 1:
        pv[0:64, PV_LBM + 2] = NEG
    pv[:, PV_HALO] = 0.0 if pos == 0 else 1.0
    pv[:, PV_ZERO] = 0.0
    return pv


def _make_in_maps(x_prompt, x_sample, cache_kv_w128, cache_kv_w512, cache_kv_w2048, state_conv,
                  attn_norm, attn_w_in, attn_q_gain, attn_k_gain, attn_w_out,
                  conv_norm, conv_w_in, conv_dw_w, conv_dw_b, conv_ln_g, conv_ln_b, conv_w_out):
    f = lambda a: np.ascontiguousarray(np.asarray(a), dtype=np.float32)
    x_prompt, x_sample = f(x_prompt), f(x_sample)
    caches = [f(cache_kv_w128), f(cache_kv_w512), f(cache_kv_w2048)]
    state_conv = f(state_conv)
    attn_w_in, attn_w_out, conv_w_in, conv_w_out = f(attn_w_in), f(attn_w_out), f(conv_w_in), f(conv_w_out)
    attn_norm, conv_norm = f(attn_norm), f(conv_norm)
    attn_q_gain, attn_k_gain = f(attn_q_gain), f(attn_k_gain)
    conv_dw_w, conv_dw_b, conv_ln_g, conv_ln_b = f(conv_dw_w), f(conv_dw_b), f(conv_ln_g), f(conv_ln_b)
    cft = _const_table()
    in_maps = []
    for c in range(NCORES):
        b, pos = c // 4, c % 4
        m = {
            "xp": np.ascontiguousarray(x_prompt[b, pos * NT:(pos + 1) * NT]),
            "xs": np.ascontiguousarray(x_sample[c]),
            "sconv": np.ascontiguousarray(state_conv[:, c]),
            "attn_w_in": attn_w_in, "attn_w_out": attn_w_out, "conv_w_in": conv_w_in, "conv_w_out": conv_w_out,
            "pvec": _pvec(c, attn_norm, conv_norm, attn_q_gain, attn_k_gain, conv_dw_w, conv_dw_b, conv_ln_g, conv_ln_b),
            "cft": cft, "idxt": _host_tables(c),
        }
        for g in range(3):
            m[f"ck{g}"] = np.ascontiguousarray(caches[g][:, c]).reshape(2, GROUPS[g][0], 2 * D)
        in_maps.append(m)
    return in_maps


def _assemble(res):
    y_prompt = np.stack([np.concatenate([res[4 * b + p]["yp"] for p in range(4)], axis=0) for b in range(2)])
    y_sample = np.stack([res[c]["ys"] for c in range(NCORES)])
    kvp = []
    for g in range(3):
        win = GROUPS[g][0]
        per_b = []
        for b in range(2):
            if g < 2:
                a = res[4 * b + 3][f"okv{g}"]
            else:
                a = np.concatenate([res[4 * b + 2]["okv2"], res[4 * b + 3]["okv2"]], axis=1)
            per_b.append(a.reshape(2, win, 2, 16, 128))
        kvp.append(np.stack(per_b, axis=1))
    conv_p = np.stack([res[4 * b + 3]["oconv"] for b in range(2)], axis=1)
    kvs = [np.stack([res[c][f"skv{g}"].reshape(2, GROUPS[g][0], 2, 16, 128) for c in range(NCORES)], axis=1) for g in range(3)]
    conv_s = np.stack([res[c]["sconv_o"] for c in range(NCORES)], axis=1)
    out = (y_prompt, y_sample, kvp[0], kvp[1], kvp[2], conv_p, kvs[0], kvs[1], kvs[2], conv_s)
    return tuple(np.ascontiguousarray(o, dtype=np.float32) for o in out)


def kernel(**inputs):
    if "nc" not in _NC_CACHE:
        _NC_CACHE["nc"] = build_nc()
    nc = _NC_CACHE["nc"]
    in_maps = _make_in_maps(**inputs)
    res = run_bass_kernel_spmd(nc, in_maps, core_ids=list(range(NCORES))).results
    return _assemble(res)
```

```python
import numpy as np
from contextlib import ExitStack
import concourse.bass as bass
import concourse.mybir as mybir
from concourse.bass_utils import run_bass_kernel_spmd

F32 = mybir.dt.float32
BF16 = mybir.dt.bfloat16
I32 = mybir.dt.int32
AF = mybir.ActivationFunctionType
ALU = mybir.AluOpType
AX = mybir.AxisListType

import os
STAGE = int(os.environ.get("KSTAGE", "7"))
NCORES = 8
D = 2048
KC = 16
NT = 1024
NS = 4
NTOK = NT + NS
TG = [(0, 512), (512, 512), (1024, 4)]
GROUPS = [(128, 1), (512, 4), (2048, 16)]
NTAIL = [128, 512, 1024]
NQKV = 3 * 3 * 2048
CW = 31
BIG = 1.0e6
NEG = -30000.0

PV_AN = 0
PV_CN = 32
PV_QG = 64
PV_KG = 70
PV_DWW = 76
PV_DWB = PV_DWW + 2 * 16 * CW
PV_LNG = PV_DWB + 32
PV_LNB = PV_LNG + 32
PV_EPS_RMS = PV_LNB + 32
PV_EPS_Q = PV_EPS_RMS + 1
PV_EPS_LN = PV_EPS_Q + 1
PV_LBM = PV_EPS_LN + 1
PV_HALO = PV_LBM + 3
PV_ZERO = PV_HALO + 1
NPV = PV_ZERO + 1
CF_D = 0
CF_DD = 256
CF_ID = 260
NCF = 260 + 128


def alibi_sd(g, h):
    n = 48
    i = g * 16 + h
    s = float(np.exp2(np.float32(-8.0) * np.float32(i + 1) / np.float32(n)).astype(np.float32))
    return s * GROUPS[g][1]


class Scope:
    def __init__(self, nc, specs):
        self.nc, self.specs = nc, specs

    _uid = [0]

    def __enter__(self):
        self.es = ExitStack()
        Scope._uid[0] += 1
        u = Scope._uid[0]
        return tuple(self.es.enter_context(self.nc.sbuf_tensor(f"{n}_{u}", list(sh), dt)) for (n, sh, dt) in self.specs)

    def __exit__(self, *a):
        self.es.close()
        return False


class Prog:
    ENGS = ("pe", "act", "dve", "pool", "sp")

    def __init__(self):
        self.q = {e: [] for e in self.ENGS}
        self.cnt = {e: 0 for e in self.ENGS}
        self.waited = {e: {} for e in self.ENGS}
        self.lastw = {}
        self.readers = {}
        self.dcnt = {}

    def _deps(self, eng, r, w):
        need = {}
        for b in r:
            for k, v in self.lastw.get(b, {}).items():
                if need.get(k, 0) < v:
                    need[k] = v
        for b in w:
            for k, v in self.lastw.get(b, {}).items():
                if need.get(k, 0) < v:
                    need[k] = v
            for k, v in self.readers.get(b, {}).items():
                if need.get(k, 0) < v:
                    need[k] = v
        waits = []
        wd = self.waited[eng]
        for k, v in need.items():
            if k == "E:pe" and eng == "pe":
                continue
            if k == "E:sp" and eng == "sp":
                continue
            if wd.get(k, 0) >= v:
                continue
            wd[k] = v
            waits.append((k, v))
        return waits

    def _record(self, tk, r, w):
        k, v = tk
        for b in r:
            d = self.readers.setdefault(b, {})
            if d.get(k, 0) < v:
                d[k] = v
        for b in w:
            self.lastw[b] = {k: v}
            self.readers[b] = {}

    def op(self, eng, fn, r=(), w=(), inc=True):
        waits = self._deps(eng, r, w)
        if inc:
            self.cnt[eng] += 1
            tk = ("E:" + eng, self.cnt[eng])
        else:
            tk = ("E:" + eng, self.cnt[eng] + 1)
        self._record(tk, r, w)
        self.q[eng].append((waits, fn, ("E:" + eng, 1) if inc else None))

    def dma(self, q, fn, r, w, sem, n=16):
        waits = self._deps(q, r, w)
        k = "D:" + q + ":" + sem
        self.dcnt[k] = self.dcnt.get(k, 0) + n
        self._record((k, self.dcnt[k]), r, w)
        self.q[q].append((waits, fn, (k, n)))

    def join(self, sem, keys, q="pool"):
        k = "D:" + q + ":" + sem
        for b in keys:
            self.lastw[b] = {k: self.dcnt[k]}

    def barrier(self, skip=None):
        allk = {("E:" + e): self.cnt[e] for e in self.ENGS if self.cnt[e] > 0}
        allk.update({k: v for k, v in self.dcnt.items() if not (skip and skip in k)})
        for e in self.ENGS:
            waits = []
            for k, v in allk.items():
                if k == "E:" + e:
                    continue
                if self.waited[e].get(k, 0) >= v:
                    continue
                self.waited[e][k] = v
                waits.append((k, v))
            if waits:
                self.q[e].append((waits, None, None))

    def final_waits(self, eng):
        waits = []
        for k, v in self.dcnt.items():
            if self.waited[eng].get(k, 0) < v:
                waits.append((k, v))
        for e in self.ENGS:
            if e != eng and self.cnt[e] > 0 and self.waited[eng].get("E:" + e, 0) < self.cnt[e]:
                waits.append(("E:" + e, self.cnt[e]))
        self.q[eng].append((waits, None, None))


def build_nc():
    nc = bass.Bass("TRN2", target_bir_lowering=False)
    P = Prog()

    def din(name, shape, dt=F32):
        return nc.dram_tensor(name, list(shape), dt, kind="ExternalInput").ap()

    def dout(name, shape, dt=F32):
        return nc.dram_tensor(name, list(shape), dt, kind="ExternalOutput").ap()

    def dint(name, shape, dt=BF16):
        return nc.dram_tensor(name, list(shape), dt, kind="Internal").ap()

    FAKE = bool(os.environ.get("KFAKE"))

    class FakeW:
        def __init__(self, ap):
            self.ap = ap

        def __getitem__(self, idx):
            if not isinstance(idx, tuple):
                return self
            if len(idx) == 2:
                rows, cols = idx
                return self.ap[rows, 0:cols.stop - cols.start]
            return self.ap[idx[1], idx[2]]

    def dbig(name, shape):
        if not FAKE:
            return din(name, shape)
        if name.startswith("ck"):
            return FakeW(dint(name, shape[1:], F32))
        return FakeW(dint(name, [D, 512], F32))

    xp = din("xp", [NT, D])
    xs = din("xs", [NS, D])
    ck = [dbig("ck0", [2, 128, 2 * D]), dbig("ck1", [2, 512, 2 * D]), dbig("ck2", [2, 2048, 2 * D])]
    sconv = din("sconv", [2, 30, D])
    w_in_a = dbig("attn_w_in", [2, D, 20480])
    w_out_a = dbig("attn_w_out", [2, D, D])
    w_in_c = dbig("conv_w_in", [2, D, 3 * D])
    w_out_c = dbig("conv_w_out", [2, D, D])
    pvec_d = din("pvec", [128, NPV])
    cf_d = din("cft", [128, NCF])
    idx_d = din("idxt", [128, 112], I32)

    yp = dout("yp", [NT, D])
    ys = dout("ys", [NS, D])
    dobig = (lambda n, sh: dint(n, sh, F32)) if FAKE else dout
    okv = [dobig("okv0", [2, 128, 2 * D]), dobig("okv1", [2, 512, 2 * D]), dobig("okv2", [2, 1024, 2 * D])]
    oconv = dout("oconv", [2, 30, D])
    skv = [dobig("skv0", [2, 128, 2 * D]), dobig("skv1", [2, 512, 2 * D]), dobig("skv2", [2, 2048, 2 * D])]
    sconv_o = dout("sconv_o", [2, 30, D])

    AGC = [1, 2, 4]
    KTown = [[dint(f"KTown{j}_{g}", [2048, NT]) for g in range(2)] for j in range(2)]
    KTown2 = [[dint(f"KTown{j}_2_{c}", [512, NT]) for c in range(4)] for j in range(2)]
    KTt0 = [dint(f"KTt{j}_0", [2048, 128]) for j in range(2)]
    KTt1 = [[dint(f"KTt{j}_1_{c}", [1024, 512]) for c in range(2)] for j in range(2)]
    KTa0 = [dint(f"KTa{j}_0", [4 * 2048, 128]) for j in range(2)]
    KTa1 = [[dint(f"KTa{j}_1_{c}", [4 * 1024, 512]) for c in range(2)] for j in range(2)]
    KTa2 = [[dint(f"KTa{j}_2_{c}", [4 * 512, NT]) for c in range(4)] for j in range(2)]
    Vown = [[dint(f"Vown{j}_{g}", [16 * NT, 128]) for g in range(2)] for j in range(2)]
    Vown2 = [[dint(f"Vown{j}_2_{c}", [4 * NT, 128]) for c in range(4)] for j in range(2)]
    Vt0 = [dint(f"Vt{j}_0", [16 * 128, 128]) for j in range(2)]
    Vt1 = [[dint(f"Vt{j}_1_{c}", [8 * 512, 128]) for c in range(2)] for j in range(2)]
    Va0 = [dint(f"Va{j}_0", [4 * 16 * 128, 128]) for j in range(2)]
    Va1 = [[dint(f"Va{j}_1_{c}", [4 * 8 * 512, 128]) for c in range(2)] for j in range(2)]
    Va2 = [[dint(f"Va{j}_2_{c}", [4 * 4 * NT, 128]) for c in range(4)] for j in range(2)]
    Vsn = [dint(f"Vsn{j}", [NS, 3 * 2048]) for j in range(2)]
    Utl = [dint(f"Utl{j}", [2048, 32], F32) for j in range(2)]
    Ua = [dint(f"Ua{j}", [4 * 2048, 32], F32) for j in range(2)]

    es = ExitStack()

    def sb(name, shape, dt=F32):
        return es.enter_context(nc.sbuf_tensor(name, list(shape), dt))

    with es:
        xT = sb("xT", [128, KC, NTOK])
        hT = sb("hT", [128, KC, NTOK], BF16)
        pv = sb("pv", [128, NPV])
        cf = sb("cf", [128, NCF])
        idx = sb("idx", [128, 112], I32)
        ones_bf = sb("ones_bf", [128, 128], BF16)
        id_bf = sb("id_bf", [128, 128], BF16)
        banks = [es.enter_context(nc.psum_tensor(f"bank{i}", [128, 512], F32)) for i in range(8)]
        BK = [f"B{i}" for i in range(8)]
        ident = cf[:, CF_ID:CF_ID + 128]

        def pvc(col, n=1):
            return pv[:, col:col + n]

        rot = {}

        def nxt(name, n):
            rot[name] = (rot.get(name, -1) + 1) % n
            return rot[name]

        def mm(out, lhsT, rhs, start, stop, r, w, inc, skip=False):
            P.op("pe", lambda e, o=out, l=lhsT, rr=rhs, s=start, t=stop, sk=skip:
                 e.matmul(o, l, rr, start=s, stop=t, skip_group_check=sk), r, w, inc)

        def tr(out, in_, idn, r, w, inc=True):
            P.op("pe", lambda e, o=out, i=in_, d=idn: e.transpose(o, i, d), r, w, inc)

        def act(out, in_, func, r, w, bias=None, scale=1.0):
            if bias is None:
                P.op("act", lambda e, o=out, i=in_, f=func, s=scale: e.activation(o, i, f, scale=s), r, w)
            else:
                P.op("act", lambda e, o=out, i=in_, f=func, s=scale, b=bias: e.activation(o, i, f, bias=b, scale=s), r, w)

        def cpy(eng, out, in_, r, w):
            if eng == "act":
                P.op("act", lambda e, o=out, i=in_: e.activation(o, i, AF.Copy), r, w)
            else:
                P.op(eng, lambda e, o=out, i=in_: e.tensor_copy(o, i), r, w)

        def tt(eng, out, a, b, op, r, w):
            P.op(eng, lambda e, o=out, x=a, y=b, p=op: e.tensor_tensor(o, x, y, p), r, w)

        def ts(eng, out, a, s1, op0, r, w, s2=None, op1=None):
            if op1 is None:
                P.op(eng, lambda e, o=out, x=a, s=s1, p=op0: e.tensor_scalar(o, x, s, None, p), r, w)
            else:
                P.op(eng, lambda e, o=out, x=a, s=s1, p=op0, t=s2, q=op1: e.tensor_scalar(o, x, s, t, p, q), r, w)

        def stt(eng, out, a, sc, b, op0, op1, r, w):
            P.op(eng, lambda e, o=out, x=a, s=sc, y=b, p=op0, q=op1: e.scalar_tensor_tensor(o, x, s, y, p, q), r, w)

        def recip(out, r, w):
            P.op("dve", lambda e, o=out: e.reciprocal(o, o), r, w)

        def mset(eng, out, val, r, w):
            P.op(eng, lambda e, o=out, v=val: e.memset(o, v), r, w)

        def dma(q, out, in_, r, w, sem):
            P.dma(q, lambda e, o=out, i=in_: e.dma_start(out=o, in_=i), r, w, sem)

        def gather(out, in_, idx_ap, r, w, sem):
            P.dma("pool", lambda e, o=out, i=in_, x=idx_ap: e.indirect_dma_start(
                out=o, out_offset=None, in_=i, in_offset=bass.IndirectOffsetOnAxis(ap=x, axis=0)), r, w, sem)

        def allgather(src, dst, r, w, sem):
            if os.environ.get("KNOAG"):
                return
            P.dma("pool", lambda e, s=src, d=dst: e.collective_compute(
                "AllGather", ALU.bypass, replica_groups=[[0, 1, 2, 3], [4, 5, 6, 7]], ins=[s], outs=[d]),
                r, w, sem, n=1)

        def load_w(dst, wsrc, r0, nk, c0, n, key, sem):
            src = wsrc[r0:r0 + nk * 128, c0:c0 + n].rearrange("(k p) c -> p k c", p=128)
            dma("pool", dst, src, [], key if isinstance(key, list) else [key], sem)

        dma("sp", pv[:], pvec_d, [], ["pv"], "pv")
        dma("sp", cf[:], cf_d, [], ["cf"], "cf")
        dma("sp", idx[:], idx_d, [], ["idx"], "idx")
        mset("dve", ones_bf[:], 1.0, [], ["ones"])
        cpy("dve", id_bf[:], ident, ["cf"], ["idbf"])

        def xkeys(kc, gi):
            if gi == 0:
                return [f"xT{kc}_{t}" for t in range(4)]
            if gi == 1:
                return [f"xT{kc}_{t}" for t in range(4, 8)]
            return [f"xT{kc}_8"]

        with nc.sbuf_tensor("xin", [128, 2, D], F32) as xin:
            for t8 in range(9):
                s = nxt("xin", 2)
                n = 128 if t8 < 8 else NS
                src = xp[t8 * 128:(t8 + 1) * 128, :] if t8 < 8 else xs
                dma("sp", xin[0:n, s, :], src, [], [f"xin{s}"], f"xin{s}")
                for k4 in range(4):
                    b = 4 + nxt("ldb", 4)
                    for kk in range(4):
                        kc = k4 * 4 + kk
                        tr(banks[b][:, kk * 128:kk * 128 + n], xin[0:n, s, kc * 128:(kc + 1) * 128], ident[0:n, 0:n],
                           [f"xin{s}", "cf"], [BK[b]], inc=(kk == 3))
                    dst = xT[:, k4 * 4:(k4 + 1) * 4, t8 * 128:t8 * 128 + n]
                    srcp = banks[b][:, :].rearrange("p (k t) -> p k t", k=4)[:, :, 0:n]
                    cpy("dve" if (k4 % 2 == 0) else "act", dst, srcp, [BK[b]],
                        [f"xT{kc_}_{t8}" for kc_ in range(k4 * 4, k4 * 4 + 4)])
            P.barrier()

        def rmsnorm(gcol):
            with Scope(nc, [("sqb", [128, 2, 512], BF16), ("rstd", [128, NTOK], F32)]) as (sqb, rstd,):
                for gi, (t0, n) in enumerate(TG):
                    b = 4 + nxt("ldb", 4)
                    for kc in range(KC):
                        s = nxt("sqb", 2)
                        act(sqb[:, s, 0:n], xT[:, kc, t0:t0 + n], AF.Square, xkeys(kc, gi), [f"sqb{s}"])
                        mm(banks[b][:, 0:n], ones_bf[:], sqb[:, s, 0:n], kc == 0, kc == KC - 1,
                           ["ones", f"sqb{s}"], [BK[b]], inc=True)
                    act(rstd[:, t0:t0 + n], banks[b][:, 0:n], AF.Sqrt, [BK[b], "pv"], [f"rstd{gi}"],
                        bias=pvc(PV_EPS_RMS), scale=1.0 / D)
                    recip(rstd[:, t0:t0 + n], [f"rstd{gi}"], [f"rstd{gi}"])
                    for kc in range(KC):
                        stt("dve", hT[:, kc, t0:t0 + n], xT[:, kc, t0:t0 + n],
                            pvc(gcol + kc), rstd[:, t0:t0 + n], ALU.mult, ALU.mult,
                            xkeys(kc, gi) + [f"rstd{gi}", "pv"], [f"hT{kc}_{gi}"])
                P.barrier()

        def pnorm_rstd(acc_ap, n, sq_ap, sqkey, ssb, rt_ap, rtkey, eps_col, scale):
            act(sq_ap, acc_ap, AF.Square, [sqkey[0]], [sqkey[1]])
            mm(banks[ssb][:, 0:n], ones_bf[:], sq_ap, True, True, ["ones", sqkey[1]], [BK[ssb]], inc=True)
            act(rt_ap, banks[ssb][:, 0:n], AF.Sqrt, [BK[ssb], "pv"], [rtkey], bias=pvc(eps_col), scale=scale)
            recip(rt_ap, [rtkey], [rtkey])

        def attn_layer(j):
            win = w_in_a[j]
            wout = w_out_a[j]
            rmsnorm(PV_AN + 16 * j)
            with Scope(nc, [("gT", [128, 8, NTOK], BF16), ("KTs", [128, 3, 16, NS], BF16)]) as (gT, KTs,):
                if STAGE < 2:
                    return
                with Scope(nc, [("wtk", [128, 2, 3, KC, 128], BF16), ("sq1", [128, 2, 512], BF16), ("rt1", [128, 2, 512], F32), ("Kn", [128, 2, 512], F32), ("KTb", [128, 2, NTOK], BF16), ("Kst", [128, 4, 128], F32)]) as (wtk, sq1, rt1, Kn, KTb, Kst,):
                    def ldk(h):
                        s = h % 2
                        keys = [f"wtk{s}_{g}" for g in range(3)]
                        for g in range(3):
                            load_w(wtk[:, s, g], win, 0, KC, g * 6144 + 2048 + h * 128, 128, keys if g == 0 else [], f"wtk{s}")
                        P.join(f"wtk{s}", keys)
                    ldk(0)
                    ldk(1)
                    for h in range(16):
                        s = h % 2
                        for g in range(3):
                            win_g, d = GROUPS[g]
                            kb = nxt("KTb", 2)
                            for gi, (t0, n) in enumerate(TG):
                                b = nxt("pb", 4)
                                for kc in range(KC):
                                    mm(banks[b][:, 0:n], wtk[:, s, g, kc, :], hT[:, kc, t0:t0 + n], kc == 0, kc == KC - 1,
                                       [f"wtk{s}_{g}", f"hT{kc}_{gi}"], [BK[b]], inc=(kc == KC - 1))
                                q2 = nxt("sq1", 2)
                                pnorm_rstd(banks[b][:, 0:n], n, sq1[:, q2, 0:n], (BK[b], f"sq1{q2}"), 4 + nxt("ssb", 2),
                                           rt1[:, q2, 0:n], f"rt1{q2}", PV_EPS_RMS, 1.0 / 128)
                                stt("dve", Kn[:, q2, 0:n], banks[b][:, 0:n], pvc(PV_KG + 3 * j + g), rt1[:, q2, 0:n],
                                    ALU.mult, ALU.mult, [BK[b], f"rt1{q2}", "pv"], [f"Kn{q2}"])
                                if gi < 2:
                                    dstv = KTb[:, kb, 0:NT].rearrange("p (r i) -> p r i", r=d)[:, :, t0 // d:(t0 + n) // d]
                                    srcv = Kn[:, q2, 0:n].rearrange("p (i r) -> p r i", r=d)
                                    cpy("pool", dstv, srcv, [f"Kn{q2}"], [f"KTb{kb}_{gi}"])
                                else:
                                    cpy("pool", KTb[:, kb, NT:NTOK], Kn[:, q2, 0:n], [f"Kn{q2}"], [f"KTb{kb}_2"])
                                    cpy("pool", KTs[:, g, h, :], Kn[:, q2, 0:n], [f"Kn{q2}"], [f"KTs{g}_{h}"])
                                if gi < 2:
                                    for t4 in range(4):
                                        tt8 = gi * 4 + t4
                                        row0 = tt8 * 128 - (NT - NTAIL[g])
                                        if row0 < 0:
                                            continue
                                        tb = 6 + nxt("tb", 2)
                                        tr(banks[tb][:, 0:128], Kn[:, q2, t4 * 128:(t4 + 1) * 128], ident, [f"Kn{q2}", "cf"], [BK[tb]])
                                        ks = nxt("Kst", 4)
                                        cpy("act", Kst[:, ks, :], banks[tb][:, 0:128], [BK[tb]], [f"Kst{ks}"])
                                        dma("sp", okv[g][j, row0:row0 + 128, h * 128:(h + 1) * 128], Kst[:, ks, :],
                                            [f"Kst{ks}"], [], f"Kst{ks}")
                                else:
                                    tb = 6 + nxt("tb", 2)
                                    tr(banks[tb][0:NS, 0:128], Kn[:, q2, 0:NS], ident, [f"Kn{q2}", "cf"], [BK[tb]])
                                    ks = nxt("Kst", 4)
                                    cpy("act", Kst[0:NS, ks, :], banks[tb][0:NS, 0:128], [BK[tb]], [f"Kst{ks}"])
                                    dma("sp", skv[g][j, win_g - NS:win_g, h * 128:(h + 1) * 128], Kst[0:NS, ks, :],
                                        [f"Kst{ks}"], [], f"Kst{ks}")
                            kr = [f"KTb{kb}_0", f"KTb{kb}_1"]
                            if g < 2:
                                dma("sp", KTown[j][g][h * 128:(h + 1) * 128, :], KTb[:, kb, 0:NT], kr, [f"KTown{g}_{h}"], f"KTb{kb}")
                            else:
                                dma("sp", KTown2[j][h // 4][(h % 4) * 128:(h % 4 + 1) * 128, :], KTb[:, kb, 0:NT], kr, [f"KTown{g}_{h}"], f"KTb{kb}")
                            if g == 0:
                                dma("sp", KTt0[j][h * 128:(h + 1) * 128, :], KTb[:, kb, NT - 128:NT], kr, [f"KTt0_{h}"], f"KTb{kb}")
                            if g == 1:
                                dma("sp", KTt1[j][h // 8][(h % 8) * 128:(h % 8 + 1) * 128, :].rearrange("p (r i) -> p r i", r=4),
                                    KTb[:, kb, 0:NT].rearrange("p (r i) -> p r i", r=4)[:, :, 128:256], kr, [f"KTt1_{h}"], f"KTb{kb}")
                        if h + 2 < 16:
                            ldk(h + 2)
                    for c in range(4):
                        allgather(KTown2[j][c], KTa2[j][c], [f"KTown2_{h}" for h in range(4 * c, 4 * c + 4)], [f"KTa2_{c}"], f"ag{j}")
                    for c in range(2):
                        allgather(KTt1[j][c], KTa1[j][c], [f"KTt1_{h}" for h in range(8 * c, 8 * c + 8)], [f"KTa1_{c}"], f"ag{j}")
                    allgather(KTt0[j], KTa0[j], [f"KTt0_{h}" for h in range(16)], ["KTa0_0"], f"ag{j}")
                    P.barrier(skip=":ag")
                if STAGE < 3:
                    return
                with Scope(nc, [("wtv", [128, 2, KC, 512], BF16), ("Vsb", [128, 3, 512], BF16), ("Vsf", [128, 3, 512], F32)]) as (wtv, Vsb, Vsf,):
                    order = [(g, hq) for g in (2, 1, 0) for hq in range(4)]
                    KV = int(os.environ.get("KV", "0"))
                    realdma = dma

                    def dmaf(bit):
                        return (lambda *a, **k: None) if (KV >> bit) & 1 else realdma

                    def ldv(i):
                        g, hq = order[i]
                        load_w(wtv[:, i % 2], win, 0, KC, g * 6144 + 4096 + hq * 512, 512, f"wtv{i % 2}", f"wtv{i % 2}")
                    ldv(0)
                    ldv(1)
                    if (KV >> 5) & 1:
                        order = order[:2]
                    for i, (g, hq) in enumerate(order):
                        s = i % 2
                        win_g, d = GROUPS[g]
                        for t8 in range(8 if (KV >> 3) & 1 else 9):
                            n = 128 if t8 < 8 else NS
                            gi = 0 if t8 < 4 else (1 if t8 < 8 else 2)
                            b = nxt("pb", 4)
                            for kc in range(KC):
                                mm(banks[b][0:n, :], hT[:, kc, t8 * 128:t8 * 128 + n], wtv[:, s, kc, :], kc == 0, kc == KC - 1,
                                   [f"wtv{s}", f"hT{kc}_{gi}"], [BK[b]], inc=(kc == KC - 1))
                            vs = nxt("Vsb", 3)
                            if not (KV >> 6) & 1:
                                cpy("dve", Vsb[0:n, vs, :], banks[b][0:n, :], [BK[b]], [f"Vsb{vs}"])
                            row0 = t8 * 128 - (NT - NTAIL[g])
                            need_f = (t8 == 8) or row0 >= 0
                            if need_f and not (KV >> 4) & 1:
                                cpy("dve", Vsf[0:n, vs, :], banks[b][0:n, :], [BK[b]], [f"Vsf{vs}"])
                            if t8 < 8:
                                if g < 2:
                                    dst = Vown[j][g].rearrange("(h t) e -> t h e", h=16)[t8 * 128:(t8 + 1) * 128, hq * 4:hq * 4 + 4, :]
                                else:
                                    dst = Vown2[j][hq].rearrange("(h t) e -> t h e", h=4)[t8 * 128:(t8 + 1) * 128, :, :]
                                dmaf(0)("sp", dst, Vsb[:, vs, :].rearrange("p (h e) -> p h e", h=4), [f"Vsb{vs}"],
                                    [f"Vown{g}_{hq}_{t8}"], f"Vsb{vs}")
                                if g < 2 and row0 >= 0:
                                    if g == 0:
                                        dst = Vt0[j].rearrange("(h t) e -> t h e", h=16)[row0:row0 + 128, hq * 4:hq * 4 + 4, :]
                                    else:
                                        dst = Vt1[j][hq // 2].rearrange("(h t) e -> t h e", h=8)[row0:row0 + 128, (hq % 2) * 4:(hq % 2) * 4 + 4, :]
                                    dmaf(0)("sp", dst, Vsb[:, vs, :].rearrange("p (h e) -> p h e", h=4), [f"Vsb{vs}"],
                                        [f"Vt{g}_{hq}_{t8}"], f"Vsb{vs}")
                                if row0 >= 0:
                                    dmaf(1)("sp", okv[g][j, row0:row0 + 128, D + hq * 512:D + (hq + 1) * 512], Vsf[:, vs, :],
                                        [f"Vsf{vs}"], [], f"Vsf{vs}")
                            else:
                                dmaf(2)("sp", Vsn[j][:, g * 2048 + hq * 512:g * 2048 + (hq + 1) * 512], Vsb[0:NS, vs, :],
                                    [f"Vsb{vs}"], [f"Vsn{g}_{hq}"], f"Vsb{vs}")
                                dmaf(1)("sp", skv[g][j, win_g - NS:win_g, D + hq * 512:D + (hq + 1) * 512], Vsf[0:NS, vs, :],
                                    [f"Vsf{vs}"], [], f"Vsf{vs}")
                        if i + 2 < len(order):
                            ldv(i + 2)
                        if hq == 3:
                            if g == 0:
                                allgather(Vt0[j], Va0[j], [f"Vt0_{q}_{t}" for q in range(4) for t in range(8)], ["Va0_0"], f"ag{j}")
                            elif g == 1:
                                for c in range(2):
                                    allgather(Vt1[j][c], Va1[j][c], [f"Vt1_{q}_{t}" for q in (2 * c, 2 * c + 1) for t in range(8)],
                                              [f"Va1_{c}"], f"ag{j}")
                            else:
                                for c in range(4):
                                    allgather(Vown2[j][c], Va2[j][c], [f"Vown2_{c}_{t}" for t in range(8)], [f"Va2_{c}"], f"ag{j}")
                    for g in range(3):
                        win_g = GROUPS[g][0]
                        for r0 in range(0, (win_g - NS) if not os.environ.get("KNOSHIFT") else 0, 128):
                            r1 = min(r0 + 128, win_g - NS)
                            dma("pool", skv[g][j, r0:r1, :], ck[g][j, NS + r0:NS + r1, :], [], [], "cshift")
                    P.barrier()
                if STAGE < 4:
                    return
                with Scope(nc, [("wq", [128, 4, KC, 128], BF16), ("gst", [128, 2048], BF16), ("QT", [128, 3, NT], BF16), ("QTs", [128, 3, NS], BF16), ("sz", [128, NTOK], F32), ("KTx", [128, 5760], BF16), ("Vx", [128, 53, 128], BF16), ("PT", [128, 3, 256], BF16), ("Sb", [128, 3, 256], F32), ("Ksc", [128, 9, 128], F32), ("KsT", [128, 9, 128], BF16), ("Vsc", [128, 9, 128], BF16), ("Vnh", [NS, 3, 128], BF16), ("og", [128, NTOK], F32), ("wo", [128, 2, 8, 128], BF16), ("sq2", [128, 2, 512], BF16), ("rt2", [128, 2, 512], F32)]) as (wq, gst, QT, QTs, sz, KTx, Vx, PT, Sb, Ksc, KsT, Vsc, Vnh, og, wo, sq2, rt2,):
                    KOFF = [0, 1152, 1152 + 1536]
                    VOFF = [0, 9, 21]
                    NCH = [9, 3, 2]

                    def ktx(g, r):
                        d = GROUPS[g][1]
                        ext = 128 + NT // d
                        return KTx[:, KOFF[g] + r * ext:KOFF[g] + (r + 1) * ext]

                    qcols = [(g * 6144 + 0) for g in range(3)] + [NQKV]

                    def ldq(hh, c):
                        k = hh * 4 + c
                        s = k % 4
                        load_w(wq[:, s], win, 0, KC, qcols[c] + hh * 128, 128, f"wq{s}", f"wq{s}")
                    for k in range(4):
                        ldq(k // 4, k % 4)

                    pend = []
                    LAG = 2

                    def unit_b(o_ap, vap, l_ap, nk, nq, s, rk, okey, lkey):
                        mm(o_ap, vap, PT[0:nk, s, 0:nq], False, False, rk[1] + [f"PT{s}"], [okey], inc=False, skip=True)
                        mm(l_ap, ones_bf[0:nk, :], PT[0:nk, s, 0:nq], False, False, ["ones", f"PT{s}"], [lkey], inc=True, skip=True)

                    def flush(keep=0):
                        while len(pend) > keep:
                            unit_b(*pend.pop(0))

                    def unit(ktap, qtap, vap, dap, sd, o_ap, l_ap, nk, nq, rk, okey, lkey, bias=None):
                        b = 4 + nxt("sbk", 2)
                        mm(banks[b][0:nk, 0:nq], ktap, qtap, True, True, rk[0], [BK[b]], inc=True)
                        s = nxt("PT", 3)
                        stt("dve", Sb[0:nk, s, 0:nq], dap, -sd, banks[b][0:nk, 0:nq], ALU.mult, ALU.add,
                            [BK[b], "cf"], [f"Sb{s}"])
                        act(PT[0:nk, s, 0:nq], Sb[0:nk, s, 0:nq], AF.Exp, [f"Sb{s}", "pv"], [f"PT{s}"],
                            bias=(bias if bias is not None else pvc(PV_ZERO)[0:nk, :]))
                        pend.append((o_ap, vap, l_ap, nk, nq, s, rk, okey, lkey))
                        flush(keep=LAG)

                    for h in range(16):
                        kxk = [f"KTx{g}" for g in range(3)]
                        vxk = [f"Vx{g}" for g in range(3)]
                        for g in range(3):
                            win_g, d = GROUPS[g]
                            ext = 128 + NT // d
                            dstk = KTx[:, KOFF[g]:KOFF[g] + d * ext].rearrange("p (r x) -> p r x", r=d)
                            ksrc = KTown[j][g][h * 128:(h + 1) * 128, :] if g < 2 else KTown2[j][h // 4][(h % 4) * 128:(h % 4 + 1) * 128, :]
                            dma("sp", dstk[:, :, 128:ext], ksrc.rearrange("p (r i) -> p r i", r=d),
                                [f"KTown{g}_{h}"], [kxk[g]], f"KTx{g}")
                            vv = Vx[:, VOFF[g]:VOFF[g] + d * NCH[g], :].rearrange("p (r c) e -> p r c e", r=d)
                            vsrc = Vown[j][g][h * NT:(h + 1) * NT, :] if g < 2 else Vown2[j][h // 4][(h % 4) * NT:(h % 4 + 1) * NT, :]
                            vr = [f"Vown{g}_{h // 4}_{t}" for t in range(8)]
                            if g == 0:
                                dma("sp", vv[:, 0, 1:9, :], vsrc.rearrange("(c p) e -> p c e", p=128), vr, [vxk[g]], f"Vx{g}")
                            elif g == 1:
                                for r in range(4):
                                    dma("sp", vv[:, r, 1:3, :], vsrc.rearrange("(c p r) e -> p r c e", p=128, r=4)[:, r],
                                        vr, [vxk[g]], f"Vx{g}")
                            else:
                                dma("sp", vv[0:64, :, 1, :], vsrc.rearrange("(p r) e -> p r e", r=16), vr, [vxk[g]], f"Vx{g}")
                        ia = idx[:, h:h + 1]
                        ik1 = idx[:, 16 + h:17 + h]
                        ik2a = idx[:, 32 + h:33 + h]
                        ik2b = idx[:, 48 + h:49 + h]
                        iv0 = idx[:, 64 + h:65 + h]
                        iv1 = idx[:, 80 + h:81 + h]
                        iv2 = idx[:, 96 + h:97 + h]
                        gather(KTx[:, 0:128], KTa0[j], ia, ["KTa0_0", "idx"], [kxk[0]], "KTx0")
                        d1 = KTx[:, KOFF[1]:KOFF[1] + 4 * 384].rearrange("p (r x) -> p r x", r=4)
                        gather(gst[:, 0:512], KTa1[j][h // 8], ik1, [f"KTa1_{h // 8}", "idx"], ["gst"], "gst")
                        cpy("pool", d1[:, :, 0:128], gst[:, 0:512].rearrange("p (r i) -> p r i", r=4), ["gst"], [kxk[1]])
                        d2 = KTx[:, KOFF[2]:KOFF[2] + 16 * 192].rearrange("p (r x) -> p r x", r=16)
                        gather(gst[:, 0:1024], KTa2[j][h // 4], ik2a, [f"KTa2_{h // 4}", "idx"], ["gst"], "gst")
                        cpy("pool", d2[:, :, 64:128], gst[:, 0:1024].rearrange("p (r i) -> p r i", r=16), ["gst"], [kxk[2]])
                        gather(gst[:, 0:1024], KTa2[j][h // 4], ik2b, [f"KTa2_{h // 4}", "idx"], ["gst"], "gst")
                        cpy("pool", d2[:, :, 0:64], gst[:, 0:1024].rearrange("p (r i) -> p r i", r=16), ["gst"], [kxk[2]])
                        gather(Vx[:, 0, :], Va0[j], iv0, ["Va0_0", "idx"], [vxk[0]], "Vx0")
                        v1 = Vx[:, VOFF[1]:VOFF[1] + 12, :].rearrange("p (r c) e -> p r c e", r=4)
                        gather(gst[:, 0:512], Va1[j][h // 8].rearrange("(n r) e -> n (r e)", r=4), iv1, [f"Va1_{h // 8}", "idx"], ["gst"], "gst")
                        cpy("pool", v1[:, :, 0, :], gst[:, 0:512].rearrange("p (r e) -> p r e", r=4), ["gst"], [vxk[1]])
                        v2 = Vx[:, VOFF[2]:VOFF[2] + 32, :].rearrange("p (r c) e -> p r c e", r=16)
                        gather(gst[:, 0:2048], Va2[j][h // 4].rearrange("(n r) e -> n (r e)", r=16), iv2, [f"Va2_{h // 4}", "idx"], ["gst"], "gst")
                        cpy("pool", v2[:, :, 0, :], gst[:, 0:2048].rearrange("p (r e) -> p r e", r=16), ["gst"], [vxk[2]])
                        ci = 0
                        for g in range(3):
                            win_g, d = GROUPS[g]
                            nres = 1 if g == 0 else NS
                            for r in range(nres):
                                rows = ck[g][j, r:win_g:d, :] if g > 0 else ck[g][j, 0:128, :]
                                dma("sp", Ksc[:, ci, :], rows[:, h * 128:(h + 1) * 128], [], [f"Ksc{ci}"], f"Ksc{ci}")
                                dma("pool", Vsc[:, ci, :], rows[:, D + h * 128:D + (h + 1) * 128], [], [f"Vsc{ci}"], f"Vsc{ci}")
                                ci += 1
                        dma("sp", Vnh[:, :, :], Vsn[j].rearrange("t (g c) -> t g c", g=3)[:, :, h * 128:(h + 1) * 128],
                            [f"Vsn{g}_{h // 4}" for g in range(3)], ["Vnh"], "Vnh")
                        for c in range(4):
                            k = h * 4 + c
                            s = k % 4
                            for gi, (t0, n) in enumerate(TG):
                                b = 6 + nxt("qb", 2)
                                for kc in range(KC):
                                    mm(banks[b][:, 0:n], wq[:, s, kc, :], hT[:, kc, t0:t0 + n], kc == 0, kc == KC - 1,
                                       [f"wq{s}", f"hT{kc}_{gi}"], [BK[b]], inc=(kc == KC - 1))
                                if c == 3:
                                    act(sz[:, t0:t0 + n], banks[b][:, 0:n], AF.Silu, [BK[b]], [f"sz{gi}"])
                                    continue
                                g = c
                                d = GROUPS[g][1]
                                q2 = nxt("sq2", 2)
                                pnorm_rstd(banks[b][:, 0:n], n, sq2[:, q2, 0:n], (BK[b], f"sq2{q2}"), 4 + nxt("sbk", 2),
                                           rt2[:, q2, 0:n], f"rt2{q2}", PV_EPS_Q, 1.0)
                                if gi < 2:
                                    dstv = QT[:, g, :].rearrange("p (r i) -> p r i", r=d)[:, :, t0 // d:(t0 + n) // d]
                                    a_v = banks[b][:, 0:n].rearrange("p (i r) -> p r i", r=d)
                                    r_v = rt2[:, q2, 0:n].rearrange("p (i r) -> p r i", r=d)
                                    stt("dve", dstv, a_v, pvc(PV_QG + 3 * j + g), r_v, ALU.mult, ALU.mult,
                                        [BK[b], f"rt2{q2}", "pv"], [f"QT{g}_{gi}"])
                                else:
                                    stt("dve", QTs[:, g, :], banks[b][:, 0:n], pvc(PV_QG + 3 * j + g), rt2[:, q2, 0:n],
                                        ALU.mult, ALU.mult, [BK[b], f"rt2{q2}", "pv"], [f"QTs{g}"])
                            if k + 4 < 64:
                                ldq((k + 4) // 4, (k + 4) % 4)
                        for ci in range(9):
                            tb = 6 + nxt("qb", 2)
                            tr(banks[tb][:, 0:128], Ksc[:, ci, :], ident, [f"Ksc{ci}", "cf"], [BK[tb]])
                            cpy("dve", KsT[:, ci, :], banks[tb][:, 0:128], [BK[tb]], [f"KsT{ci}"])
                        for b in range(4):
                            mset("dve", banks[b][:, :], 0.0, [], [BK[b]])
                        mset("dve", banks[7][:, 0:16], 0.0, [], [BK[7]])
                        for g in range(3):
                            win_g, d = GROUPS[g]
                            nown = NT // d
                            sd = alibi_sd(g, h)
                            for r in range(d):
                                kt = ktx(g, r)
                                for c in range(NCH[g]):
                                    nk = min(128, 128 + nown - c * 128)
                                    ilo = max(0, 128 * (c - 1))
                                    ihi = min(nown, 128 * (c - 1) + 256)
                                    vap = Vx[0:nk, VOFF[g] + r * NCH[g] + c, :]
                                    bias = pvc(PV_LBM + g) if c == 0 else None
                                    half_n = 512 // d
                                    for hf in range(2):
                                        a = max(ilo, hf * half_n)
                                        e_ = min(ihi, (hf + 1) * half_n)
                                        if a >= e_:
                                            continue
                                        nq = e_ - a
                                        j0 = a - 128 * (c - 1)
                                        col0 = (a - hf * half_n) * d + r
                                        cols = slice(col0, col0 + (nq - 1) * d + 1, d)
                                        unit(kt[:, c * 128:c * 128 + nk], QT[:, g, r * nown + a:r * nown + e_], vap,
                                             cf[0:nk, CF_D + j0:CF_D + j0 + nq], sd,
                                             banks[hf][:, cols], banks[2 + hf][:, cols], nk, nq,
                                             ([f"KTx{g}", f"QT{g}_{hf}"], [f"Vx{g}"]), BK[hf], BK[2 + hf], bias=bias)
                        ci = 0
                        for g in range(3):
                            win_g, d = GROUPS[g]
                            sd = alibi_sd(g, h)
                            nres = 1 if g == 0 else NS
                            for r in range(nres):
                                q0, nq = (0, NS) if g == 0 else (r, 1)
                                unit(KsT[:, ci, :], QTs[:, g, q0:q0 + nq], Vsc[:, ci, :],
                                     cf[:, CF_D + 128 + (0 if g == 0 else 0):CF_D + 128 + nq], sd,
                                     banks[7][:, q0:q0 + nq], banks[7][:, 8 + q0:8 + q0 + nq], 128, nq,
                                     ([f"KsT{ci}", f"QTs{g}"], [f"Vsc{ci}"]), BK[7], BK[7])
                                ci += 1
                            dd = cf[0:NS, CF_D:CF_D + NS] if g == 0 else cf[0:NS, CF_DD:CF_DD + NS]
                            unit(KTs[:, g, h, :], QTs[:, g, :], Vnh[:, g, :], dd, sd,
                                 banks[7][:, 0:NS], banks[7][:, 8:8 + NS], NS, NS,
                                 ([f"KTs{g}_{h}", f"QTs{g}"], ["Vnh"]), BK[7], BK[7])
                        flush()
                        for hf in range(2):
                            cpy("act", og[:, hf * 512:(hf + 1) * 512], banks[2 + hf][:, :], [BK[2 + hf]], [f"og{hf}"])
                            recip(og[:, hf * 512:(hf + 1) * 512], [f"og{hf}"], [f"og{hf}"])
                            tt("dve", og[:, hf * 512:(hf + 1) * 512], og[:, hf * 512:(hf + 1) * 512], banks[hf][:, :], ALU.mult,
                               [f"og{hf}", BK[hf]], [f"og{hf}"])
                            tt("pool", gT[:, h % 8, hf * 512:(hf + 1) * 512], og[:, hf * 512:(hf + 1) * 512],
                               sz[:, hf * 512:(hf + 1) * 512], ALU.mult, [f"og{hf}", f"sz{hf}"], [f"gT{h % 8}_{hf}"])
                        cpy("act", og[:, NT:NTOK], banks[7][:, 8:8 + NS], [BK[7]], ["og2"])
                        recip(og[:, NT:NTOK], ["og2"], ["og2"])
                        tt("dve", og[:, NT:NTOK], og[:, NT:NTOK], banks[7][:, 0:NS], ALU.mult, ["og2", BK[7]], ["og2"])
                        tt("pool", gT[:, h % 8, NT:NTOK], og[:, NT:NTOK], sz[:, NT:NTOK], ALU.mult, ["og2", "sz2"], [f"gT{h % 8}_2"])
                        if h % 8 == 7:
                            hb = h - 7
                            for dc in range(16):
                                s = nxt("wo", 2)
                                load_w(wo[:, s], wout, hb * 128, 8, dc * 128, 128, f"wo{s}", f"wo{s}")
                                for gi, (t0, n) in enumerate(TG):
                                    b = 6 + nxt("qb", 2)
                                    for hh in range(8):
                                        mm(banks[b][:, 0:n], wo[:, s, hh, :], gT[:, hh, t0:t0 + n], hh == 0, hh == 7,
                                           [f"wo{s}", f"gT{hh}_{gi}"], [BK[b]], inc=(hh == 7))
                                    tt("dve", xT[:, dc, t0:t0 + n], xT[:, dc, t0:t0 + n], banks[b][:, 0:n], ALU.add,
                                       xkeys(dc, gi) + [BK[b]], xkeys(dc, gi))
                    P.barrier()

        def conv_layer(j):
            win = w_in_c[j]
            wout = w_out_c[j]
            rmsnorm(PV_CN + 16 * j)
            dwc = PV_DWW + j * 16 * CW
            with Scope(nc, [("cT", [128, KC, NTOK], BF16), ("szc", [128, KC, NTOK], BF16), ("Us", [128, KC, 34], F32), ("Uh", [128, KC, 64], F32)]) as (cT, szc, Us, Uh,):
                with Scope(nc, [("wc", [128, 3, KC, 128], BF16), ("Ub", [128, NT], F32), ("sg", [128, 2, 512], F32), ("ac", [128, 2, NT], F32), ("prs", [128, NS * CW], F32), ("cs", [128, 32], F32)]) as (wc, Ub, sg, ac, prs, cs,):
                    st = bass.AP(ac, 0, [list(ac[0:32, 0, 0:1].ap[0]), [1, D]])

                    def ldc(k):
                        cc, a = k // 3, k % 3
                        load_w(wc[:, a], win, 0, KC, a * D + cc * 128, 128, f"wc{a}", f"wc{a}")
                    for k in range(3):
                        ldc(k)
                    dma("sp", st[0:30, :], sconv[j], [], ["st"], "st")
                    for cc in range(KC):
                        tb = 6 + nxt("qb", 2)
                        tr(banks[tb][:, 0:30], st[0:30, cc * 128:(cc + 1) * 128], ident[0:30, 0:30], ["st", "cf"], [BK[tb]])
                        cpy("act", Us[:, cc, 0:30], banks[tb][:, 0:30], [BK[tb]], [f"Us{cc}"])
                    dma("pool", sconv_o[j, 0:26, :], sconv[j, 4:30, :], [], [], "cshift")
                    P.barrier()
                    for cc in range(KC):
                        for a in range(3):
                            for gi, (t0, n) in enumerate(TG):
                                b = nxt("pb", 6)
                                for kc in range(KC):
                                    mm(banks[b][:, 0:n], wc[:, a, kc, :], hT[:, kc, t0:t0 + n], kc == 0, kc == KC - 1,
                                       [f"wc{a}", f"hT{kc}_{gi}"], [BK[b]], inc=(kc == KC - 1))
                                udst = Ub[:, t0:t0 + n] if gi < 2 else Us[:, cc, 30:34]
                                ukey = f"Ub_{gi}" if gi < 2 else f"Us{cc}"
                                if a == 0:
                                    cpy("act", udst, banks[b][:, 0:n], [BK[b]], [ukey])
                                elif a == 1:
                                    q2 = nxt("sg", 2)
                                    act(sg[:, q2, 0:n], banks[b][:, 0:n], AF.Sigmoid, [BK[b]], [f"sg{q2}"])
                                    tt("dve", udst, udst, sg[:, q2, 0:n], ALU.mult, [f"sg{q2}", ukey], [ukey])
                                else:
                                    act(szc[:, cc, t0:t0 + n], banks[b][:, 0:n], AF.Silu, [BK[b]], [f"szc{cc}_{gi}"])
                            if cc + 1 < KC:
                                ldc((cc + 1) * 3 + a)
                        uk = ["Ub_0", "Ub_1"]
                        cpy("pool", Uh[:, cc, 32:64], Ub[:, 0:32], uk, [f"Uh{cc}"])
                        dma("sp", Utl[j][cc * 128:(cc + 1) * 128, :], Ub[:, NT - 32:NT], uk, [f"Utl{cc}"], "Ub")
                        L = NT - 30
                        a0 = ac[:, 0, 0:L]
                        a1 = ac[:, 1, 0:L]
                        ts("dve", a0, Ub[:, 0:L], pvc(dwc + cc * CW + 0), ALU.mult, uk + ["pv"], ["ac0"],
                           s2=pvc(PV_DWB + 16 * j + cc), op1=ALU.add)
                        for tap in range(1, CW):
                            stt("dve", a0, Ub[:, tap:tap + L], pvc(dwc + cc * CW + tap), a0, ALU.mult, ALU.add, uk + ["ac0", "pv"], ["ac0"])
                        cpy("pool", cT[:, cc, 30:NT], a0, ["ac0"], [f"cT{cc}_m"])
                        base = Us[:, cc, 0:1]
                        win_ap = bass.AP(Us, base.offset, [list(base.ap[0]), [1, NS], [1, CW]])
                        wcol = pvc(dwc + cc * CW, CW)
                        w_ap = bass.AP(pv, wcol.offset, [list(wcol.ap[0]), [0, NS], [1, CW]])
                        pr = prs[:, :].rearrange("p (t k) -> p t k", k=CW)
                        tt("dve", pr, win_ap, w_ap, ALU.mult, [f"Us{cc}", "pv"], ["prs"])
                        P.op("dve", lambda e, o=cs[:, 0:NS], i=pr: e.tensor_reduce(o, i, AX.X, ALU.add), ["prs"], ["cs"])
                        ts("dve", cT[:, cc, NT:NTOK], cs[:, 0:NS], pvc(PV_DWB + 16 * j + cc), ALU.add, ["cs", "pv"], [f"cT{cc}_s"])
                    allgather(Utl[j], Ua[j], [f"Utl{cc}" for cc in range(KC)], ["Ua"], f"agu{j}")
                    for cc in range(KC):
                        gather(Uh[:, cc, 0:32], Ua[j], idx[:, cc:cc + 1], ["Ua", "idx"], [f"Uh{cc}"] if cc else [f"Uh{c_}" for c_ in range(KC)], "Uh")
                    P.join("Uh", [f"Uh{cc}" for cc in range(KC)])
                    prb = bass.AP(sg, 0, [list(sg[:, 0, 0:1].ap[0]), [CW, 30], [1, CW]])
                    for cc in range(KC):
                        ts("dve", Uh[:, cc, 0:32], Uh[:, cc, 0:32], pvc(PV_HALO), ALU.mult, [f"Uh{cc}", "pv"], [f"Uh{cc}"])
                        base = Uh[:, cc, 2:3]
                        win_ap = bass.AP(Uh, base.offset, [list(base.ap[0]), [1, 30], [1, CW]])
                        wcol = pvc(dwc + cc * CW, CW)
                        w_ap = bass.AP(pv, wcol.offset, [list(wcol.ap[0]), [0, 30], [1, CW]])
                        tt("dve", prb, win_ap, w_ap, ALU.mult, [f"Uh{cc}", "pv", "sg0", "sg1"], ["sg0", "sg1"])
                        P.op("dve", lambda e, o=cs[:, 0:30], i=prb: e.tensor_reduce(o, i, AX.X, ALU.add), ["sg0", "sg1"], ["cs"])
                        ts("dve", cT[:, cc, 0:30], cs[:, 0:30], pvc(PV_DWB + 16 * j + cc), ALU.add, ["cs", "pv"], [f"cT{cc}_h"])
                    P.barrier()
                    dma("sp", Uh[:, :, 0:32], Utl[j].rearrange("(c p) t -> p c t", p=128), [], [f"Uh{cc}" for cc in range(KC)], "Uh")
                    for cc in range(KC):
                        tb = 6 + nxt("qb", 2)
                        tr(banks[tb][0:32, 0:128], Uh[:, cc, 0:32], ident, [f"Uh{cc}", "cf"], [BK[tb]])
                        cpy("act", st[0:32, cc * 128:(cc + 1) * 128], banks[tb][0:32, 0:128], [BK[tb]], ["st"])
                    dma("sp", oconv[j], st[2:32, :], ["st"], [], "st")
                    for cc in range(KC):
                        tb = 6 + nxt("qb", 2)
                        tr(banks[tb][0:NS, 0:128], Us[:, cc, 30:34], ident, [f"Us{cc}", "cf"], [BK[tb]])
                        cpy("act", st[0:NS, cc * 128:(cc + 1) * 128], banks[tb][0:NS, 0:128], [BK[tb]], ["st"])
                    dma("sp", sconv_o[j, 26:30, :], st[0:NS, :], ["st"], [], "st")
                    P.barrier()
                with Scope(nc, [("Sq16", [128, 2, 512], BF16), ("sg2", [128, 2, 512], F32), ("mean", [128, NTOK], F32), ("rs", [128, NTOK], F32), ("wo2", [128, 2, KC, 128], BF16)]) as (Sq16, sg2, mean, rs, wo2,):
                    def ckeys(cc, gi):
                        return [f"cT{cc}_m", f"cT{cc}_h"] if gi == 0 else ([f"cT{cc}_m"] if gi == 1 else [f"cT{cc}_s"])
                    for gi, (t0, n) in enumerate(TG):
                        b1 = 0 + gi % 2
                        b2 = 2 + gi % 2
                        for cc in range(KC):
                            mm(banks[b1][:, 0:n], ones_bf[:], cT[:, cc, t0:t0 + n], cc == 0, cc == KC - 1,
                               ["ones"] + ckeys(cc, gi), [BK[b1]], inc=True)
                            q2 = nxt("sqc", 2)
                            act(Sq16[:, q2, 0:n], cT[:, cc, t0:t0 + n], AF.Square, ckeys(cc, gi), [f"sq16{q2}"])
                            mm(banks[b2][:, 0:n], ones_bf[:], Sq16[:, q2, 0:n], cc == 0, cc == KC - 1,
                               ["ones", f"sq16{q2}"], [BK[b2]], inc=True)
                        mk = f"mean{gi}"
                        rk = f"rs{gi}"
                        m_ap = mean[:, t0:t0 + n]
                        r_ap = rs[:, t0:t0 + n]
                        ts("dve", m_ap, banks[b1][:, 0:n], 1.0 / D, ALU.mult, [BK[b1]], [mk])
                        tt("dve", r_ap, m_ap, m_ap, ALU.mult, [mk], [rk])
                        stt("dve", r_ap, banks[b2][:, 0:n], 1.0 / D, r_ap, ALU.mult, ALU.subtract, [BK[b2], rk], [rk])
                        act(r_ap, r_ap, AF.Sqrt, [rk, "pv"], [rk], bias=pvc(PV_EPS_LN), scale=1.0)
                        recip(r_ap, [rk], [rk])
                        for cc in range(KC):
                            q2 = nxt("sg2", 2)
                            tmp = sg2[:, q2, 0:n]
                            tt("dve", tmp, cT[:, cc, t0:t0 + n], m_ap, ALU.subtract, ckeys(cc, gi) + [mk], [f"sg2{q2}"])
                            tt("pool", tmp, tmp, r_ap, ALU.mult, [f"sg2{q2}", rk], [f"sg2{q2}"])
                            act(tmp, tmp, AF.Silu, [f"sg2{q2}", "pv"], [f"sg2{q2}"], bias=pvc(PV_LNB + 16 * j + cc),
                                scale=pvc(PV_LNG + 16 * j + cc))
                            tt("dve", hT[:, cc, t0:t0 + n], tmp, szc[:, cc, t0:t0 + n], ALU.mult,
                               [f"sg2{q2}", f"szc{cc}_{gi}"], [f"hT{cc}_{gi}"])
                    for dc in range(16):
                        s = dc % 2
                        load_w(wo2[:, s], wout, 0, KC, dc * 128, 128, f"wo2{s}", f"wo2{s}")
                        for gi, (t0, n) in enumerate(TG):
                            b = 4 + nxt("ldb", 4)
                            for cc in range(KC):
                                mm(banks[b][:, 0:n], wo2[:, s, cc, :], hT[:, cc, t0:t0 + n], cc == 0, cc == KC - 1,
                                   [f"wo2{s}", f"hT{cc}_{gi}"], [BK[b]], inc=(cc == KC - 1))
                            tt("dve", xT[:, dc, t0:t0 + n], xT[:, dc, t0:t0 + n], banks[b][:, 0:n], ALU.add,
                               xkeys(dc, gi) + [BK[b]], xkeys(dc, gi))
                    P.barrier()


        if STAGE >= 1:
            attn_layer(0)
        if STAGE >= 5:
            conv_layer(0)
        if STAGE >= 6:
            attn_layer(1)
        if STAGE >= 7:
            conv_layer(1)

        with nc.sbuf_tensor("xo", [128, 2, D], F32) as xo:
            for t8 in range(9):
                s = nxt("xo", 2)
                n = 128 if t8 < 8 else NS
                gi = 0 if t8 < 4 else (1 if t8 < 8 else 2)
                for k4 in range(4):
                    b = 4 + nxt("ldb", 4)
                    for kk in range(4):
                        kc = k4 * 4 + kk
                        tr(banks[b][0:n, kk * 128:(kk + 1) * 128], xT[:, kc, t8 * 128:t8 * 128 + n], ident,
                           xkeys(kc, gi) + ["cf"], [BK[b]], inc=(kk == 3))
                    cpy("dve" if k4 % 2 == 0 else "act", xo[0:n, s, k4 * 512:(k4 + 1) * 512], banks[b][0:n, :], [BK[b]], [f"xo{s}_{k4}"])
                dst = yp[t8 * 128:(t8 + 1) * 128, :] if t8 < 8 else ys
                dma("sp", dst, xo[0:n, s, :], [f"xo{s}_{k}" for k in range(4)], [], f"xo{s}")
        P.final_waits("sp")

        sems = {}

        def sem_of(k):
            if k not in sems:
                sems[k] = es.enter_context(nc.semaphore(k.replace(":", "_")))
            return sems[k]
        for e in Prog.ENGS:
            sem_of("E:" + e)
        for k in P.dcnt:
            sem_of(k)

        with nc.Block() as block:
            def run(engname):
                def f(eng):
                    for waits, fn, inc in P.q[engname]:
                        for k, v in waits:
                            eng.wait_ge(sems[k], v)
                        if fn is not None:
                            ins = fn(eng)
                            if inc is not None:
                                ins.then_inc(sems[inc[0]], inc[1])
                return f
            block.tensor(run("pe"))
            block.scalar(run("act"))
            block.vector(run("dve"))
            block.gpsimd(run("pool"))
            block.sync(run("sp"))
    return nc


_NC_CACHE = {}


def _host_tables(c):
    pos = c % 4
    r1 = max(pos - 1, 0)
    r2 = max(pos - 2, 0)
    p = np.arange(128)
    idx = np.zeros((128, 112), np.int32)
    for h in range(16):
        idx[:, h] = r1 * 2048 + h * 128 + p
        idx[:, 16 + h] = r1 * 1024 + (h % 8) * 128 + p
        idx[:, 32 + h] = r1 * 512 + (h % 4) * 128 + p
        idx[:, 48 + h] = r2 * 512 + (h % 4) * 128 + p
        idx[:, 64 + h] = (r1 * 16 + h) * 128 + p
        idx[:, 80 + h] = (r1 * 8 + h % 8) * 128 + p
        idx[:, 96 + h] = np.where(p < 64, (r2 * 4 + h % 4) * 64 + p, (r1 * 4 + h % 4) * 64 + (p - 64))
    return idx


def _const_table():
    cf = np.zeros((128, NCF), np.float32)
    k = np.arange(128)[:, None]
    jj = np.arange(256)[None, :]
    dist = (jj - k).astype(np.float32)
    cf[:, CF_D:CF_D + 256] = np.where((dist >= 0) & (dist <= 128), dist, BIG)
    cf[:, CF_DD:CF_DD + 4] = np.where(np.arange(4)[None, :] == k, 0.0, BIG)
    cf[:, CF_ID:CF_ID + 128] = np.eye(128, dtype=np.float32)
    return cf


def _pvec(c, attn_norm, conv_norm, q_gain, k_gain, dw_w, dw_b, ln_g, ln_b):
    pos = c % 4
    pv = np.zeros((128, NPV), np.float32)

    def fm(v):
        return np.ascontiguousarray(v.reshape(16, 128).T)
    for l in range(2):
        pv[:, PV_AN + 16 * l:PV_AN + 16 * l + 16] = fm(attn_norm[l])
        pv[:, PV_CN + 16 * l:PV_CN + 16 * l + 16] = fm(conv_norm[l])
        for g in range(3):
            pv[:, PV_QG + 3 * l + g] = q_gain[l, g]
            pv[:, PV_KG + 3 * l + g] = k_gain[l, g]
        pv[:, PV_DWW + l * 16 * CW:PV_DWW + (l + 1) * 16 * CW] = dw_w[l].T.reshape(16, 128, CW).transpose(1, 0, 2).reshape(128, 16 * CW)
        pv[:, PV_DWB + 16 * l:PV_DWB + 16 * l + 16] = fm(dw_b[l])
        pv[:, PV_LNG + 16 * l:PV_LNG + 16 * l + 16] = fm(ln_g[l])
        pv[:, PV_LNB + 16 * l:PV_LNB + 16 * l + 16] = fm(ln_b[l])
    pv[:, PV_EPS_RMS] = 1e-6
    pv[:, PV_EPS_Q] = 128 * 1e-6
    pv[:, PV_EPS_LN] = 1e-5
    if pos == 0:
        pv[:, PV_LBM:PV_LBM + 3] = NEG
    elif pos == 1:
        pv[0:64, PV_LBM + 2] = NEG
    pv[:, PV_HALO] = 0.0 if pos == 0 else 1.0
    pv[:, PV_ZERO] = 0.0
    return pv


def _make_in_maps(x_prompt, x_sample, cache_kv_w128, cache_kv_w512, cache_kv_w2048, state_conv,
                  attn_norm, attn_w_in, attn_q_gain, attn_k_gain, attn_w_out,
                  conv_norm, conv_w_in, conv_dw_w, conv_dw_b, conv_ln_g, conv_ln_b, conv_w_out):
    f = lambda a: np.ascontiguousarray(np.asarray(a), dtype=np.float32)
    x_prompt, x_sample = f(x_prompt), f(x_sample)
    caches = [f(cache_kv_w128), f(cache_kv_w512), f(cache_kv_w2048)]
    state_conv = f(state_conv)
    attn_w_in, attn_w_out, conv_w_in, conv_w_out = f(attn_w_in), f(attn_w_out), f(conv_w_in), f(conv_w_out)
    attn_norm, conv_norm = f(attn_norm), f(conv_norm)
    attn_q_gain, attn_k_gain = f(attn_q_gain), f(attn_k_gain)
    conv_dw_w, conv_dw_b, conv_ln_g, conv_ln_b = f(conv_dw_w), f(conv_dw_b), f(conv_ln_g), f(conv_ln_b)
    cft = _const_table()
    in_maps = []
    for c in range(NCORES):
        b, pos = c // 4, c % 4
        m = {
            "xp": np.ascontiguousarray(x_prompt[b, pos * NT:(pos + 1) * NT]),
            "xs": np.ascontiguousarray(x_sample[c]),
            "sconv": np.ascontiguousarray(state_conv[:, c]),
            "attn_w_in": attn_w_in, "attn_w_out": attn_w_out, "conv_w_in": conv_w_in, "conv_w_out": conv_w_out,
            "pvec": _pvec(c, attn_norm, conv_norm, attn_q_gain, attn_k_gain, conv_dw_w, conv_dw_b, conv_ln_g, conv_ln_b),
            "cft": cft, "idxt": _host_tables(c),
        }
        for g in range(3):
            m[f"ck{g}"] = np.ascontiguousarray(caches[g][:, c]).reshape(2, GROUPS[g][0], 2 * D)
        in_maps.append(m)
    return in_maps


def _assemble(res):
    y_prompt = np.stack([np.concatenate([res[4 * b + p]["yp"] for p in range(4)], axis=0) for b in range(2)])
    y_sample = np.stack([res[c]["ys"] for c in range(NCORES)])
    kvp = []
    for g in range(3):
        win = GROUPS[g][0]
        per_b = []
        for b in range(2):
            if g < 2:
                a = res[4 * b + 3][f"okv{g}"]
            else:
                a = np.concatenate([res[4 * b + 2]["okv2"], res[4 * b + 3]["okv2"]], axis=1)
            per_b.append(a.reshape(2, win, 2, 16, 128))
        kvp.append(np.stack(per_b, axis=1))
    conv_p = np.stack([res[4 * b + 3]["oconv"] for b in range(2)], axis=1)
    kvs = [np.stack([res[c][f"skv{g}"].reshape(2, GROUPS[g][0], 2, 16, 128) for c in range(NCORES)], axis=1) for g in range(3)]
    conv_s = np.stack([res[c]["sconv_o"] for c in range(NCORES)], axis=1)
    out = (y_prompt, y_sample, kvp[0], kvp[1], kvp[2], conv_p, kvs[0], kvs[1], kvs[2], conv_s)
    return tuple(np.ascontiguousarray(o, dtype=np.float32) for o in out)


def kernel(**inputs):
    if "nc" not in _NC_CACHE:
        _NC_CACHE["nc"] = build_nc()
    nc = _NC_CACHE["nc"]
    in_maps = _make_in_maps(**inputs)
    res = run_bass_kernel_spmd(nc, in_maps, core_ids=list(range(NCORES))).results
    return _assemble(res)
```

```python
import numpy as np
from contextlib import ExitStack
import concourse.bass as bass
import concourse.mybir as mybir
from concourse.bass_utils import run_bass_kernel_spmd

F32 = mybir.dt.float32
BF16 = mybir.dt.bfloat16
I32 = mybir.dt.int32
AF = mybir.ActivationFunctionType
ALU = mybir.AluOpType
AX = mybir.AxisListType

import os
STAGE = int(os.environ.get("KSTAGE", "7"))
NCORES = 8
D = 2048
KC = 16
NT = 1024
NS = 4
NTOK = NT + NS
TG = [(0, 512), (512, 512), (1024, 4)]
GROUPS = [(128, 1), (512, 4), (2048, 16)]
NTAIL = [128, 512, 1024]
NQKV = 3 * 3 * 2048
CW = 31
BIG = 1.0e6
NEG = -30000.0

PV_AN = 0
PV_CN = 32
PV_QG = 64
PV_KG = 70
PV_DWW = 76
PV_DWB = PV_DWW + 2 * 16 * CW
PV_LNG = PV_DWB + 32
PV_LNB = PV_LNG + 32
PV_EPS_RMS = PV_LNB + 32
PV_EPS_Q = PV_EPS_RMS + 1
PV_EPS_LN = PV_EPS_Q + 1
PV_LBM = PV_EPS_LN + 1
PV_HALO = PV_LBM + 3
PV_ZERO = PV_HALO + 1
NPV = PV_ZERO + 1
CF_D = 0
CF_DD = 256
CF_ID = 260
NCF = 260 + 128


def alibi_sd(g, h):
    n = 48
    i = g * 16 + h
    s = float(np.exp2(np.float32(-8.0) * np.float32(i + 1) / np.float32(n)).astype(np.float32))
    return s * GROUPS[g][1]


class Scope:
    def __init__(self, nc, specs):
        self.nc, self.specs = nc, specs

    _uid = [0]

    def __enter__(self):
        self.es = ExitStack()
        Scope._uid[0] += 1
        u = Scope._uid[0]
        return tuple(self.es.enter_context(self.nc.sbuf_tensor(f"{n}_{u}", list(sh), dt)) for (n, sh, dt) in self.specs)

    def __exit__(self, *a):
        self.es.close()
        return False


class Prog:
    ENGS = ("pe", "act", "dve", "pool", "sp")

    def __init__(self):
        self.q = {e: [] for e in self.ENGS}
        self.cnt = {e: 0 for e in self.ENGS}
        self.waited = {e: {} for e in self.ENGS}
        self.lastw = {}
        self.readers = {}
        self.dcnt = {}

    def _deps(self, eng, r, w):
        need = {}
        for b in r:
            for k, v in self.lastw.get(b, {}).items():
                if need.get(k, 0) < v:
                    need[k] = v
        for b in w:
            for k, v in self.lastw.get(b, {}).items():
                if need.get(k, 0) < v:
                    need[k] = v
            for k, v in self.readers.get(b, {}).items():
                if need.get(k, 0) < v:
                    need[k] = v
        waits = []
        wd = self.waited[eng]
        for k, v in need.items():
            if k == "E:pe" and eng == "pe":
                continue
            if k == "E:sp" and eng == "sp":
                continue
            if wd.get(k, 0) >= v:
                continue
            wd[k] = v
            waits.append((k, v))
        return waits

    def _record(self, tk, r, w):
        k, v = tk
        for b in r:
            d = self.readers.setdefault(b, {})
            if d.get(k, 0) < v:
                d[k] = v
        for b in w:
            self.lastw[b] = {k: v}
            self.readers[b] = {}

    def op(self, eng, fn, r=(), w=(), inc=True):
        waits = self._deps(eng, r, w)
        if inc:
            self.cnt[eng] += 1
            tk = ("E:" + eng, self.cnt[eng])
        else:
            tk = ("E:" + eng, self.cnt[eng] + 1)
        self._record(tk, r, w)
        self.q[eng].append((waits, fn, ("E:" + eng, 1) if inc else None))

    def dma(self, q, fn, r, w, sem, n=16):
        waits = self._deps(q, r, w)
        k = "D:" + q + ":" + sem
        self.dcnt[k] = self.dcnt.get(k, 0) + n
        self._record((k, self.dcnt[k]), r, w)
        self.q[q].append((waits, fn, (k, n)))

    def join(self, sem, keys, q="pool"):
        k = "D:" + q + ":" + sem
        for b in keys:
            self.lastw[b] = {k: self.dcnt[k]}

    def barrier(self, skip=None):
        allk = {("E:" + e): self.cnt[e] for e in self.ENGS if self.cnt[e] > 0}
        allk.update({k: v for k, v in self.dcnt.items() if not (skip and skip in k)})
        for e in self.ENGS:
            waits = []
            for k, v in allk.items():
                if k == "E:" + e:
                    continue
                if self.waited[e].get(k, 0) >= v:
                    continue
                self.waited[e][k] = v
                waits.append((k, v))
            if waits:
                self.q[e].append((waits, None, None))

    def final_waits(self, eng):
        waits = []
        for k, v in self.dcnt.items():
            if self.waited[eng].get(k, 0) < v:
                waits.append((k, v))
        for e in self.ENGS:
            if e != eng and self.cnt[e] > 0 and self.waited[eng].get("E:" + e, 0) < self.cnt[e]:
                waits.append(("E:" + e, self.cnt[e]))
        self.q[eng].append((waits, None, None))


def build_nc():
    nc = bass.Bass("TRN2", target_bir_lowering=False)
    P = Prog()

    def din(name, shape, dt=F32):
        return nc.dram_tensor(name, list(shape), dt, kind="ExternalInput").ap()

    def dout(name, shape, dt=F32):
        return nc.dram_tensor(name, list(shape), dt, kind="ExternalOutput").ap()

    def dint(name, shape, dt=BF16):
        return nc.dram_tensor(name, list(shape), dt, kind="Internal").ap()

    FAKE = bool(os.environ.get("KFAKE"))

    class FakeW:
        def __init__(self, ap):
            self.ap = ap

        def __getitem__(self, idx):
            if not isinstance(idx, tuple):
                return self
            if len(idx) == 2:
                rows, cols = idx
                return self.ap[rows, 0:cols.stop - cols.start]
            return self.ap[idx[1], idx[2]]

    def dbig(name, shape):
        if not FAKE:
            return din(name, shape)
        if name.startswith("ck"):
            return FakeW(dint(name, shape[1:], F32))
        return FakeW(dint(name, [D, 512], F32))

    xp = din("xp", [NT, D])
    xs = din("xs", [NS, D])
    ck = [dbig("ck0", [2, 128, 2 * D]), dbig("ck1", [2, 512, 2 * D]), dbig("ck2", [2, 2048, 2 * D])]
    sconv = din("sconv", [2, 30, D])
    w_in_a = dbig("attn_w_in", [2, D, 20480])
    w_out_a = dbig("attn_w_out", [2, D, D])
    w_in_c = dbig("conv_w_in", [2, D, 3 * D])
    w_out_c = dbig("conv_w_out", [2, D, D])
    pvec_d = din("pvec", [128, NPV])
    cf_d = din("cft", [128, NCF])
    idx_d = din("idxt", [128, 112], I32)

    yp = dout("yp", [NT, D])
    ys = dout("ys", [NS, D])
    dobig = (lambda n, sh: dint(n, sh, F32)) if FAKE else dout
    okv = [dobig("okv0", [2, 128, 2 * D]), dobig("okv1", [2, 512, 2 * D]), dobig("okv2", [2, 1024, 2 * D])]
    oconv = dout("oconv", [2, 30, D])
    skv = [dobig("skv0", [2, 128, 2 * D]), dobig("skv1", [2, 512, 2 * D]), dobig("skv2", [2, 2048, 2 * D])]
    sconv_o = dout("sconv_o", [2, 30, D])

    AGC = [1, 2, 4]
    KTown = [[dint(f"KTown{j}_{g}", [2048, NT]) for g in range(2)] for j in range(2)]
    KTown2 = [[dint(f"KTown{j}_2_{c}", [512, NT]) for c in range(4)] for j in range(2)]
    KTt0 = [dint(f"KTt{j}_0", [2048, 128]) for j in range(2)]
    KTt1 = [[dint(f"KTt{j}_1_{c}", [1024, 512]) for c in range(2)] for j in range(2)]
    KTa0 = [dint(f"KTa{j}_0", [4 * 2048, 128]) for j in range(2)]
    KTa1 = [[dint(f"KTa{j}_1_{c}", [4 * 1024, 512]) for c in range(2)] for j in range(2)]
    KTa2 = [[dint(f"KTa{j}_2_{c}", [4 * 512, NT]) for c in range(4)] for j in range(2)]
    Vown = [[dint(f"Vown{j}_{g}", [16 * NT, 128]) for g in range(2)] for j in range(2)]
    Vown2 = [[dint(f"Vown{j}_2_{c}", [4 * NT, 128]) for c in range(4)] for j in range(2)]
    Vt0 = [dint(f"Vt{j}_0", [16 * 128, 128]) for j in range(2)]
    Vt1 = [[dint(f"Vt{j}_1_{c}", [8 * 512, 128]) for c in range(2)] for j in range(2)]
    Va0 = [dint(f"Va{j}_0", [4 * 16 * 128, 128]) for j in range(2)]
    Va1 = [[dint(f"Va{j}_1_{c}", [4 * 8 * 512, 128]) for c in range(2)] for j in range(2)]
    Va2 = [[dint(f"Va{j}_2_{c}", [4 * 4 * NT, 128]) for c in range(4)] for j in range(2)]
    Vsn = [dint(f"Vsn{j}", [NS, 3 * 2048]) for j in range(2)]
    Utl = [dint(f"Utl{j}", [2048, 32], F32) for j in range(2)]
    Ua = [dint(f"Ua{j}", [4 * 2048, 32], F32) for j in range(2)]

    es = ExitStack()

    def sb(name, shape, dt=F32):
        return es.enter_context(nc.sbuf_tensor(name, list(shape), dt))

    with es:
        xT = sb("xT", [128, KC, NTOK])
        hT = sb("hT", [128, KC, NTOK], BF16)
        pv = sb("pv", [128, NPV])
        cf = sb("cf", [128, NCF])
        idx = sb("idx", [128, 112], I32)
        ones_bf = sb("ones_bf", [128, 128], BF16)
        id_bf = sb("id_bf", [128, 128], BF16)
        banks = [es.enter_context(nc.psum_tensor(f"bank{i}", [128, 512], F32)) for i in range(8)]
        BK = [f"B{i}" for i in range(8)]
        ident = cf[:, CF_ID:CF_ID + 128]

        def pvc(col, n=1):
            return pv[:, col:col + n]

        rot = {}

        def nxt(name, n):
            rot[name] = (rot.get(name, -1) + 1) % n
            return rot[name]

        def mm(out, lhsT, rhs, start, stop, r, w, inc, skip=False):
            P.op("pe", lambda e, o=out, l=lhsT, rr=rhs, s=start, t=stop, sk=skip:
                 e.matmul(o, l, rr, start=s, stop=t, skip_group_check=sk), r, w, inc)

        def tr(out, in_, idn, r, w, inc=True):
            P.op("pe", lambda e, o=out, i=in_, d=idn: e.transpose(o, i, d), r, w, inc)

        def act(out, in_, func, r, w, bias=None, scale=1.0):
            if bias is None:
                P.op("act", lambda e, o=out, i=in_, f=func, s=scale: e.activation(o, i, f, scale=s), r, w)
            else:
                P.op("act", lambda e, o=out, i=in_, f=func, s=scale, b=bias: e.activation(o, i, f, bias=b, scale=s), r, w)

        def cpy(eng, out, in_, r, w):
            if eng == "act":
                P.op("act", lambda e, o=out, i=in_: e.activation(o, i, AF.Copy), r, w)
            else:
                P.op(eng, lambda e, o=out, i=in_: e.tensor_copy(o, i), r, w)

        def tt(eng, out, a, b, op, r, w):
            P.op(eng, lambda e, o=out, x=a, y=b, p=op: e.tensor_tensor(o, x, y, p), r, w)

        def ts(eng, out, a, s1, op0, r, w, s2=None, op1=None):
            if op1 is None:
                P.op(eng, lambda e, o=out, x=a, s=s1, p=op0: e.tensor_scalar(o, x, s, None, p), r, w)
            else:
                P.op(eng, lambda e, o=out, x=a, s=s1, p=op0, t=s2, q=op1: e.tensor_scalar(o, x, s, t, p, q), r, w)

        def stt(eng, out, a, sc, b, op0, op1, r, w):
            P.op(eng, lambda e, o=out, x=a, s=sc, y=b, p=op0, q=op1: e.scalar_tensor_tensor(o, x, s, y, p, q), r, w)

        def recip(out, r, w):
            P.op("dve", lambda e, o=out: e.reciprocal(o, o), r, w)

        def mset(eng, out, val, r, w):
            P.op(eng, lambda e, o=out, v=val: e.memset(o, v), r, w)

        def dma(q, out, in_, r, w, sem):
            P.dma(q, lambda e, o=out, i=in_: e.dma_start(out=o, in_=i), r, w, sem)

        def gather(out, in_, idx_ap, r, w, sem):
            P.dma("pool", lambda e, o=out, i=in_, x=idx_ap: e.indirect_dma_start(
                out=o, out_offset=None, in_=i, in_offset=bass.IndirectOffsetOnAxis(ap=x, axis=0)), r, w, sem)

        def allgather(src, dst, r, w, sem):
            if os.environ.get("KNOAG"):
                return
            P.dma("pool", lambda e, s=src, d=dst: e.collective_compute(
                "AllGather", ALU.bypass, replica_groups=[[0, 1, 2, 3], [4, 5, 6, 7]], ins=[s], outs=[d]),
                r, w, sem, n=1)

        def load_w(dst, wsrc, r0, nk, c0, n, key, sem):
            src = wsrc[r0:r0 + nk * 128, c0:c0 + n].rearrange("(k p) c -> p k c", p=128)
            dma("pool", dst, src, [], key if isinstance(key, list) else [key], sem)

        dma("sp", pv[:], pvec_d, [], ["pv"], "pv")
        dma("sp", cf[:], cf_d, [], ["cf"], "cf")
        dma("sp", idx[:], idx_d, [], ["idx"], "idx")
        mset("dve", ones_bf[:], 1.0, [], ["ones"])
        cpy("dve", id_bf[:], ident, ["cf"], ["idbf"])

        def xkeys(kc, gi):
            if gi == 0:
                return [f"xT{kc}_{t}" for t in range(4)]
            if gi == 1:
                return [f"xT{kc}_{t}" for t in range(4, 8)]
            return [f"xT{kc}_8"]

        with nc.sbuf_tensor("xin", [128, 2, D], F32) as xin:
            for t8 in range(9):
                s = nxt("xin", 2)
                n = 128 if t8 < 8 else NS
                src = xp[t8 * 128:(t8 + 1) * 128, :] if t8 < 8 else xs
                dma("sp", xin[0:n, s, :], src, [], [f"xin{s}"], f"xin{s}")
                for k4 in range(4):
                    b = 4 + nxt("ldb", 4)
                    for kk in range(4):
                        kc = k4 * 4 + kk
                        tr(banks[b][:, kk * 128:kk * 128 + n], xin[0:n, s, kc * 128:(kc + 1) * 128], ident[0:n, 0:n],
                           [f"xin{s}", "cf"], [BK[b]], inc=(kk == 3))
                    dst = xT[:, k4 * 4:(k4 + 1) * 4, t8 * 128:t8 * 128 + n]
                    srcp = banks[b][:, :].rearrange("p (k t) -> p k t", k=4)[:, :, 0:n]
                    cpy("dve" if (k4 % 2 == 0) else "act", dst, srcp, [BK[b]],
                        [f"xT{kc_}_{t8}" for kc_ in range(k4 * 4, k4 * 4 + 4)])
            P.barrier()

        def rmsnorm(gcol):
            with Scope(nc, [("sqb", [128, 2, 512], BF16), ("rstd", [128, NTOK], F32)]) as (sqb, rstd,):
                for gi, (t0, n) in enumerate(TG):
                    b = 4 + nxt("ldb", 4)
                    for kc in range(KC):
                        s = nxt("sqb", 2)
                        act(sqb[:, s, 0:n], xT[:, kc, t0:t0 + n], AF.Square, xkeys(kc, gi), [f"sqb{s}"])
                        mm(banks[b][:, 0:n], ones_bf[:], sqb[:, s, 0:n], kc == 0, kc == KC - 1,
                           ["ones", f"sqb{s}"], [BK[b]], inc=True)
                    act(rstd[:, t0:t0 + n], banks[b][:, 0:n], AF.Sqrt, [BK[b], "pv"], [f"rstd{gi}"],
                        bias=pvc(PV_EPS_RMS), scale=1.0 / D)
                    recip(rstd[:, t0:t0 + n], [f"rstd{gi}"], [f"rstd{gi}"])
                    for kc in range(KC):
                        stt("dve", hT[:, kc, t0:t0 + n], xT[:, kc, t0:t0 + n],
                            pvc(gcol + kc), rstd[:, t0:t0 + n], ALU.mult, ALU.mult,
                            xkeys(kc, gi) + [f"rstd{gi}", "pv"], [f"hT{kc}_{gi}"])
                P.barrier()

        def pnorm_rstd(acc_ap, n, sq_ap, sqkey, ssb, rt_ap, rtkey, eps_col, scale):
            act(sq_ap, acc_ap, AF.Square, [sqkey[0]], [sqkey[1]])
            mm(banks[ssb][:, 0:n], ones_bf[:], sq_ap, True, True, ["ones", sqkey[1]], [BK[ssb]], inc=True)
            act(rt_ap, banks[ssb][:, 0:n], AF.Sqrt, [BK[ssb], "pv"], [rtkey], bias=pvc(eps_col), scale=scale)
            recip(rt_ap, [rtkey], [rtkey])

        def attn_layer(j):
            win = w_in_a[j]
            wout = w_out_a[j]
            rmsnorm(PV_AN + 16 * j)
            with Scope(nc, [("gT", [128, 8, NTOK], BF16), ("KTs", [128, 3, 16, NS], BF16)]) as (gT, KTs,):
                if STAGE < 2:
                    return
                with Scope(nc, [("wtk", [128, 2, 3, KC, 128], BF16), ("sq1", [128, 2, 512], BF16), ("rt1", [128, 2, 512], F32), ("Kn", [128, 2, 512], F32), ("KTb", [128, 2, NTOK], BF16), ("Kst", [128, 4, 128], F32)]) as (wtk, sq1, rt1, Kn, KTb, Kst,):
                    def ldk(h):
                        s = h % 2
                        keys = [f"wtk{s}_{g}" for g in range(3)]
                        for g in range(3):
                            load_w(wtk[:, s, g], win, 0, KC, g * 6144 + 2048 + h * 128, 128, keys if g == 0 else [], f"wtk{s}")
                        P.join(f"wtk{s}", keys)
                    ldk(0)
                    ldk(1)
                    for h in range(16):
                        s = h % 2
                        for g in range(3):
                            win_g, d = GROUPS[g]
                            kb = nxt("KTb", 2)
                            for gi, (t0, n) in enumerate(TG):
                                b = nxt("pb", 4)
                                for kc in range(KC):
                                    mm(banks[b][:, 0:n], wtk[:, s, g, kc, :], hT[:, kc, t0:t0 + n], kc == 0, kc == KC - 1,
                                       [f"wtk{s}_{g}", f"hT{kc}_{gi}"], [BK[b]], inc=(kc == KC - 1))
                                q2 = nxt("sq1", 2)
                                pnorm_rstd(banks[b][:, 0:n], n, sq1[:, q2, 0:n], (BK[b], f"sq1{q2}"), 4 + nxt("ssb", 2),
                                           rt1[:, q2, 0:n], f"rt1{q2}", PV_EPS_RMS, 1.0 / 128)
                                stt("dve", Kn[:, q2, 0:n], banks[b][:, 0:n], pvc(PV_KG + 3 * j + g), rt1[:, q2, 0:n],
                                    ALU.mult, ALU.mult, [BK[b], f"rt1{q2}", "pv"], [f"Kn{q2}"])
                                if gi < 2:
                                    dstv = KTb[:, kb, 0:NT].rearrange("p (r i) -> p r i", r=d)[:, :, t0 // d:(t0 + n) // d]
                                    srcv = Kn[:, q2, 0:n].rearrange("p (i r) -> p r i", r=d)
                                    cpy("pool", dstv, srcv, [f"Kn{q2}"], [f"KTb{kb}_{gi}"])
                                else:
                                    cpy("pool", KTb[:, kb, NT:NTOK], Kn[:, q2, 0:n], [f"Kn{q2}"], [f"KTb{kb}_2"])
                                    cpy("pool", KTs[:, g, h, :], Kn[:, q2, 0:n], [f"Kn{q2}"], [f"KTs{g}_{h}"])
                                if gi < 2:
                                    for t4 in range(4):
                                        tt8 = gi * 4 + t4
                                        row0 = tt8 * 128 - (NT - NTAIL[g])
                                        if row0 < 0:
                                            continue
                                        tb = 6 + nxt("tb", 2)
                                        tr(banks[tb][:, 0:128], Kn[:, q2, t4 * 128:(t4 + 1) * 128], ident, [f"Kn{q2}", "cf"], [BK[tb]])
                                        ks = nxt("Kst", 4)
                                        cpy("act", Kst[:, ks, :], banks[tb][:, 0:128], [BK[tb]], [f"Kst{ks}"])
                                        dma("sp", okv[g][j, row0:row0 + 128, h * 128:(h + 1) * 128], Kst[:, ks, :],
                                            [f"Kst{ks}"], [], f"Kst{ks}")
                                else:
                                    tb = 6 + nxt("tb", 2)
                                    tr(banks[tb][0:NS, 0:128], Kn[:, q2, 0:NS], ident, [f"Kn{q2}", "cf"], [BK[tb]])
                                    ks = nxt("Kst", 4)
                                    cpy("act", Kst[0:NS, ks, :], banks[tb][0:NS, 0:128], [BK[tb]], [f"Kst{ks}"])
                                    dma("sp", skv[g][j, win_g - NS:win_g, h * 128:(h + 1) * 128], Kst[0:NS, ks, :],
                                        [f"Kst{ks}"], [], f"Kst{ks}")
                            kr = [f"KTb{kb}_0", f"KTb{kb}_1"]
                            if g < 2:
                                dma("sp", KTown[j][g][h * 128:(h + 1) * 128, :], KTb[:, kb, 0:NT], kr, [f"KTown{g}_{h}"], f"KTb{kb}")
                            else:
                                dma("sp", KTown2[j][h // 4][(h % 4) * 128:(h % 4 + 1) * 128, :], KTb[:, kb, 0:NT], kr, [f"KTown{g}_{h}"], f"KTb{kb}")
                            if g == 0:
                                dma("sp", KTt0[j][h * 128:(h + 1) * 128, :], KTb[:, kb, NT - 128:NT], kr, [f"KTt0_{h}"], f"KTb{kb}")
                            if g == 1:
                                dma("sp", KTt1[j][h // 8][(h % 8) * 128:(h % 8 + 1) * 128, :].rearrange("p (r i) -> p r i", r=4),
                                    KTb[:, kb, 0:NT].rearrange("p (r i) -> p r i", r=4)[:, :, 128:256], kr, [f"KTt1_{h}"], f"KTb{kb}")
                        if h + 2 < 16:
                            ldk(h + 2)
                        if h % 4 == 3:
                            c = h // 4
                            allgather(KTown2[j][c], KTa2[j][c], [f"KTown2_{hh}" for hh in range(4 * c, 4 * c + 4)], [f"KTa2_{c}"], f"ag{j}")
                        if h % 8 == 7:
                            c = h // 8
                            allgather(KTt1[j][c], KTa1[j][c], [f"KTt1_{hh}" for hh in range(8 * c, 8 * c + 8)], [f"KTa1_{c}"], f"ag{j}")
                        if h == 15:
                            allgather(KTt0[j], KTa0[j], [f"KTt0_{hh}" for hh in range(16)], ["KTa0_0"], f"ag{j}")
                    P.barrier(skip=":ag")
                if STAGE < 3:
                    return
                with Scope(nc, [("wtv", [128, 2, KC, 512], BF16), ("Vsb", [128, 3, 512], BF16), ("Vsf", [128, 3, 512], F32)]) as (wtv, Vsb, Vsf,):
                    order = [(g, hq) for g in (2, 1, 0) for hq in range(4)]
                    KV = int(os.environ.get("KV", "0"))
                    realdma = dma

                    def dmaf(bit):
                        return (lambda *a, **k: None) if (KV >> bit) & 1 else realdma

                    def ldv(i):
                        g, hq = order[i]
                        load_w(wtv[:, i % 2], win, 0, KC, g * 6144 + 4096 + hq * 512, 512, f"wtv{i % 2}", f"wtv{i % 2}")
                    ldv(0)
                    ldv(1)
                    if (KV >> 5) & 1:
                        order = order[:2]
                    for i, (g, hq) in enumerate(order):
                        s = i % 2
                        win_g, d = GROUPS[g]
                        for t8 in range(8 if (KV >> 3) & 1 else 9):
                            n = 128 if t8 < 8 else NS
                            gi = 0 if t8 < 4 else (1 if t8 < 8 else 2)
                            b = nxt("pb", 4)
                            for kc in range(KC):
                                mm(banks[b][0:n, :], hT[:, kc, t8 * 128:t8 * 128 + n], wtv[:, s, kc, :], kc == 0, kc == KC - 1,
                                   [f"wtv{s}", f"hT{kc}_{gi}"], [BK[b]], inc=(kc == KC - 1))
                            vs = nxt("Vsb", 3)
                            if not (KV >> 6) & 1:
                                cpy("dve", Vsb[0:n, vs, :], banks[b][0:n, :], [BK[b]], [f"Vsb{vs}"])
                            row0 = t8 * 128 - (NT - NTAIL[g])
                            need_f = (t8 == 8) or row0 >= 0
                            if need_f and not (KV >> 4) & 1:
                                cpy("dve", Vsf[0:n, vs, :], banks[b][0:n, :], [BK[b]], [f"Vsf{vs}"])
                            if t8 < 8:
                                if g < 2:
                                    dst = Vown[j][g].rearrange("(h t) e -> t h e", h=16)[t8 * 128:(t8 + 1) * 128, hq * 4:hq * 4 + 4, :]
                                else:
                                    dst = Vown2[j][hq].rearrange("(h t) e -> t h e", h=4)[t8 * 128:(t8 + 1) * 128, :, :]
                                dmaf(0)("sp", dst, Vsb[:, vs, :].rearrange("p (h e) -> p h e", h=4), [f"Vsb{vs}"],
                                    [f"Vown{g}_{hq}_{t8}"], f"Vsb{vs}")
                                if g < 2 and row0 >= 0:
                                    if g == 0:
                                        dst = Vt0[j].rearrange("(h t) e -> t h e", h=16)[row0:row0 + 128, hq * 4:hq * 4 + 4, :]
                                    else:
                                        dst = Vt1[j][hq // 2].rearrange("(h t) e -> t h e", h=8)[row0:row0 + 128, (hq % 2) * 4:(hq % 2) * 4 + 4, :]
                                    dmaf(0)("sp", dst, Vsb[:, vs, :].rearrange("p (h e) -> p h e", h=4), [f"Vsb{vs}"],
                                        [f"Vt{g}_{hq}_{t8}"], f"Vsb{vs}")
                                if row0 >= 0:
                                    dmaf(1)("sp", okv[g][j, row0:row0 + 128, D + hq * 512:D + (hq + 1) * 512], Vsf[:, vs, :],
                                        [f"Vsf{vs}"], [], f"Vsf{vs}")
                            else:
                                dmaf(2)("sp", Vsn[j][:, g * 2048 + hq * 512:g * 2048 + (hq + 1) * 512], Vsb[0:NS, vs, :],
                                    [f"Vsb{vs}"], [f"Vsn{g}_{hq}"], f"Vsb{vs}")
                                dmaf(1)("sp", skv[g][j, win_g - NS:win_g, D + hq * 512:D + (hq + 1) * 512], Vsf[0:NS, vs, :],
                                    [f"Vsf{vs}"], [], f"Vsf{vs}")
                        if i + 2 < len(order):
                            ldv(i + 2)
                        if g == 2:
                            allgather(Vown2[j][hq], Va2[j][hq], [f"Vown2_{hq}_{t}" for t in range(8)], [f"Va2_{hq}"], f"ag{j}")
                        elif g == 1 and hq % 2 == 1:
                            c = hq // 2
                            allgather(Vt1[j][c], Va1[j][c], [f"Vt1_{q}_{t}" for q in (2 * c, 2 * c + 1) for t in range(8)], [f"Va1_{c}"], f"ag{j}")
                        elif g == 0 and hq == 3:
                            allgather(Vt0[j], Va0[j], [f"Vt0_{q}_{t}" for q in range(4) for t in range(8)], ["Va0_0"], f"ag{j}")
                    for g in range(3):
                        win_g = GROUPS[g][0]
                        for r0 in range(0, (win_g - NS) if not os.environ.get("KNOSHIFT") else 0, 128):
                            r1 = min(r0 + 128, win_g - NS)
                            dma("pool", skv[g][j, r0:r1, :], ck[g][j, NS + r0:NS + r1, :], [], [], "cshift")
                    P.barrier()
                if STAGE < 4:
                    return
                with Scope(nc, [("wq", [128, 4, KC, 128], BF16), ("gst", [128, 2048], BF16), ("QT", [128, 3, NT], BF16), ("QTs", [128, 3, NS], BF16), ("sz", [128, NTOK], F32), ("KTx", [128, 5760], BF16), ("Vx", [128, 53, 128], BF16), ("PT", [128, 3, 256], BF16), ("Sb", [128, 3, 256], F32), ("Ksc", [128, 9, 128], F32), ("KsT", [128, 9, 128], BF16), ("Vsc", [128, 9, 128], BF16), ("Vnh", [NS, 3, 128], BF16), ("og", [128, NTOK], F32), ("wo", [128, 2, 8, 128], BF16), ("sq2", [128, 2, 512], BF16), ("rt2", [128, 2, 512], F32)]) as (wq, gst, QT, QTs, sz, KTx, Vx, PT, Sb, Ksc, KsT, Vsc, Vnh, og, wo, sq2, rt2,):
                    KOFF = [0, 1152, 1152 + 1536]
                    VOFF = [0, 9, 21]
                    NCH = [9, 3, 2]

                    def ktx(g, r):
                        d = GROUPS[g][1]
                        ext = 128 + NT // d
                        return KTx[:, KOFF[g] + r * ext:KOFF[g] + (r + 1) * ext]

                    qcols = [(g * 6144 + 0) for g in range(3)] + [NQKV]

                    def ldq(hh, c):
                        k = hh * 4 + c
                        s = k % 4
                        load_w(wq[:, s], win, 0, KC, qcols[c] + hh * 128, 128, f"wq{s}", f"wq{s}")
                    for k in range(4):
                        ldq(k // 4, k % 4)

                    pend = []
                    LAG = 2

                    def unit_b(o_ap, vap, l_ap, nk, nq, s, rk, okey, lkey):
                        mm(o_ap, vap, PT[0:nk, s, 0:nq], False, False, rk[1] + [f"PT{s}"], [okey], inc=False, skip=True)
                        mm(l_ap, ones_bf[0:nk, :], PT[0:nk, s, 0:nq], False, False, ["ones", f"PT{s}"], [lkey], inc=True, skip=True)

                    def flush(keep=0):
                        while len(pend) > keep:
                            unit_b(*pend.pop(0))

                    def unit(ktap, qtap, vap, dap, sd, o_ap, l_ap, nk, nq, rk, okey, lkey, bias=None):
                        b = 4 + nxt("sbk", 2)
                        mm(banks[b][0:nk, 0:nq], ktap, qtap, True, True, rk[0], [BK[b]], inc=True)
                        s = nxt("PT", 3)
                        stt("dve", Sb[0:nk, s, 0:nq], dap, -sd, banks[b][0:nk, 0:nq], ALU.mult, ALU.add,
                            [BK[b], "cf"], [f"Sb{s}"])
                        act(PT[0:nk, s, 0:nq], Sb[0:nk, s, 0:nq], AF.Exp, [f"Sb{s}", "pv"], [f"PT{s}"],
                            bias=(bias if bias is not None else pvc(PV_ZERO)[0:nk, :]))
                        pend.append((o_ap, vap, l_ap, nk, nq, s, rk, okey, lkey))
                        flush(keep=LAG)

                    for h in range(16):
                        kxk = [f"KTx{g}" for g in range(3)]
                        vxk = [f"Vx{g}" for g in range(3)]
                        for g in range(3):
                            win_g, d = GROUPS[g]
                            ext = 128 + NT // d
                            dstk = KTx[:, KOFF[g]:KOFF[g] + d * ext].rearrange("p (r x) -> p r x", r=d)
                            ksrc = KTown[j][g][h * 128:(h + 1) * 128, :] if g < 2 else KTown2[j][h // 4][(h % 4) * 128:(h % 4 + 1) * 128, :]
                            dma("sp", dstk[:, :, 128:ext], ksrc.rearrange("p (r i) -> p r i", r=d),
                                [f"KTown{g}_{h}"], [kxk[g]], f"KTx{g}")
                            vv = Vx[:, VOFF[g]:VOFF[g] + d * NCH[g], :].rearrange("p (r c) e -> p r c e", r=d)
                            vsrc = Vown[j][g][h * NT:(h + 1) * NT, :] if g < 2 else Vown2[j][h // 4][(h % 4) * NT:(h % 4 + 1) * NT, :]
                            vr = [f"Vown{g}_{h // 4}_{t}" for t in range(8)]
                            if g == 0:
                                dma("sp", vv[:, 0, 1:9, :], vsrc.rearrange("(c p) e -> p c e", p=128), vr, [vxk[g]], f"Vx{g}")
                            elif g == 1:
                                for r in range(4):
                                    dma("sp", vv[:, r, 1:3, :], vsrc.rearrange("(c p r) e -> p r c e", p=128, r=4)[:, r],
                                        vr, [vxk[g]], f"Vx{g}")
                            else:
                                dma("sp", vv[0:64, :, 1, :], vsrc.rearrange("(p r) e -> p r e", r=16), vr, [vxk[g]], f"Vx{g}")
                        ia = idx[:, h:h + 1]
                        ik1 = idx[:, 16 + h:17 + h]
                        ik2a = idx[:, 32 + h:33 + h]
                        ik2b = idx[:, 48 + h:49 + h]
                        iv0 = idx[:, 64 + h:65 + h]
                        iv1 = idx[:, 80 + h:81 + h]
                        iv2 = idx[:, 96 + h:97 + h]
                        gather(KTx[:, 0:128], KTa0[j], ia, ["KTa0_0", "idx"], [kxk[0]], "KTx0")
                        d1 = KTx[:, KOFF[1]:KOFF[1] + 4 * 384].rearrange("p (r x) -> p r x", r=4)
                        gather(gst[:, 0:512], KTa1[j][h // 8], ik1, [f"KTa1_{h // 8}", "idx"], ["gst"], "gst")
                        cpy("pool", d1[:, :, 0:128], gst[:, 0:512].rearrange("p (r i) -> p r i", r=4), ["gst"], [kxk[1]])
                        d2 = KTx[:, KOFF[2]:KOFF[2] + 16 * 192].rearrange("p (r x) -> p r x", r=16)
                        gather(gst[:, 0:1024], KTa2[j][h // 4], ik2a, [f"KTa2_{h // 4}", "idx"], ["gst"], "gst")
                        cpy("pool", d2[:, :, 64:128], gst[:, 0:1024].rearrange("p (r i) -> p r i", r=16), ["gst"], [kxk[2]])
                        gather(gst[:, 0:1024], KTa2[j][h // 4], ik2b, [f"KTa2_{h // 4}", "idx"], ["gst"], "gst")
                        cpy("pool", d2[:, :, 0:64], gst[:, 0:1024].rearrange("p (r i) -> p r i", r=16), ["gst"], [kxk[2]])
                        gather(Vx[:, 0, :], Va0[j], iv0, ["Va0_0", "idx"], [vxk[0]], "Vx0")
                        v1 = Vx[:, VOFF[1]:VOFF[1] + 12, :].rearrange("p (r c) e -> p r c e", r=4)
                        gather(gst[:, 0:512], Va1[j][h // 8].rearrange("(n r) e -> n (r e)", r=4), iv1, [f"Va1_{h // 8}", "idx"], ["gst"], "gst")
                        cpy("pool", v1[:, :, 0, :], gst[:, 0:512].rearrange("p (r e) -> p r e", r=4), ["gst"], [vxk[1]])
                        v2 = Vx[:, VOFF[2]:VOFF[2] + 32, :].rearrange("p (r c) e -> p r c e", r=16)
                        gather(gst[:, 0:2048], Va2[j][h // 4].rearrange("(n r) e -> n (r e)", r=16), iv2, [f"Va2_{h // 4}", "idx"], ["gst"], "gst")
                        cpy("pool", v2[:, :, 0, :], gst[:, 0:2048].rearrange("p (r e) -> p r e", r=16), ["gst"], [vxk[2]])
                        ci = 0
                        for g in range(3):
                            win_g, d = GROUPS[g]
                            nres = 1 if g == 0 else NS
                            for r in range(nres):
                                rows = ck[g][j, r:win_g:d, :] if g > 0 else ck[g][j, 0:128, :]
                                dma("sp", Ksc[:, ci, :], rows[:, h * 128:(h + 1) * 128], [], [f"Ksc{ci}"], f"Ksc{ci}")
                                dma("pool", Vsc[:, ci, :], rows[:, D + h * 128:D + (h + 1) * 128], [], [f"Vsc{ci}"], f"Vsc{ci}")
                                ci += 1
                        dma("sp", Vnh[:, :, :], Vsn[j].rearrange("t (g c) -> t g c", g=3)[:, :, h * 128:(h + 1) * 128],
                            [f"Vsn{g}_{h // 4}" for g in range(3)], ["Vnh"], "Vnh")
                        for c in range(4):
                            k = h * 4 + c
                            s = k % 4
                            for gi, (t0, n) in enumerate(TG):
                                b = 6 + nxt("qb", 2)
                                for kc in range(KC):
                                    mm(banks[b][:, 0:n], wq[:, s, kc, :], hT[:, kc, t0:t0 + n], kc == 0, kc == KC - 1,
                                       [f"wq{s}", f"hT{kc}_{gi}"], [BK[b]], inc=(kc == KC - 1))
                                if c == 3:
                                    act(sz[:, t0:t0 + n], banks[b][:, 0:n], AF.Silu, [BK[b]], [f"sz{gi}"])
                                    continue
                                g = c
                                d = GROUPS[g][1]
                                q2 = nxt("sq2", 2)
                                pnorm_rstd(banks[b][:, 0:n], n, sq2[:, q2, 0:n], (BK[b], f"sq2{q2}"), 4 + nxt("sbk", 2),
                                           rt2[:, q2, 0:n], f"rt2{q2}", PV_EPS_Q, 1.0)
                                if gi < 2:
                                    dstv = QT[:, g, :].rearrange("p (r i) -> p r i", r=d)[:, :, t0 // d:(t0 + n) // d]
                                    a_v = banks[b][:, 0:n].rearrange("p (i r) -> p r i", r=d)
                                    r_v = rt2[:, q2, 0:n].rearrange("p (i r) -> p r i", r=d)
                                    stt("dve", dstv, a_v, pvc(PV_QG + 3 * j + g), r_v, ALU.mult, ALU.mult,
                                        [BK[b], f"rt2{q2}", "pv"], [f"QT{g}_{gi}"])
                                else:
                                    stt("dve", QTs[:, g, :], banks[b][:, 0:n], pvc(PV_QG + 3 * j + g), rt2[:, q2, 0:n],
                                        ALU.mult, ALU.mult, [BK[b], f"rt2{q2}", "pv"], [f"QTs{g}"])
                            if k + 4 < 64:
                                ldq((k + 4) // 4, (k + 4) % 4)
                        for ci in range(9):
                            tb = 6 + nxt("qb", 2)
                            tr(banks[tb][:, 0:128], Ksc[:, ci, :], ident, [f"Ksc{ci}", "cf"], [BK[tb]])
                            cpy("dve", KsT[:, ci, :], banks[tb][:, 0:128], [BK[tb]], [f"KsT{ci}"])
                        for b in range(4):
                            mset("dve", banks[b][:, :], 0.0, [], [BK[b]])
                        mset("dve", banks[7][:, 0:16], 0.0, [], [BK[7]])
                        for g in range(3):
                            win_g, d = GROUPS[g]
                            nown = NT // d
                            sd = alibi_sd(g, h)
                            for r in range(d):
                                kt = ktx(g, r)
                                for c in range(NCH[g]):
                                    nk = min(128, 128 + nown - c * 128)
                                    ilo = max(0, 128 * (c - 1))
                                    ihi = min(nown, 128 * (c - 1) + 256)
                                    vap = Vx[0:nk, VOFF[g] + r * NCH[g] + c, :]
                                    bias = pvc(PV_LBM + g) if c == 0 else None
                                    half_n = 512 // d
                                    for hf in range(2):
                                        a = max(ilo, hf * half_n)
                                        e_ = min(ihi, (hf + 1) * half_n)
                                        if a >= e_:
                                            continue
                                        nq = e_ - a
                                        j0 = a - 128 * (c - 1)
                                        col0 = (a - hf * half_n) * d + r
                                        cols = slice(col0, col0 + (nq - 1) * d + 1, d)
                                        unit(kt[:, c * 128:c * 128 + nk], QT[:, g, r * nown + a:r * nown + e_], vap,
                                             cf[0:nk, CF_D + j0:CF_D + j0 + nq], sd,
                                             banks[hf][:, cols], banks[2 + hf][:, cols], nk, nq,
                                             ([f"KTx{g}", f"QT{g}_{hf}"], [f"Vx{g}"]), BK[hf], BK[2 + hf], bias=bias)
                        ci = 0
                        for g in range(3):
                            win_g, d = GROUPS[g]
                            sd = alibi_sd(g, h)
                            nres = 1 if g == 0 else NS
                            for r in range(nres):
                                q0, nq = (0, NS) if g == 0 else (r, 1)
                                unit(KsT[:, ci, :], QTs[:, g, q0:q0 + nq], Vsc[:, ci, :],
                                     cf[:, CF_D + 128 + (0 if g == 0 else 0):CF_D + 128 + nq], sd,
                                     banks[7][:, q0:q0 + nq], banks[7][:, 8 + q0:8 + q0 + nq], 128, nq,
                                     ([f"KsT{ci}", f"QTs{g}"], [f"Vsc{ci}"]), BK[7], BK[7])
                                ci += 1
                            dd = cf[0:NS, CF_D:CF_D + NS] if g == 0 else cf[0:NS, CF_DD:CF_DD + NS]
                            unit(KTs[:, g, h, :], QTs[:, g, :], Vnh[:, g, :], dd, sd,
                                 banks[7][:, 0:NS], banks[7][:, 8:8 + NS], NS, NS,
                                 ([f"KTs{g}_{h}", f"QTs{g}"], ["Vnh"]), BK[7], BK[7])
                        flush()
                        for hf in range(2):
                            cpy("act", og[:, hf * 512:(hf + 1) * 512], banks[2 + hf][:, :], [BK[2 + hf]], [f"og{hf}"])
                            recip(og[:, hf * 512:(hf + 1) * 512], [f"og{hf}"], [f"og{hf}"])
                            tt("dve", og[:, hf * 512:(hf + 1) * 512], og[:, hf * 512:(hf + 1) * 512], banks[hf][:, :], ALU.mult,
                               [f"og{hf}", BK[hf]], [f"og{hf}"])
                            tt("pool", gT[:, h % 8, hf * 512:(hf + 1) * 512], og[:, hf * 512:(hf + 1) * 512],
                               sz[:, hf * 512:(hf + 1) * 512], ALU.mult, [f"og{hf}", f"sz{hf}"], [f"gT{h % 8}_{hf}"])
                        cpy("act", og[:, NT:NTOK], banks[7][:, 8:8 + NS], [BK[7]], ["og2"])
                        recip(og[:, NT:NTOK], ["og2"], ["og2"])
                        tt("dve", og[:, NT:NTOK], og[:, NT:NTOK], banks[7][:, 0:NS], ALU.mult, ["og2", BK[7]], ["og2"])
                        tt("pool", gT[:, h % 8, NT:NTOK], og[:, NT:NTOK], sz[:, NT:NTOK], ALU.mult, ["og2", "sz2"], [f"gT{h % 8}_2"])
                        if h % 8 == 7:
                            hb = h - 7
                            for dc in range(16):
                                s = nxt("wo", 2)
                                load_w(wo[:, s], wout, hb * 128, 8, dc * 128, 128, f"wo{s}", f"wo{s}")
                                for gi, (t0, n) in enumerate(TG):
                                    b = 6 + nxt("qb", 2)
                                    for hh in range(8):
                                        mm(banks[b][:, 0:n], wo[:, s, hh, :], gT[:, hh, t0:t0 + n], hh == 0, hh == 7,
                                           [f"wo{s}", f"gT{hh}_{gi}"], [BK[b]], inc=(hh == 7))
                                    tt("dve", xT[:, dc, t0:t0 + n], xT[:, dc, t0:t0 + n], banks[b][:, 0:n], ALU.add,
                                       xkeys(dc, gi) + [BK[b]], xkeys(dc, gi))
                    P.barrier()

        def conv_layer(j):
            win = w_in_c[j]
            wout = w_out_c[j]
            rmsnorm(PV_CN + 16 * j)
            dwc = PV_DWW + j * 16 * CW
            with Scope(nc, [("cT", [128, KC, NTOK], BF16), ("szc", [128, KC, NTOK], BF16), ("Us", [128, KC, 34], F32), ("Uh", [128, KC, 64], F32)]) as (cT, szc, Us, Uh,):
                with Scope(nc, [("wc", [128, 3, KC, 128], BF16), ("Ub", [128, NT], F32), ("sg", [128, 2, 512], F32), ("ac", [128, 2, NT], F32), ("prs", [128, NS * CW], F32), ("cs", [128, 32], F32)]) as (wc, Ub, sg, ac, prs, cs,):
                    st = bass.AP(ac, 0, [list(ac[0:32, 0, 0:1].ap[0]), [1, D]])

                    def ldc(k):
                        cc, a = k // 3, k % 3
                        load_w(wc[:, a], win, 0, KC, a * D + cc * 128, 128, f"wc{a}", f"wc{a}")
                    for k in range(3):
                        ldc(k)
                    dma("sp", st[0:30, :], sconv[j], [], ["st"], "st")
                    for cc in range(KC):
                        tb = 6 + nxt("qb", 2)
                        tr(banks[tb][:, 0:30], st[0:30, cc * 128:(cc + 1) * 128], ident[0:30, 0:30], ["st", "cf"], [BK[tb]])
                        cpy("act", Us[:, cc, 0:30], banks[tb][:, 0:30], [BK[tb]], [f"Us{cc}"])
                    dma("pool", sconv_o[j, 0:26, :], sconv[j, 4:30, :], [], [], "cshift")
                    P.barrier()
                    for cc in range(KC):
                        for a in range(3):
                            for gi, (t0, n) in enumerate(TG):
                                b = nxt("pb", 6)
                                for kc in range(KC):
                                    mm(banks[b][:, 0:n], wc[:, a, kc, :], hT[:, kc, t0:t0 + n], kc == 0, kc == KC - 1,
                                       [f"wc{a}", f"hT{kc}_{gi}"], [BK[b]], inc=(kc == KC - 1))
                                udst = Ub[:, t0:t0 + n] if gi < 2 else Us[:, cc, 30:34]
                                ukey = f"Ub_{gi}" if gi < 2 else f"Us{cc}"
                                if a == 0:
                                    cpy("act", udst, banks[b][:, 0:n], [BK[b]], [ukey])
                                elif a == 1:
                                    q2 = nxt("sg", 2)
                                    act(sg[:, q2, 0:n], banks[b][:, 0:n], AF.Sigmoid, [BK[b]], [f"sg{q2}"])
                                    tt("dve", udst, udst, sg[:, q2, 0:n], ALU.mult, [f"sg{q2}", ukey], [ukey])
                                else:
                                    act(szc[:, cc, t0:t0 + n], banks[b][:, 0:n], AF.Silu, [BK[b]], [f"szc{cc}_{gi}"])
                            if cc + 1 < KC:
                                ldc((cc + 1) * 3 + a)
                        uk = ["Ub_0", "Ub_1"]
                        cpy("pool", Uh[:, cc, 32:64], Ub[:, 0:32], uk, [f"Uh{cc}"])
                        dma("sp", Utl[j][cc * 128:(cc + 1) * 128, :], Ub[:, NT - 32:NT], uk, [f"Utl{cc}"], "Ub")
                        L = NT - 30
                        a0 = ac[:, 0, 0:L]
                        a1 = ac[:, 1, 0:L]
                        ts("dve", a0, Ub[:, 0:L], pvc(dwc + cc * CW + 0), ALU.mult, uk + ["pv"], ["ac0"],
                           s2=pvc(PV_DWB + 16 * j + cc), op1=ALU.add)
                        for tap in range(1, CW):
                            stt("dve", a0, Ub[:, tap:tap + L], pvc(dwc + cc * CW + tap), a0, ALU.mult, ALU.add, uk + ["ac0", "pv"], ["ac0"])
                        cpy("pool", cT[:, cc, 30:NT], a0, ["ac0"], [f"cT{cc}_m"])
                        base = Us[:, cc, 0:1]
                        win_ap = bass.AP(Us, base.offset, [list(base.ap[0]), [1, NS], [1, CW]])
                        wcol = pvc(dwc + cc * CW, CW)
                        w_ap = bass.AP(pv, wcol.offset, [list(wcol.ap[0]), [0, NS], [1, CW]])
                        pr = prs[:, :].rearrange("p (t k) -> p t k", k=CW)
                        tt("dve", pr, win_ap, w_ap, ALU.mult, [f"Us{cc}", "pv"], ["prs"])
                        P.op("dve", lambda e, o=cs[:, 0:NS], i=pr: e.tensor_reduce(o, i, AX.X, ALU.add), ["prs"], ["cs"])
                        ts("dve", cT[:, cc, NT:NTOK], cs[:, 0:NS], pvc(PV_DWB + 16 * j + cc), ALU.add, ["cs", "pv"], [f"cT{cc}_s"])
                    allgather(Utl[j], Ua[j], [f"Utl{cc}" for cc in range(KC)], ["Ua"], f"agu{j}")
                    for cc in range(KC):
                        gather(Uh[:, cc, 0:32], Ua[j], idx[:, cc:cc + 1], ["Ua", "idx"], [f"Uh{cc}"] if cc else [f"Uh{c_}" for c_ in range(KC)], "Uh")
                    P.join("Uh", [f"Uh{cc}" for cc in range(KC)])
                    prb = bass.AP(sg, 0, [list(sg[:, 0, 0:1].ap[0]), [CW, 30], [1, CW]])
                    for cc in range(KC):
                        ts("dve", Uh[:, cc, 0:32], Uh[:, cc, 0:32], pvc(PV_HALO), ALU.mult, [f"Uh{cc}", "pv"], [f"Uh{cc}"])
                        base = Uh[:, cc, 2:3]
                        win_ap = bass.AP(Uh, base.offset, [list(base.ap[0]), [1, 30], [1, CW]])
                        wcol = pvc(dwc + cc * CW, CW)
                        w_ap = bass.AP(pv, wcol.offset, [list(wcol.ap[0]), [0, 30], [1, CW]])
                        tt("dve", prb, win_ap, w_ap, ALU.mult, [f"Uh{cc}", "pv", "sg0", "sg1"], ["sg0", "sg1"])
                        P.op("dve", lambda e, o=cs[:, 0:30], i=prb: e.tensor_reduce(o, i, AX.X, ALU.add), ["sg0", "sg1"], ["cs"])
                        ts("dve", cT[:, cc, 0:30], cs[:, 0:30], pvc(PV_DWB + 16 * j + cc), ALU.add, ["cs", "pv"], [f"cT{cc}_h"])
                    P.barrier()
                    dma("sp", Uh[:, :, 0:32], Utl[j].rearrange("(c p) t -> p c t", p=128), [], [f"Uh{cc}" for cc in range(KC)], "Uh")
                    for cc in range(KC):
                        tb = 6 + nxt("qb", 2)
                        tr(banks[tb][0:32, 0:128], Uh[:, cc, 0:32], ident, [f"Uh{cc}", "cf"], [BK[tb]])
                        cpy("act", st[0:32, cc * 128:(cc + 1) * 128], banks[tb][0:32, 0:128], [BK[tb]], ["st"])
                    dma("sp", oconv[j], st[2:32, :], ["st"], [], "st")
                    for cc in range(KC):
                        tb = 6 + nxt("qb", 2)
                        tr(banks[tb][0:NS, 0:128], Us[:, cc, 30:34], ident, [f"Us{cc}", "cf"], [BK[tb]])
                        cpy("act", st[0:NS, cc * 128:(cc + 1) * 128], banks[tb][0:NS, 0:128], [BK[tb]], ["st"])
                    dma("sp", sconv_o[j, 26:30, :], st[0:NS, :], ["st"], [], "st")
                    P.barrier()
                with Scope(nc, [("Sq16", [128, 2, 512], BF16), ("sg2", [128, 2, 512], F32), ("mean", [128, NTOK], F32), ("rs", [128, NTOK], F32), ("wo2", [128, 2, KC, 128], BF16)]) as (Sq16, sg2, mean, rs, wo2,):
                    def ckeys(cc, gi):
                        return [f"cT{cc}_m", f"cT{cc}_h"] if gi == 0 else ([f"cT{cc}_m"] if gi == 1 else [f"cT{cc}_s"])
                    for gi, (t0, n) in enumerate(TG):
                        b1 = 0 + gi % 2
                        b2 = 2 + gi % 2
                        for cc in range(KC):
                            mm(banks[b1][:, 0:n], ones_bf[:], cT[:, cc, t0:t0 + n], cc == 0, cc == KC - 1,
                               ["ones"] + ckeys(cc, gi), [BK[b1]], inc=True)
                            q2 = nxt("sqc", 2)
                            act(Sq16[:, q2, 0:n], cT[:, cc, t0:t0 + n], AF.Square, ckeys(cc, gi), [f"sq16{q2}"])
                            mm(banks[b2][:, 0:n], ones_bf[:], Sq16[:, q2, 0:n], cc == 0, cc == KC - 1,
                               ["ones", f"sq16{q2}"], [BK[b2]], inc=True)
                        mk = f"mean{gi}"
                        rk = f"rs{gi}"
                        m_ap = mean[:, t0:t0 + n]
                        r_ap = rs[:, t0:t0 + n]
                        ts("dve", m_ap, banks[b1][:, 0:n], 1.0 / D, ALU.mult, [BK[b1]], [mk])
                        tt("dve", r_ap, m_ap, m_ap, ALU.mult, [mk], [rk])
                        stt("dve", r_ap, banks[b2][:, 0:n], 1.0 / D, r_ap, ALU.mult, ALU.subtract, [BK[b2], rk], [rk])
                        act(r_ap, r_ap, AF.Sqrt, [rk, "pv"], [rk], bias=pvc(PV_EPS_LN), scale=1.0)
                        recip(r_ap, [rk], [rk])
                        for cc in range(KC):
                            q2 = nxt("sg2", 2)
                            tmp = sg2[:, q2, 0:n]
                            tt("dve", tmp, cT[:, cc, t0:t0 + n], m_ap, ALU.subtract, ckeys(cc, gi) + [mk], [f"sg2{q2}"])
                            tt("pool", tmp, tmp, r_ap, ALU.mult, [f"sg2{q2}", rk], [f"sg2{q2}"])
                            act(tmp, tmp, AF.Silu, [f"sg2{q2}", "pv"], [f"sg2{q2}"], bias=pvc(PV_LNB + 16 * j + cc),
                                scale=pvc(PV_LNG + 16 * j + cc))
                            tt("dve", hT[:, cc, t0:t0 + n], tmp, szc[:, cc, t0:t0 + n], ALU.mult,
                               [f"sg2{q2}", f"szc{cc}_{gi}"], [f"hT{cc}_{gi}"])
                    for dc in range(16):
                        s = dc % 2
                        load_w(wo2[:, s], wout, 0, KC, dc * 128, 128, f"wo2{s}", f"wo2{s}")
                        for gi, (t0, n) in enumerate(TG):
                            b = 4 + nxt("ldb", 4)
                            for cc in range(KC):
                                mm(banks[b][:, 0:n], wo2[:, s, cc, :], hT[:, cc, t0:t0 + n], cc == 0, cc == KC - 1,
                                   [f"wo2{s}", f"hT{cc}_{gi}"], [BK[b]], inc=(cc == KC - 1))
                            tt("dve", xT[:, dc, t0:t0 + n], xT[:, dc, t0:t0 + n], banks[b][:, 0:n], ALU.add,
                               xkeys(dc, gi) + [BK[b]], xkeys(dc, gi))
                    P.barrier()


        if STAGE >= 1:
            attn_layer(0)
        if STAGE >= 5:
            conv_layer(0)
        if STAGE >= 6:
            attn_layer(1)
        if STAGE >= 7:
            conv_layer(1)

        with nc.sbuf_tensor("xo", [128, 2, D], F32) as xo:
            for t8 in range(9):
                s = nxt("xo", 2)
                n = 128 if t8 < 8 else NS
                gi = 0 if t8 < 4 else (1 if t8 < 8 else 2)
                for k4 in range(4):
                    b = 4 + nxt("ldb", 4)
                    for kk in range(4):
                        kc = k4 * 4 + kk
                        tr(banks[b][0:n, kk * 128:(kk + 1) * 128], xT[:, kc, t8 * 128:t8 * 128 + n], ident,
                           xkeys(kc, gi) + ["cf"], [BK[b]], inc=(kk == 3))
                    cpy("dve" if k4 % 2 == 0 else "act", xo[0:n, s, k4 * 512:(k4 + 1) * 512], banks[b][0:n, :], [BK[b]], [f"xo{s}_{k4}"])
                dst = yp[t8 * 128:(t8 + 1) * 128, :] if t8 < 8 else ys
                dma("sp", dst, xo[0:n, s, :], [f"xo{s}_{k}" for k in range(4)], [], f"xo{s}")
        P.final_waits("sp")

        sems = {}

        def sem_of(k):
            if k not in sems:
                sems[k] = es.enter_context(nc.semaphore(k.replace(":", "_")))
            return sems[k]
        for e in Prog.ENGS:
            sem_of("E:" + e)
        for k in P.dcnt:
            sem_of(k)

        with nc.Block() as block:
            def run(engname):
                def f(eng):
                    for waits, fn, inc in P.q[engname]:
                        for k, v in waits:
                            eng.wait_ge(sems[k], v)
                        if fn is not None:
                            ins = fn(eng)
                            if inc is not None:
                                ins.then_inc(sems[inc[0]], inc[1])
                return f
            block.tensor(run("pe"))
            block.scalar(run("act"))
            block.vector(run("dve"))
            block.gpsimd(run("pool"))
            block.sync(run("sp"))
    return nc


_NC_CACHE = {}


def _host_tables(c):
    pos = c % 4
    r1 = max(pos - 1, 0)
    r2 = max(pos - 2, 0)
    p = np.arange(128)
    idx = np.zeros((128, 112), np.int32)
    for h in range(16):
        idx[:, h] = r1 * 2048 + h * 128 + p
        idx[:, 16 + h] = r1 * 1024 + (h % 8) * 128 + p
        idx[:, 32 + h] = r1 * 512 + (h % 4) * 128 + p
        idx[:, 48 + h] = r2 * 512 + (h % 4) * 128 + p
        idx[:, 64 + h] = (r1 * 16 + h) * 128 + p
        idx[:, 80 + h] = (r1 * 8 + h % 8) * 128 + p
        idx[:, 96 + h] = np.where(p < 64, (r2 * 4 + h % 4) * 64 + p, (r1 * 4 + h % 4) * 64 + (p - 64))
    return idx


def _const_table():
    cf = np.zeros((128, NCF), np.float32)
    k = np.arange(128)[:, None]
    jj = np.arange(256)[None, :]
    dist = (jj - k).astype(np.float32)
    cf[:, CF_D:CF_D + 256] = np.where((dist >= 0) & (dist <= 128), dist, BIG)
    cf[:, CF_DD:CF_DD + 4] = np.where(np.arange(4)[None, :] == k, 0.0, BIG)
    cf[:, CF_ID:CF_ID + 128] = np.eye(128, dtype=np.float32)
    return cf


def _pvec(c, attn_norm, conv_norm, q_gain, k_gain, dw_w, dw_b, ln_g, ln_b):
    pos = c % 4
    pv = np.zeros((128, NPV), np.float32)

    def fm(v):
        return np.ascontiguousarray(v.reshape(16, 128).T)
    for l in range(2):
        pv[:, PV_AN + 16 * l:PV_AN + 16 * l + 16] = fm(attn_norm[l])
        pv[:, PV_CN + 16 * l:PV_CN + 16 * l + 16] = fm(conv_norm[l])
        for g in range(3):
            pv[:, PV_QG + 3 * l + g] = q_gain[l, g]
            pv[:, PV_KG + 3 * l + g] = k_gain[l, g]
        pv[:, PV_DWW + l * 16 * CW:PV_DWW + (l + 1) * 16 * CW] = dw_w[l].T.reshape(16, 128, CW).transpose(1, 0, 2).reshape(128, 16 * CW)
        pv[:, PV_DWB + 16 * l:PV_DWB + 16 * l + 16] = fm(dw_b[l])
        pv[:, PV_LNG + 16 * l:PV_LNG + 16 * l + 16] = fm(ln_g[l])
        pv[:, PV_LNB + 16 * l:PV_LNB + 16 * l + 16] = fm(ln_b[l])
    pv[:, PV_EPS_RMS] = 1e-6
    pv[:, PV_EPS_Q] = 128 * 1e-6
    pv[:, PV_EPS_LN] = 1e-5
    if pos == 0:
        pv[:, PV_LBM:PV_LBM + 3] = NEG
    elif pos == 1:
        pv[0:64, PV_LBM + 2] = NEG
    pv[:, PV_HALO] = 0.0 if pos == 0 else 1.0
    pv[:, PV_ZERO] = 0.0
    return pv


def _make_in_maps(x_prompt, x_sample, cache_kv_w128, cache_kv_w512, cache_kv_w2048, state_conv,
                  attn_norm, attn_w_in, attn_q_gain, attn_k_gain, attn_w_out,
                  conv_norm, conv_w_in, conv_dw_w, conv_dw_b, conv_ln_g, conv_ln_b, conv_w_out):
    f = lambda a: np.ascontiguousarray(np.asarray(a), dtype=np.float32)
    x_prompt, x_sample = f(x_prompt), f(x_sample)
    caches = [f(cache_kv_w128), f(cache_kv_w512), f(cache_kv_w2048)]
    state_conv = f(state_conv)
    attn_w_in, attn_w_out, conv_w_in, conv_w_out = f(attn_w_in), f(attn_w_out), f(conv_w_in), f(conv_w_out)
    attn_norm, conv_norm = f(attn_norm), f(conv_norm)
    attn_q_gain, attn_k_gain = f(attn_q_gain), f(attn_k_gain)
    conv_dw_w, conv_dw_b, conv_ln_g, conv_ln_b = f(conv_dw_w), f(conv_dw_b), f(conv_ln_g), f(conv_ln_b)
    cft = _const_table()
    in_maps = []
    for c in range(NCORES):
        b, pos = c // 4, c % 4
        m = {
            "xp": np.ascontiguousarray(x_prompt[b, pos * NT:(pos + 1) * NT]),
            "xs": np.ascontiguousarray(x_sample[c]),
            "sconv": np.ascontiguousarray(state_conv[:, c]),
            "attn_w_in": attn_w_in, "attn_w_out": attn_w_out, "conv_w_in": conv_w_in, "conv_w_out": conv_w_out,
            "pvec": _pvec(c, attn_norm, conv_norm, attn_q_gain, attn_k_gain, conv_dw_w, conv_dw_b, conv_ln_g, conv_ln_b),
            "cft": cft, "idxt": _host_tables(c),
        }
        for g in range(3):
            m[f"ck{g}"] = np.ascontiguousarray(caches[g][:, c]).reshape(2, GROUPS[g][0], 2 * D)
        in_maps.append(m)
    return in_maps


def _assemble(res):
    y_prompt = np.stack([np.concatenate([res[4 * b + p]["yp"] for p in range(4)], axis=0) for b in range(2)])
    y_sample = np.stack([res[c]["ys"] for c in range(NCORES)])
    kvp = []
    for g in range(3):
        win = GROUPS[g][0]
        per_b = []
        for b in range(2):
            if g < 2:
                a = res[4 * b + 3][f"okv{g}"]
            else:
                a = np.concatenate([res[4 * b + 2]["okv2"], res[4 * b + 3]["okv2"]], axis=1)
            per_b.append(a.reshape(2, win, 2, 16, 128))
        kvp.append(np.stack(per_b, axis=1))
    conv_p = np.stack([res[4 * b + 3]["oconv"] for b in range(2)], axis=1)
    kvs = [np.stack([res[c][f"skv{g}"].reshape(2, GROUPS[g][0], 2, 16, 128) for c in range(NCORES)], axis=1) for g in range(3)]
    conv_s = np.stack([res[c]["sconv_o"] for c in range(NCORES)], axis=1)
    out = (y_prompt, y_sample, kvp[0], kvp[1], kvp[2], conv_p, kvs[0], kvs[1], kvs[2], conv_s)
    return tuple(np.ascontiguousarray(o, dtype=np.float32) for o in out)


def kernel(**inputs):
    if "nc" not in _NC_CACHE:
        _NC_CACHE["nc"] = build_nc()
    nc = _NC_CACHE["nc"]
    in_maps = _make_in_maps(**inputs)
    res = run_bass_kernel_spmd(nc, in_maps, core_ids=list(range(NCORES))).results
    return _assemble(res)
```

```python
import numpy as np
from contextlib import ExitStack
import concourse.bass as bass
import concourse.mybir as mybir
from concourse.bass_utils import run_bass_kernel_spmd

F32 = mybir.dt.float32
BF16 = mybir.dt.bfloat16
I32 = mybir.dt.int32
AF = mybir.ActivationFunctionType
ALU = mybir.AluOpType
AX = mybir.AxisListType

import os
STAGE = int(os.environ.get("KSTAGE", "7"))
NCORES = 8
D = 2048
KC = 16
NT = 1024
NS = 4
NTOK = NT + NS
TG = [(0, 512), (512, 512), (1024, 4)]
GROUPS = [(128, 1), (512, 4), (2048, 16)]
NTAIL = [128, 512, 1024]
NQKV = 3 * 3 * 2048
CW = 31
BIG = 1.0e6
NEG = -30000.0

PV_AN = 0
PV_CN = 32
PV_QG = 64
PV_KG = 70
PV_DWW = 76
PV_DWB = PV_DWW + 2 * 16 * CW
PV_LNG = PV_DWB + 32
PV_LNB = PV_LNG + 32
PV_EPS_RMS = PV_LNB + 32
PV_EPS_Q = PV_EPS_RMS + 1
PV_EPS_LN = PV_EPS_Q + 1
PV_LBM = PV_EPS_LN + 1
PV_HALO = PV_LBM + 3
PV_ZERO = PV_HALO + 1
NPV = PV_ZERO + 1
CF_D = 0
CF_DD = 256
CF_ID = 260
NCF = 260 + 128


def alibi_sd(g, h):
    n = 48
    i = g * 16 + h
    s = float(np.exp2(np.float32(-8.0) * np.float32(i + 1) / np.float32(n)).astype(np.float32))
    return s * GROUPS[g][1]


class Scope:
    def __init__(self, nc, specs):
        self.nc, self.specs = nc, specs

    _uid = [0]

    def __enter__(self):
        self.es = ExitStack()
        Scope._uid[0] += 1
        u = Scope._uid[0]
        return tuple(self.es.enter_context(self.nc.sbuf_tensor(f"{n}_{u}", list(sh), dt)) for (n, sh, dt) in self.specs)

    def __exit__(self, *a):
        self.es.close()
        return False


class Prog:
    ENGS = ("pe", "act", "dve", "pool", "sp")

    def __init__(self):
        self.q = {e: [] for e in self.ENGS}
        self.cnt = {e: 0 for e in self.ENGS}
        self.waited = {e: {} for e in self.ENGS}
        self.lastw = {}
        self.readers = {}
        self.dcnt = {}

    def _deps(self, eng, r, w):
        need = {}
        for b in r:
            for k, v in self.lastw.get(b, {}).items():
                if need.get(k, 0) < v:
                    need[k] = v
        for b in w:
            for k, v in self.lastw.get(b, {}).items():
                if need.get(k, 0) < v:
                    need[k] = v
            for k, v in self.readers.get(b, {}).items():
                if need.get(k, 0) < v:
                    need[k] = v
        waits = []
        wd = self.waited[eng]
        for k, v in need.items():
            if k == "E:pe" and eng == "pe":
                continue
            if k == "E:sp" and eng == "sp":
                continue
            if wd.get(k, 0) >= v:
                continue
            wd[k] = v
            waits.append((k, v))
        return waits

    def _record(self, tk, r, w):
        k, v = tk
        for b in r:
            d = self.readers.setdefault(b, {})
            if d.get(k, 0) < v:
                d[k] = v
        for b in w:
            self.lastw[b] = {k: v}
            self.readers[b] = {}

    def op(self, eng, fn, r=(), w=(), inc=True):
        waits = self._deps(eng, r, w)
        if inc:
            self.cnt[eng] += 1
            tk = ("E:" + eng, self.cnt[eng])
        else:
            tk = ("E:" + eng, self.cnt[eng] + 1)
        self._record(tk, r, w)
        self.q[eng].append((waits, fn, ("E:" + eng, 1) if inc else None))

    def dma(self, q, fn, r, w, sem, n=16):
        waits = self._deps(q, r, w)
        k = "D:" + q + ":" + sem
        self.dcnt[k] = self.dcnt.get(k, 0) + n
        self._record((k, self.dcnt[k]), r, w)
        self.q[q].append((waits, fn, (k, n)))

    def join(self, sem, keys, q="pool"):
        k = "D:" + q + ":" + sem
        for b in keys:
            self.lastw[b] = {k: self.dcnt[k]}

    def barrier(self, skip=None):
        allk = {("E:" + e): self.cnt[e] for e in self.ENGS if self.cnt[e] > 0}
        allk.update({k: v for k, v in self.dcnt.items() if not (skip and skip in k)})
        for e in self.ENGS:
            waits = []
            for k, v in allk.items():
                if k == "E:" + e:
                    continue
                if self.waited[e].get(k, 0) >= v:
                    continue
                self.waited[e][k] = v
                waits.append((k, v))
            if waits:
                self.q[e].append((waits, None, None))

    def final_waits(self, eng):
        waits = []
        for k, v in self.dcnt.items():
            if self.waited[eng].get(k, 0) < v:
                waits.append((k, v))
        for e in self.ENGS:
            if e != eng and self.cnt[e] > 0 and self.waited[eng].get("E:" + e, 0) < self.cnt[e]:
                waits.append(("E:" + e, self.cnt[e]))
        self.q[eng].append((waits, None, None))


def build_nc():
    nc = bass.Bass("TRN2", target_bir_lowering=False)
    P = Prog()

    def din(name, shape, dt=F32):
        return nc.dram_tensor(name, list(shape), dt, kind="ExternalInput").ap()

    def dout(name, shape, dt=F32):
        return nc.dram_tensor(name, list(shape), dt, kind="ExternalOutput").ap()

    def dint(name, shape, dt=BF16):
        return nc.dram_tensor(name, list(shape), dt, kind="Internal").ap()

    FAKE = bool(os.environ.get("KFAKE"))

    class FakeW:
        def __init__(self, ap):
            self.ap = ap

        def __getitem__(self, idx):
            if not isinstance(idx, tuple):
                return self
            if len(idx) == 2:
                rows, cols = idx
                return self.ap[rows, 0:cols.stop - cols.start]
            return self.ap[idx[1], idx[2]]

    def dbig(name, shape):
        if not FAKE:
            return din(name, shape)
        if name.startswith("ck"):
            return FakeW(dint(name, shape[1:], F32))
        return FakeW(dint(name, [D, 512], F32))

    xp = din("xp", [NT, D])
    xs = din("xs", [NS, D])
    ck = [dbig("ck0", [2, 128, 2 * D]), dbig("ck1", [2, 512, 2 * D]), dbig("ck2", [2, 2048, 2 * D])]
    sconv = din("sconv", [2, 30, D])
    w_in_a = dbig("attn_w_in", [2, D, 20480])
    w_out_a = dbig("attn_w_out", [2, D, D])
    w_in_c = dbig("conv_w_in", [2, D, 3 * D])
    w_out_c = dbig("conv_w_out", [2, D, D])
    pvec_d = din("pvec", [128, NPV])
    cf_d = din("cft", [128, NCF])
    idx_d = din("idxt", [128, 112], I32)

    yp = dout("yp", [NT, D])
    ys = dout("ys", [NS, D])
    dobig = (lambda n, sh: dint(n, sh, F32)) if FAKE else dout
    okv = [dobig("okv0", [2, 128, 2 * D]), dobig("okv1", [2, 512, 2 * D]), dobig("okv2", [2, 1024, 2 * D])]
    oconv = dout("oconv", [2, 30, D])
    skv = [dobig("skv0", [2, 128, 2 * D]), dobig("skv1", [2, 512, 2 * D]), dobig("skv2", [2, 2048, 2 * D])]
    sconv_o = dout("sconv_o", [2, 30, D])

    AGC = [1, 2, 4]
    KTown = [[dint(f"KTown{j}_{g}", [2048, NT]) for g in range(2)] for j in range(2)]
    KTown2 = [[dint(f"KTown{j}_2_{c}", [512, NT]) for c in range(4)] for j in range(2)]
    KTt0 = [dint(f"KTt{j}_0", [2048, 128]) for j in range(2)]
    KTt1 = [[dint(f"KTt{j}_1_{c}", [1024, 512]) for c in range(2)] for j in range(2)]
    KTa0 = [dint(f"KTa{j}_0", [4 * 2048, 128]) for j in range(2)]
    KTa1 = [[dint(f"KTa{j}_1_{c}", [4 * 1024, 512]) for c in range(2)] for j in range(2)]
    KTa2 = [[dint(f"KTa{j}_2_{c}", [4 * 512, NT]) for c in range(4)] for j in range(2)]
    Vown = [[dint(f"Vown{j}_{g}", [16 * NT, 128]) for g in range(2)] for j in range(2)]
    Vown2 = [[dint(f"Vown{j}_2_{c}", [4 * NT, 128]) for c in range(4)] for j in range(2)]
    Vt0 = [dint(f"Vt{j}_0", [16 * 128, 128]) for j in range(2)]
    Vt1 = [[dint(f"Vt{j}_1_{c}", [8 * 512, 128]) for c in range(2)] for j in range(2)]
    Va0 = [dint(f"Va{j}_0", [4 * 16 * 128, 128]) for j in range(2)]
    Va1 = [[dint(f"Va{j}_1_{c}", [4 * 8 * 512, 128]) for c in range(2)] for j in range(2)]
    Va2 = [[dint(f"Va{j}_2_{c}", [4 * 4 * NT, 128]) for c in range(4)] for j in range(2)]
    Vsn = [dint(f"Vsn{j}", [NS, 3 * 2048]) for j in range(2)]
    Utl = [dint(f"Utl{j}", [2048, 32], F32) for j in range(2)]
    Ua = [dint(f"Ua{j}", [4 * 2048, 32], F32) for j in range(2)]

    es = ExitStack()

    def sb(name, shape, dt=F32):
        return es.enter_context(nc.sbuf_tensor(name, list(shape), dt))

    with es:
        xT = sb("xT", [128, KC, NTOK])
        hT = sb("hT", [128, KC, NTOK], BF16)
        pv = sb("pv", [128, NPV])
        cf = sb("cf", [128, NCF])
        idx = sb("idx", [128, 112], I32)
        ones_bf = sb("ones_bf", [128, 128], BF16)
        id_bf = sb("id_bf", [128, 128], BF16)
        banks = [es.enter_context(nc.psum_tensor(f"bank{i}", [128, 512], F32)) for i in range(8)]
        BK = [f"B{i}" for i in range(8)]
        ident = cf[:, CF_ID:CF_ID + 128]

        def pvc(col, n=1):
            return pv[:, col:col + n]

        rot = {}

        def nxt(name, n):
            rot[name] = (rot.get(name, -1) + 1) % n
            return rot[name]

        def mm(out, lhsT, rhs, start, stop, r, w, inc, skip=False):
            P.op("pe", lambda e, o=out, l=lhsT, rr=rhs, s=start, t=stop, sk=skip:
                 e.matmul(o, l, rr, start=s, stop=t, skip_group_check=sk), r, w, inc)

        def tr(out, in_, idn, r, w, inc=True):
            P.op("pe", lambda e, o=out, i=in_, d=idn: e.transpose(o, i, d), r, w, inc)

        def act(out, in_, func, r, w, bias=None, scale=1.0):
            if bias is None:
                P.op("act", lambda e, o=out, i=in_, f=func, s=scale: e.activation(o, i, f, scale=s), r, w)
            else:
                P.op("act", lambda e, o=out, i=in_, f=func, s=scale, b=bias: e.activation(o, i, f, bias=b, scale=s), r, w)

        def cpy(eng, out, in_, r, w):
            if eng == "act":
                P.op("act", lambda e, o=out, i=in_: e.activation(o, i, AF.Copy), r, w)
            else:
                P.op(eng, lambda e, o=out, i=in_: e.tensor_copy(o, i), r, w)

        def tt(eng, out, a, b, op, r, w):
            P.op(eng, lambda e, o=out, x=a, y=b, p=op: e.tensor_tensor(o, x, y, p), r, w)

        def ts(eng, out, a, s1, op0, r, w, s2=None, op1=None):
            if op1 is None:
                P.op(eng, lambda e, o=out, x=a, s=s1, p=op0: e.tensor_scalar(o, x, s, None, p), r, w)
            else:
                P.op(eng, lambda e, o=out, x=a, s=s1, p=op0, t=s2, q=op1: e.tensor_scalar(o, x, s, t, p, q), r, w)

        def stt(eng, out, a, sc, b, op0, op1, r, w):
            P.op(eng, lambda e, o=out, x=a, s=sc, y=b, p=op0, q=op1: e.scalar_tensor_tensor(o, x, s, y, p, q), r, w)

        def recip(out, r, w):
            P.op("dve", lambda e, o=out: e.reciprocal(o, o), r, w)

        def mset(eng, out, val, r, w):
            P.op(eng, lambda e, o=out, v=val: e.memset(o, v), r, w)

        def dma(q, out, in_, r, w, sem):
            P.dma(q, lambda e, o=out, i=in_: e.dma_start(out=o, in_=i), r, w, sem)

        def gather(out, in_, idx_ap, r, w, sem):
            P.dma("pool", lambda e, o=out, i=in_, x=idx_ap: e.indirect_dma_start(
                out=o, out_offset=None, in_=i, in_offset=bass.IndirectOffsetOnAxis(ap=x, axis=0)), r, w, sem)

        def allgather(src, dst, r, w, sem):
            if os.environ.get("KNOAG"):
                return
            P.dma("pool", lambda e, s=src, d=dst: e.collective_compute(
                "AllGather", ALU.bypass, replica_groups=[[0, 1, 2, 3], [4, 5, 6, 7]], ins=[s], outs=[d]),
                r, w, sem, n=1)

        def load_w(dst, wsrc, r0, nk, c0, n, key, sem):
            src = wsrc[r0:r0 + nk * 128, c0:c0 + n].rearrange("(k p) c -> p k c", p=128)
            dma("pool", dst, src, [], key if isinstance(key, list) else [key], sem)

        dma("sp", pv[:], pvec_d, [], ["pv"], "pv")
        dma("sp", cf[:], cf_d, [], ["cf"], "cf")
        dma("sp", idx[:], idx_d, [], ["idx"], "idx")
        mset("dve", ones_bf[:], 1.0, [], ["ones"])
        cpy("dve", id_bf[:], ident, ["cf"], ["idbf"])

        def xkeys(kc, gi):
            if gi == 0:
                return [f"xT{kc}_{t}" for t in range(4)]
            if gi == 1:
                return [f"xT{kc}_{t}" for t in range(4, 8)]
            return [f"xT{kc}_8"]

        with nc.sbuf_tensor("xin", [128, 2, D], F32) as xin:
            for t8 in range(9):
                s = nxt("xin", 2)
                n = 128 if t8 < 8 else NS
                src = xp[t8 * 128:(t8 + 1) * 128, :] if t8 < 8 else xs
                dma("sp", xin[0:n, s, :], src, [], [f"xin{s}"], f"xin{s}")
                for k4 in range(4):
                    b = 4 + nxt("ldb", 4)
                    for kk in range(4):
                        kc = k4 * 4 + kk
                        tr(banks[b][:, kk * 128:kk * 128 + n], xin[0:n, s, kc * 128:(kc + 1) * 128], ident[0:n, 0:n],
                           [f"xin{s}", "cf"], [BK[b]], inc=(kk == 3))
                    dst = xT[:, k4 * 4:(k4 + 1) * 4, t8 * 128:t8 * 128 + n]
                    srcp = banks[b][:, :].rearrange("p (k t) -> p k t", k=4)[:, :, 0:n]
                    cpy("dve" if (k4 % 2 == 0) else "act", dst, srcp, [BK[b]],
                        [f"xT{kc_}_{t8}" for kc_ in range(k4 * 4, k4 * 4 + 4)])
            P.barrier()

        def rmsnorm(gcol):
            with Scope(nc, [("sqb", [128, 2, 512], BF16), ("rstd", [128, NTOK], F32)]) as (sqb, rstd,):
                for gi, (t0, n) in enumerate(TG):
                    b = 4 + nxt("ldb", 4)
                    for kc in range(KC):
                        s = nxt("sqb", 2)
                        act(sqb[:, s, 0:n], xT[:, kc, t0:t0 + n], AF.Square, xkeys(kc, gi), [f"sqb{s}"])
                        mm(banks[b][:, 0:n], ones_bf[:], sqb[:, s, 0:n], kc == 0, kc == KC - 1,
                           ["ones", f"sqb{s}"], [BK[b]], inc=True)
                    act(rstd[:, t0:t0 + n], banks[b][:, 0:n], AF.Sqrt, [BK[b], "pv"], [f"rstd{gi}"],
                        bias=pvc(PV_EPS_RMS), scale=1.0 / D)
                    recip(rstd[:, t0:t0 + n], [f"rstd{gi}"], [f"rstd{gi}"])
                    for kc in range(KC):
                        stt("dve", hT[:, kc, t0:t0 + n], xT[:, kc, t0:t0 + n],
                            pvc(gcol + kc), rstd[:, t0:t0 + n], ALU.mult, ALU.mult,
                            xkeys(kc, gi) + [f"rstd{gi}", "pv"], [f"hT{kc}_{gi}"])
                P.barrier()

        def pnorm_rstd(acc_ap, n, sq_ap, sqkey, ssb, rt_ap, rtkey, eps_col, scale):
            act(sq_ap, acc_ap, AF.Square, [sqkey[0]], [sqkey[1]])
            mm(banks[ssb][:, 0:n], ones_bf[:], sq_ap, True, True, ["ones", sqkey[1]], [BK[ssb]], inc=True)
            act(rt_ap, banks[ssb][:, 0:n], AF.Sqrt, [BK[ssb], "pv"], [rtkey], bias=pvc(eps_col), scale=scale)
            recip(rt_ap, [rtkey], [rtkey])

        def attn_layer(j):
            win = w_in_a[j]
            wout = w_out_a[j]
            rmsnorm(PV_AN + 16 * j)
            with Scope(nc, [("gT", [128, 8, NTOK], BF16), ("KTs", [128, 3, 16, NS], BF16)]) as (gT, KTs,):
                if STAGE < 2:
                    return
                with Scope(nc, [("wtk", [128, 2, 3, KC, 128], BF16), ("sq1", [128, 2, 512], BF16), ("rt1", [128, 2, 512], F32), ("Kn", [128, 2, 512], F32), ("KTb", [128, 2, NTOK], BF16), ("Kst", [128, 4, 128], F32)]) as (wtk, sq1, rt1, Kn, KTb, Kst,):
                    def ldk(h):
                        s = h % 2
                        keys = [f"wtk{s}_{g}" for g in range(3)]
                        for g in range(3):
                            load_w(wtk[:, s, g], win, 0, KC, g * 6144 + 2048 + h * 128, 128, keys if g == 0 else [], f"wtk{s}")
                        P.join(f"wtk{s}", keys)
                    ldk(0)
                    ldk(1)
                    for h in range(16):
                        s = h % 2
                        for g in range(3):
                            win_g, d = GROUPS[g]
                            kb = nxt("KTb", 2)
                            for gi, (t0, n) in enumerate(TG):
                                b = nxt("pb", 4)
                                for kc in range(KC):
                                    mm(banks[b][:, 0:n], wtk[:, s, g, kc, :], hT[:, kc, t0:t0 + n], kc == 0, kc == KC - 1,
                                       [f"wtk{s}_{g}", f"hT{kc}_{gi}"], [BK[b]], inc=(kc == KC - 1))
                                q2 = nxt("sq1", 2)
                                pnorm_rstd(banks[b][:, 0:n], n, sq1[:, q2, 0:n], (BK[b], f"sq1{q2}"), 4 + nxt("ssb", 2),
                                           rt1[:, q2, 0:n], f"rt1{q2}", PV_EPS_RMS, 1.0 / 128)
                                stt("dve", Kn[:, q2, 0:n], banks[b][:, 0:n], pvc(PV_KG + 3 * j + g), rt1[:, q2, 0:n],
                                    ALU.mult, ALU.mult, [BK[b], f"rt1{q2}", "pv"], [f"Kn{q2}"])
                                if gi < 2:
                                    dstv = KTb[:, kb, 0:NT].rearrange("p (r i) -> p r i", r=d)[:, :, t0 // d:(t0 + n) // d]
                                    srcv = Kn[:, q2, 0:n].rearrange("p (i r) -> p r i", r=d)
                                    cpy("pool", dstv, srcv, [f"Kn{q2}"], [f"KTb{kb}_{gi}"])
                                else:
                                    cpy("pool", KTb[:, kb, NT:NTOK], Kn[:, q2, 0:n], [f"Kn{q2}"], [f"KTb{kb}_2"])
                                    cpy("pool", KTs[:, g, h, :], Kn[:, q2, 0:n], [f"Kn{q2}"], [f"KTs{g}_{h}"])
                                if gi < 2:
                                    for t4 in range(4):
                                        tt8 = gi * 4 + t4
                                        row0 = tt8 * 128 - (NT - NTAIL[g])
                                        if row0 < 0:
                                            continue
                                        tb = 6 + nxt("tb", 2)
                                        tr(banks[tb][:, 0:128], Kn[:, q2, t4 * 128:(t4 + 1) * 128], ident, [f"Kn{q2}", "cf"], [BK[tb]])
                                        ks = nxt("Kst", 4)
                                        cpy("act", Kst[:, ks, :], banks[tb][:, 0:128], [BK[tb]], [f"Kst{ks}"])
                                        dma("sp", okv[g][j, row0:row0 + 128, h * 128:(h + 1) * 128], Kst[:, ks, :],
                                            [f"Kst{ks}"], [], f"Kst{ks}")
                                else:
                                    tb = 6 + nxt("tb", 2)
                                    tr(banks[tb][0:NS, 0:128], Kn[:, q2, 0:NS], ident, [f"Kn{q2}", "cf"], [BK[tb]])
                                    ks = nxt("Kst", 4)
                                    cpy("act", Kst[0:NS, ks, :], banks[tb][0:NS, 0:128], [BK[tb]], [f"Kst{ks}"])
                                    dma("sp", skv[g][j, win_g - NS:win_g, h * 128:(h + 1) * 128], Kst[0:NS, ks, :],
                                        [f"Kst{ks}"], [], f"Kst{ks}")
                            kr = [f"KTb{kb}_0", f"KTb{kb}_1"]
                            if g < 2:
                                dma("sp", KTown[j][g][h * 128:(h + 1) * 128, :], KTb[:, kb, 0:NT], kr, [f"KTown{g}_{h}"], f"KTb{kb}")
                            else:
                                dma("sp", KTown2[j][h // 4][(h % 4) * 128:(h % 4 + 1) * 128, :], KTb[:, kb, 0:NT], kr, [f"KTown{g}_{h}"], f"KTb{kb}")
                            if g == 0:
                                dma("sp", KTt0[j][h * 128:(h + 1) * 128, :], KTb[:, kb, NT - 128:NT], kr, [f"KTt0_{h}"], f"KTb{kb}")
                            if g == 1:
                                dma("sp", KTt1[j][h // 8][(h % 8) * 128:(h % 8 + 1) * 128, :].rearrange("p (r i) -> p r i", r=4),
                                    KTb[:, kb, 0:NT].rearrange("p (r i) -> p r i", r=4)[:, :, 128:256], kr, [f"KTt1_{h}"], f"KTb{kb}")
                        if h + 2 < 16:
                            ldk(h + 2)
                        if h % 4 == 3:
                            c = h // 4
                            allgather(KTown2[j][c], KTa2[j][c], [f"KTown2_{hh}" for hh in range(4 * c, 4 * c + 4)], [f"KTa2_{c}"], f"ag{j}")
                        if h % 8 == 7:
                            c = h // 8
                            allgather(KTt1[j][c], KTa1[j][c], [f"KTt1_{hh}" for hh in range(8 * c, 8 * c + 8)], [f"KTa1_{c}"], f"ag{j}")
                        if h == 15:
                            allgather(KTt0[j], KTa0[j], [f"KTt0_{hh}" for hh in range(16)], ["KTa0_0"], f"ag{j}")
                    P.barrier(skip=":ag")
                if STAGE < 3:
                    return
                with Scope(nc, [("wtv", [128, 2, KC, 512], BF16), ("Vsb", [128, 3, 512], BF16), ("Vsf", [128, 3, 512], F32)]) as (wtv, Vsb, Vsf,):
                    order = [(g, hq) for g in (2, 1, 0) for hq in range(4)]
                    KV = int(os.environ.get("KV", "0"))
                    realdma = dma

                    def dmaf(bit):
                        return (lambda *a, **k: None) if (KV >> bit) & 1 else realdma

                    def ldv(i):
                        g, hq = order[i]
                        load_w(wtv[:, i % 2], win, 0, KC, g * 6144 + 4096 + hq * 512, 512, f"wtv{i % 2}", f"wtv{i % 2}")
                    ldv(0)
                    ldv(1)
                    if (KV >> 5) & 1:
                        order = order[:2]
                    for i, (g, hq) in enumerate(order):
                        s = i % 2
                        win_g, d = GROUPS[g]
                        for t8 in range(8 if (KV >> 3) & 1 else 9):
                            n = 128 if t8 < 8 else NS
                            gi = 0 if t8 < 4 else (1 if t8 < 8 else 2)
                            b = nxt("pb", 4)
                            for kc in range(KC):
                                mm(banks[b][0:n, :], hT[:, kc, t8 * 128:t8 * 128 + n], wtv[:, s, kc, :], kc == 0, kc == KC - 1,
                                   [f"wtv{s}", f"hT{kc}_{gi}"], [BK[b]], inc=(kc == KC - 1))
                            vs = nxt("Vsb", 3)
                            if not (KV >> 6) & 1:
                                cpy("dve", Vsb[0:n, vs, :], banks[b][0:n, :], [BK[b]], [f"Vsb{vs}"])
                            row0 = t8 * 128 - (NT - NTAIL[g])
                            need_f = (t8 == 8) or row0 >= 0
                            if need_f and not (KV >> 4) & 1:
                                cpy("dve", Vsf[0:n, vs, :], banks[b][0:n, :], [BK[b]], [f"Vsf{vs}"])
                            if t8 < 8:
                                if g < 2:
                                    dst = Vown[j][g].rearrange("(h t) e -> t h e", h=16)[t8 * 128:(t8 + 1) * 128, hq * 4:hq * 4 + 4, :]
                                else:
                                    dst = Vown2[j][hq].rearrange("(h t) e -> t h e", h=4)[t8 * 128:(t8 + 1) * 128, :, :]
                                dmaf(0)("sp", dst, Vsb[:, vs, :].rearrange("p (h e) -> p h e", h=4), [f"Vsb{vs}"],
                                    [f"Vown{g}_{hq}_{t8}"], f"Vsb{vs}")
                                if g < 2 and row0 >= 0:
                                    if g == 0:
                                        dst = Vt0[j].rearrange("(h t) e -> t h e", h=16)[row0:row0 + 128, hq * 4:hq * 4 + 4, :]
                                    else:
                                        dst = Vt1[j][hq // 2].rearrange("(h t) e -> t h e", h=8)[row0:row0 + 128, (hq % 2) * 4:(hq % 2) * 4 + 4, :]
                                    dmaf(0)("sp", dst, Vsb[:, vs, :].rearrange("p (h e) -> p h e", h=4), [f"Vsb{vs}"],
                                        [f"Vt{g}_{hq}_{t8}"], f"Vsb{vs}")
                                if row0 >= 0:
                                    dmaf(1)("sp", okv[g][j, row0:row0 + 128, D + hq * 512:D + (hq + 1) * 512], Vsf[:, vs, :],
                                        [f"Vsf{vs}"], [], f"Vsf{vs}")
                            else:
                                dmaf(2)("sp", Vsn[j][:, g * 2048 + hq * 512:g * 2048 + (hq + 1) * 512], Vsb[0:NS, vs, :],
                                    [f"Vsb{vs}"], [f"Vsn{g}_{hq}"], f"Vsb{vs}")
                                dmaf(1)("sp", skv[g][j, win_g - NS:win_g, D + hq * 512:D + (hq + 1) * 512], Vsf[0:NS, vs, :],
                                    [f"Vsf{vs}"], [], f"Vsf{vs}")
                        if i + 2 < len(order):
                            ldv(i + 2)
                        if g == 2:
                            allgather(Vown2[j][hq], Va2[j][hq], [f"Vown2_{hq}_{t}" for t in range(8)], [f"Va2_{hq}"], f"ag{j}")
                        elif g == 1 and hq % 2 == 1:
                            c = hq // 2
                            allgather(Vt1[j][c], Va1[j][c], [f"Vt1_{q}_{t}" for q in (2 * c, 2 * c + 1) for t in range(8)], [f"Va1_{c}"], f"ag{j}")
                        elif g == 0 and hq == 3:
                            allgather(Vt0[j], Va0[j], [f"Vt0_{q}_{t}" for q in range(4) for t in range(8)], ["Va0_0"], f"ag{j}")
                    for g in range(3):
                        win_g = GROUPS[g][0]
                        for r0 in range(0, (win_g - NS) if not os.environ.get("KNOSHIFT") else 0, 128):
                            r1 = min(r0 + 128, win_g - NS)
                            dma("pool", skv[g][j, r0:r1, :], ck[g][j, NS + r0:NS + r1, :], [], [], "cshift")
                    P.barrier()
                if STAGE < 4:
                    return
                with Scope(nc, [("wq", [128, 4, KC, 128], BF16), ("gst", [128, 2048], BF16), ("QT", [128, 3, NT], BF16), ("QTs", [128, 3, NS], BF16), ("sz", [128, NTOK], F32), ("KTx", [128, 5760], BF16), ("Vx", [128, 53, 128], BF16), ("PT", [128, 4, 256], BF16), ("Sb", [128, 4, 256], F32), ("Ksc", [128, 9, 128], F32), ("KsT", [128, 9, 128], BF16), ("Vsc", [128, 9, 128], BF16), ("Vnh", [NS, 3, 128], BF16), ("og", [128, NTOK], F32), ("wo", [128, 2, 8, 128], BF16), ("sq2", [128, 2, 512], BF16), ("rt2", [128, 2, 512], F32)]) as (wq, gst, QT, QTs, sz, KTx, Vx, PT, Sb, Ksc, KsT, Vsc, Vnh, og, wo, sq2, rt2,):
                    KOFF = [0, 1152, 1152 + 1536]
                    VOFF = [0, 9, 21]
                    NCH = [9, 3, 2]

                    def ktx(g, r):
                        d = GROUPS[g][1]
                        ext = 128 + NT // d
                        return KTx[:, KOFF[g] + r * ext:KOFF[g] + (r + 1) * ext]

                    qcols = [(g * 6144 + 0) for g in range(3)] + [NQKV]

                    def ldq(hh, c):
                        k = hh * 4 + c
                        s = k % 4
                        load_w(wq[:, s], win, 0, KC, qcols[c] + hh * 128, 128, f"wq{s}", f"wq{s}")
                    for k in range(4):
                        ldq(k // 4, k % 4)

                    pend = []
                    LAG = 3

                    def unit_b(o_ap, vap, l_ap, nk, nq, s, rk, okey, lkey):
                        mm(o_ap, vap, PT[0:nk, s, 0:nq], False, False, rk[1] + [f"PT{s}"], [okey], inc=False, skip=True)
                        mm(l_ap, ones_bf[0:nk, :], PT[0:nk, s, 0:nq], False, False, ["ones", f"PT{s}"], [lkey], inc=True, skip=True)

                    def flush(keep=0):
                        while len(pend) > keep:
                            unit_b(*pend.pop(0))

                    def unit(ktap, qtap, vap, dap, sd, o_ap, l_ap, nk, nq, rk, okey, lkey, bias=None):
                        b = 4 + nxt("sbk", 2)
                        mm(banks[b][0:nk, 0:nq], ktap, qtap, True, True, rk[0], [BK[b]], inc=True)
                        s = nxt("PT", 4)
                        stt("dve", Sb[0:nk, s, 0:nq], dap, -sd, banks[b][0:nk, 0:nq], ALU.mult, ALU.add,
                            [BK[b], "cf"], [f"Sb{s}"])
                        act(PT[0:nk, s, 0:nq], Sb[0:nk, s, 0:nq], AF.Exp, [f"Sb{s}", "pv"], [f"PT{s}"],
                            bias=(bias if bias is not None else pvc(PV_ZERO)[0:nk, :]))
                        pend.append((o_ap, vap, l_ap, nk, nq, s, rk, okey, lkey))
                        flush(keep=LAG)

                    for h in range(16):
                        kxk = [f"KTx{g}" for g in range(3)]
                        vxk = [f"Vx{g}" for g in range(3)]
                        for g in range(3):
                            win_g, d = GROUPS[g]
                            ext = 128 + NT // d
                            dstk = KTx[:, KOFF[g]:KOFF[g] + d * ext].rearrange("p (r x) -> p r x", r=d)
                            ksrc = KTown[j][g][h * 128:(h + 1) * 128, :] if g < 2 else KTown2[j][h // 4][(h % 4) * 128:(h % 4 + 1) * 128, :]
                            dma("sp", dstk[:, :, 128:ext], ksrc.rearrange("p (r i) -> p r i", r=d),
                                [f"KTown{g}_{h}"], [kxk[g]], f"KTx{g}")
                            vv = Vx[:, VOFF[g]:VOFF[g] + d * NCH[g], :].rearrange("p (r c) e -> p r c e", r=d)
                            vsrc = Vown[j][g][h * NT:(h + 1) * NT, :] if g < 2 else Vown2[j][h // 4][(h % 4) * NT:(h % 4 + 1) * NT, :]
                            vr = [f"Vown{g}_{h // 4}_{t}" for t in range(8)]
                            if g == 0:
                                dma("sp", vv[:, 0, 1:9, :], vsrc.rearrange("(c p) e -> p c e", p=128), vr, [vxk[g]], f"Vx{g}")
                            elif g == 1:
                                for r in range(4):
                                    dma("sp", vv[:, r, 1:3, :], vsrc.rearrange("(c p r) e -> p r c e", p=128, r=4)[:, r],
                                        vr, [vxk[g]], f"Vx{g}")
                            else:
                                dma("sp", vv[0:64, :, 1, :], vsrc.rearrange("(p r) e -> p r e", r=16), vr, [vxk[g]], f"Vx{g}")
                        ia = idx[:, h:h + 1]
                        ik1 = idx[:, 16 + h:17 + h]
                        ik2a = idx[:, 32 + h:33 + h]
                        ik2b = idx[:, 48 + h:49 + h]
                        iv0 = idx[:, 64 + h:65 + h]
                        iv1 = idx[:, 80 + h:81 + h]
                        iv2 = idx[:, 96 + h:97 + h]
                        gather(KTx[:, 0:128], KTa0[j], ia, ["KTa0_0", "idx"], [kxk[0]], "KTx0")
                        d1 = KTx[:, KOFF[1]:KOFF[1] + 4 * 384].rearrange("p (r x) -> p r x", r=4)
                        gather(gst[:, 0:512], KTa1[j][h // 8], ik1, [f"KTa1_{h // 8}", "idx"], ["gst"], "gst")
                        cpy("pool", d1[:, :, 0:128], gst[:, 0:512].rearrange("p (r i) -> p r i", r=4), ["gst"], [kxk[1]])
                        d2 = KTx[:, KOFF[2]:KOFF[2] + 16 * 192].rearrange("p (r x) -> p r x", r=16)
                        gather(gst[:, 0:1024], KTa2[j][h // 4], ik2a, [f"KTa2_{h // 4}", "idx"], ["gst"], "gst")
                        cpy("pool", d2[:, :, 64:128], gst[:, 0:1024].rearrange("p (r i) -> p r i", r=16), ["gst"], [kxk[2]])
                        gather(gst[:, 0:1024], KTa2[j][h // 4], ik2b, [f"KTa2_{h // 4}", "idx"], ["gst"], "gst")
                        cpy("pool", d2[:, :, 0:64], gst[:, 0:1024].rearrange("p (r i) -> p r i", r=16), ["gst"], [kxk[2]])
                        gather(Vx[:, 0, :], Va0[j], iv0, ["Va0_0", "idx"], [vxk[0]], "Vx0")
                        v1 = Vx[:, VOFF[1]:VOFF[1] + 12, :].rearrange("p (r c) e -> p r c e", r=4)
                        gather(gst[:, 0:512], Va1[j][h // 8].rearrange("(n r) e -> n (r e)", r=4), iv1, [f"Va1_{h // 8}", "idx"], ["gst"], "gst")
                        cpy("pool", v1[:, :, 0, :], gst[:, 0:512].rearrange("p (r e) -> p r e", r=4), ["gst"], [vxk[1]])
                        v2 = Vx[:, VOFF[2]:VOFF[2] + 32, :].rearrange("p (r c) e -> p r c e", r=16)
                        gather(gst[:, 0:2048], Va2[j][h // 4].rearrange("(n r) e -> n (r e)", r=16), iv2, [f"Va2_{h // 4}", "idx"], ["gst"], "gst")
                        cpy("pool", v2[:, :, 0, :], gst[:, 0:2048].rearrange("p (r e) -> p r e", r=16), ["gst"], [vxk[2]])
                        ci = 0
                        for g in range(3):
                            win_g, d = GROUPS[g]
                            nres = 1 if g == 0 else NS
                            for r in range(nres):
                                rows = ck[g][j, r:win_g:d, :] if g > 0 else ck[g][j, 0:128, :]
                                dma("sp", Ksc[:, ci, :], rows[:, h * 128:(h + 1) * 128], [], [f"Ksc{ci}"], f"Ksc{ci}")
                                dma("pool", Vsc[:, ci, :], rows[:, D + h * 128:D + (h + 1) * 128], [], [f"Vsc{ci}"], f"Vsc{ci}")
                                ci += 1
                        dma("sp", Vnh[:, :, :], Vsn[j].rearrange("t (g c) -> t g c", g=3)[:, :, h * 128:(h + 1) * 128],
                            [f"Vsn{g}_{h // 4}" for g in range(3)], ["Vnh"], "Vnh")
                        for c in range(4):
                            k = h * 4 + c
                            s = k % 4
                            for gi, (t0, n) in enumerate(TG):
                                b = 6 + nxt("qb", 2)
                                for kc in range(KC):
                                    mm(banks[b][:, 0:n], wq[:, s, kc, :], hT[:, kc, t0:t0 + n], kc == 0, kc == KC - 1,
                                       [f"wq{s}", f"hT{kc}_{gi}"], [BK[b]], inc=(kc == KC - 1))
                                if c == 3:
                                    act(sz[:, t0:t0 + n], banks[b][:, 0:n], AF.Silu, [BK[b]], [f"sz{gi}"])
                                    continue
                                g = c
                                d = GROUPS[g][1]
                                q2 = nxt("sq2", 2)
                                pnorm_rstd(banks[b][:, 0:n], n, sq2[:, q2, 0:n], (BK[b], f"sq2{q2}"), 4 + nxt("sbk", 2),
                                           rt2[:, q2, 0:n], f"rt2{q2}", PV_EPS_Q, 1.0)
                                if gi < 2:
                                    dstv = QT[:, g, :].rearrange("p (r i) -> p r i", r=d)[:, :, t0 // d:(t0 + n) // d]
                                    a_v = banks[b][:, 0:n].rearrange("p (i r) -> p r i", r=d)
                                    r_v = rt2[:, q2, 0:n].rearrange("p (i r) -> p r i", r=d)
                                    stt("dve", dstv, a_v, pvc(PV_QG + 3 * j + g), r_v, ALU.mult, ALU.mult,
                                        [BK[b], f"rt2{q2}", "pv"], [f"QT{g}_{gi}"])
                                else:
                                    stt("dve", QTs[:, g, :], banks[b][:, 0:n], pvc(PV_QG + 3 * j + g), rt2[:, q2, 0:n],
                                        ALU.mult, ALU.mult, [BK[b], f"rt2{q2}", "pv"], [f"QTs{g}"])
                            if k + 4 < 64:
                                ldq((k + 4) // 4, (k + 4) % 4)
                        for ci in range(9):
                            tb = 6 + nxt("qb", 2)
                            tr(banks[tb][:, 0:128], Ksc[:, ci, :], ident, [f"Ksc{ci}", "cf"], [BK[tb]])
                            cpy("dve", KsT[:, ci, :], banks[tb][:, 0:128], [BK[tb]], [f"KsT{ci}"])
                        for b in range(4):
                            mset("dve", banks[b][:, :], 0.0, [], [BK[b]])
                        mset("dve", banks[7][:, 0:16], 0.0, [], [BK[7]])
                        for g in range(3):
                            win_g, d = GROUPS[g]
                            nown = NT // d
                            sd = alibi_sd(g, h)
                            for r in range(d):
                                kt = ktx(g, r)
                                for c in range(NCH[g]):
                                    nk = min(128, 128 + nown - c * 128)
                                    ilo = max(0, 128 * (c - 1))
                                    ihi = min(nown, 128 * (c - 1) + 256)
                                    vap = Vx[0:nk, VOFF[g] + r * NCH[g] + c, :]
                                    bias = pvc(PV_LBM + g) if c == 0 else None
                                    half_n = 512 // d
                                    for hf in range(2):
                                        a = max(ilo, hf * half_n)
                                        e_ = min(ihi, (hf + 1) * half_n)
                                        if a >= e_:
                                            continue
                                        nq = e_ - a
                                        j0 = a - 128 * (c - 1)
                                        col0 = (a - hf * half_n) * d + r
                                        cols = slice(col0, col0 + (nq - 1) * d + 1, d)
                                        unit(kt[:, c * 128:c * 128 + nk], QT[:, g, r * nown + a:r * nown + e_], vap,
                                             cf[0:nk, CF_D + j0:CF_D + j0 + nq], sd,
                                             banks[hf][:, cols], banks[2 + hf][:, cols], nk, nq,
                                             ([f"KTx{g}", f"QT{g}_{hf}"], [f"Vx{g}"]), BK[hf], BK[2 + hf], bias=bias)
                        ci = 0
                        for g in range(3):
                            win_g, d = GROUPS[g]
                            sd = alibi_sd(g, h)
                            nres = 1 if g == 0 else NS
                            for r in range(nres):
                                q0, nq = (0, NS) if g == 0 else (r, 1)
                                unit(KsT[:, ci, :], QTs[:, g, q0:q0 + nq], Vsc[:, ci, :],
                                     cf[:, CF_D + 128 + (0 if g == 0 else 0):CF_D + 128 + nq], sd,
                                     banks[7][:, q0:q0 + nq], banks[7][:, 8 + q0:8 + q0 + nq], 128, nq,
                                     ([f"KsT{ci}", f"QTs{g}"], [f"Vsc{ci}"]), BK[7], BK[7])
                                ci += 1
                            dd = cf[0:NS, CF_D:CF_D + NS] if g == 0 else cf[0:NS, CF_DD:CF_DD + NS]
                            unit(KTs[:, g, h, :], QTs[:, g, :], Vnh[:, g, :], dd, sd,
                                 banks[7][:, 0:NS], banks[7][:, 8:8 + NS], NS, NS,
                                 ([f"KTs{g}_{h}", f"QTs{g}"], ["Vnh"]), BK[7], BK[7])
                        flush()
                        for hf in range(2):
                            cpy("act", og[:, hf * 512:(hf + 1) * 512], banks[2 + hf][:, :], [BK[2 + hf]], [f"og{hf}"])
                            recip(og[:, hf * 512:(hf + 1) * 512], [f"og{hf}"], [f"og{hf}"])
                            tt("dve", og[:, hf * 512:(hf + 1) * 512], og[:, hf * 512:(hf + 1) * 512], banks[hf][:, :], ALU.mult,
                               [f"og{hf}", BK[hf]], [f"og{hf}"])
                            tt("pool", gT[:, h % 8, hf * 512:(hf + 1) * 512], og[:, hf * 512:(hf + 1) * 512],
                               sz[:, hf * 512:(hf + 1) * 512], ALU.mult, [f"og{hf}", f"sz{hf}"], [f"gT{h % 8}_{hf}"])
                        cpy("act", og[:, NT:NTOK], banks[7][:, 8:8 + NS], [BK[7]], ["og2"])
                        recip(og[:, NT:NTOK], ["og2"], ["og2"])
                        tt("dve", og[:, NT:NTOK], og[:, NT:NTOK], banks[7][:, 0:NS], ALU.mult, ["og2", BK[7]], ["og2"])
                        tt("pool", gT[:, h % 8, NT:NTOK], og[:, NT:NTOK], sz[:, NT:NTOK], ALU.mult, ["og2", "sz2"], [f"gT{h % 8}_2"])
                        if h % 8 == 7:
                            hb = h - 7
                            for dc in range(16):
                                s = nxt("wo", 2)
                                load_w(wo[:, s], wout, hb * 128, 8, dc * 128, 128, f"wo{s}", f"wo{s}")
                                for gi, (t0, n) in enumerate(TG):
                                    b = 6 + nxt("qb", 2)
                                    for hh in range(8):
                                        mm(banks[b][:, 0:n], wo[:, s, hh, :], gT[:, hh, t0:t0 + n], hh == 0, hh == 7,
                                           [f"wo{s}", f"gT{hh}_{gi}"], [BK[b]], inc=(hh == 7))
                                    tt("dve", xT[:, dc, t0:t0 + n], xT[:, dc, t0:t0 + n], banks[b][:, 0:n], ALU.add,
                                       xkeys(dc, gi) + [BK[b]], xkeys(dc, gi))
                    P.barrier()

        def conv_layer(j):
            win = w_in_c[j]
            wout = w_out_c[j]
            rmsnorm(PV_CN + 16 * j)
            dwc = PV_DWW + j * 16 * CW
            with Scope(nc, [("cT", [128, KC, NTOK], BF16), ("szc", [128, KC, NTOK], BF16), ("Us", [128, KC, 34], F32), ("Uh", [128, KC, 64], F32)]) as (cT, szc, Us, Uh,):
                with Scope(nc, [("wc", [128, 3, KC, 128], BF16), ("Ub", [128, NT], F32), ("sg", [128, 2, 512], F32), ("ac", [128, 2, NT], F32), ("prs", [128, NS * CW], F32), ("cs", [128, 32], F32)]) as (wc, Ub, sg, ac, prs, cs,):
                    st = bass.AP(ac, 0, [list(ac[0:32, 0, 0:1].ap[0]), [1, D]])

                    def ldc(k):
                        cc, a = k // 3, k % 3
                        load_w(wc[:, a], win, 0, KC, a * D + cc * 128, 128, f"wc{a}", f"wc{a}")
                    for k in range(3):
                        ldc(k)
                    dma("sp", st[0:30, :], sconv[j], [], ["st"], "st")
                    for cc in range(KC):
                        tb = 6 + nxt("qb", 2)
                        tr(banks[tb][:, 0:30], st[0:30, cc * 128:(cc + 1) * 128], ident[0:30, 0:30], ["st", "cf"], [BK[tb]])
                        cpy("act", Us[:, cc, 0:30], banks[tb][:, 0:30], [BK[tb]], [f"Us{cc}"])
                    dma("pool", sconv_o[j, 0:26, :], sconv[j, 4:30, :], [], [], "cshift")
                    P.barrier()
                    for cc in range(KC):
                        for a in range(3):
                            for gi, (t0, n) in enumerate(TG):
                                b = nxt("pb", 6)
                                for kc in range(KC):
                                    mm(banks[b][:, 0:n], wc[:, a, kc, :], hT[:, kc, t0:t0 + n], kc == 0, kc == KC - 1,
                                       [f"wc{a}", f"hT{kc}_{gi}"], [BK[b]], inc=(kc == KC - 1))
                                udst = Ub[:, t0:t0 + n] if gi < 2 else Us[:, cc, 30:34]
                                ukey = f"Ub_{gi}" if gi < 2 else f"Us{cc}"
                                if a == 0:
                                    cpy("act", udst, banks[b][:, 0:n], [BK[b]], [ukey])
                                elif a == 1:
                                    q2 = nxt("sg", 2)
                                    act(sg[:, q2, 0:n], banks[b][:, 0:n], AF.Sigmoid, [BK[b]], [f"sg{q2}"])
                                    tt("dve", udst, udst, sg[:, q2, 0:n], ALU.mult, [f"sg{q2}", ukey], [ukey])
                                else:
                                    act(szc[:, cc, t0:t0 + n], banks[b][:, 0:n], AF.Silu, [BK[b]], [f"szc{cc}_{gi}"])
                            if cc + 1 < KC:
                                ldc((cc + 1) * 3 + a)
                        uk = ["Ub_0", "Ub_1"]
                        cpy("pool", Uh[:, cc, 32:64], Ub[:, 0:32], uk, [f"Uh{cc}"])
                        dma("sp", Utl[j][cc * 128:(cc + 1) * 128, :], Ub[:, NT - 32:NT], uk, [f"Utl{cc}"], "Ub")
                        L = NT - 30
                        a0 = ac[:, 0, 0:L]
                        a1 = ac[:, 1, 0:L]
                        ts("dve", a0, Ub[:, 0:L], pvc(dwc + cc * CW + 0), ALU.mult, uk + ["pv"], ["ac0"],
                           s2=pvc(PV_DWB + 16 * j + cc), op1=ALU.add)
                        for tap in range(1, CW):
                            stt("dve", a0, Ub[:, tap:tap + L], pvc(dwc + cc * CW + tap), a0, ALU.mult, ALU.add, uk + ["ac0", "pv"], ["ac0"])
                        cpy("pool", cT[:, cc, 30:NT], a0, ["ac0"], [f"cT{cc}_m"])
                        base = Us[:, cc, 0:1]
                        win_ap = bass.AP(Us, base.offset, [list(base.ap[0]), [1, NS], [1, CW]])
                        wcol = pvc(dwc + cc * CW, CW)
                        w_ap = bass.AP(pv, wcol.offset, [list(wcol.ap[0]), [0, NS], [1, CW]])
                        pr = prs[:, :].rearrange("p (t k) -> p t k", k=CW)
                        tt("dve", pr, win_ap, w_ap, ALU.mult, [f"Us{cc}", "pv"], ["prs"])
                        P.op("dve", lambda e, o=cs[:, 0:NS], i=pr: e.tensor_reduce(o, i, AX.X, ALU.add), ["prs"], ["cs"])
                        ts("dve", cT[:, cc, NT:NTOK], cs[:, 0:NS], pvc(PV_DWB + 16 * j + cc), ALU.add, ["cs", "pv"], [f"cT{cc}_s"])
                    allgather(Utl[j], Ua[j], [f"Utl{cc}" for cc in range(KC)], ["Ua"], f"agu{j}")
                    for cc in range(KC):
                        gather(Uh[:, cc, 0:32], Ua[j], idx[:, cc:cc + 1], ["Ua", "idx"], [f"Uh{cc}"] if cc else [f"Uh{c_}" for c_ in range(KC)], "Uh")
                    P.join("Uh", [f"Uh{cc}" for cc in range(KC)])
                    prb = bass.AP(sg, 0, [list(sg[:, 0, 0:1].ap[0]), [CW, 30], [1, CW]])
                    for cc in range(KC):
                        ts("dve", Uh[:, cc, 0:32], Uh[:, cc, 0:32], pvc(PV_HALO), ALU.mult, [f"Uh{cc}", "pv"], [f"Uh{cc}"])
                        base = Uh[:, cc, 2:3]
                        win_ap = bass.AP(Uh, base.offset, [list(base.ap[0]), [1, 30], [1, CW]])
                        wcol = pvc(dwc + cc * CW, CW)
                        w_ap = bass.AP(pv, wcol.offset, [list(wcol.ap[0]), [0, 30], [1, CW]])
                        tt("dve", prb, win_ap, w_ap, ALU.mult, [f"Uh{cc}", "pv", "sg0", "sg1"], ["sg0", "sg1"])
                        P.op("dve", lambda e, o=cs[:, 0:30], i=prb: e.tensor_reduce(o, i, AX.X, ALU.add), ["sg0", "sg1"], ["cs"])
                        ts("dve", cT[:, cc, 0:30], cs[:, 0:30], pvc(PV_DWB + 16 * j + cc), ALU.add, ["cs", "pv"], [f"cT{cc}_h"])
                    P.barrier()
                    dma("sp", Uh[:, :, 0:32], Utl[j].rearrange("(c p) t -> p c t", p=128), [], [f"Uh{cc}" for cc in range(KC)], "Uh")
                    for cc in range(KC):
                        tb = 6 + nxt("qb", 2)
                        tr(banks[tb][0:32, 0:128], Uh[:, cc, 0:32], ident, [f"Uh{cc}", "cf"], [BK[tb]])
                        cpy("act", st[0:32, cc * 128:(cc + 1) * 128], banks[tb][0:32, 0:128], [BK[tb]], ["st"])
                    dma("sp", oconv[j], st[2:32, :], ["st"], [], "st")
                    for cc in range(KC):
                        tb = 6 + nxt("qb", 2)
                        tr(banks[tb][0:NS, 0:128], Us[:, cc, 30:34], ident, [f"Us{cc}", "cf"], [BK[tb]])
                        cpy("act", st[0:NS, cc * 128:(cc + 1) * 128], banks[tb][0:NS, 0:128], [BK[tb]], ["st"])
                    dma("sp", sconv_o[j, 26:30, :], st[0:NS, :], ["st"], [], "st")
                    P.barrier()
                with Scope(nc, [("Sq16", [128, 2, 512], BF16), ("sg2", [128, 2, 512], F32), ("mean", [128, NTOK], F32), ("rs", [128, NTOK], F32), ("wo2", [128, 2, KC, 128], BF16)]) as (Sq16, sg2, mean, rs, wo2,):
                    def ckeys(cc, gi):
                        return [f"cT{cc}_m", f"cT{cc}_h"] if gi == 0 else ([f"cT{cc}_m"] if gi == 1 else [f"cT{cc}_s"])
                    for gi, (t0, n) in enumerate(TG):
                        b1 = 0 + gi % 2
                        b2 = 2 + gi % 2
                        for cc in range(KC):
                            mm(banks[b1][:, 0:n], ones_bf[:], cT[:, cc, t0:t0 + n], cc == 0, cc == KC - 1,
                               ["ones"] + ckeys(cc, gi), [BK[b1]], inc=True)
                            q2 = nxt("sqc", 2)
                            act(Sq16[:, q2, 0:n], cT[:, cc, t0:t0 + n], AF.Square, ckeys(cc, gi), [f"sq16{q2}"])
                            mm(banks[b2][:, 0:n], ones_bf[:], Sq16[:, q2, 0:n], cc == 0, cc == KC - 1,
                               ["ones", f"sq16{q2}"], [BK[b2]], inc=True)
                        mk = f"mean{gi}"
                        rk = f"rs{gi}"
                        m_ap = mean[:, t0:t0 + n]
                        r_ap = rs[:, t0:t0 + n]
                        ts("dve", m_ap, banks[b1][:, 0:n], 1.0 / D, ALU.mult, [BK[b1]], [mk])
                        tt("dve", r_ap, m_ap, m_ap, ALU.mult, [mk], [rk])
                        stt("dve", r_ap, banks[b2][:, 0:n], 1.0 / D, r_ap, ALU.mult, ALU.subtract, [BK[b2], rk], [rk])
                        act(r_ap, r_ap, AF.Sqrt, [rk, "pv"], [rk], bias=pvc(PV_EPS_LN), scale=1.0)
                        recip(r_ap, [rk], [rk])
                        for cc in range(KC):
                            q2 = nxt("sg2", 2)
                            tmp = sg2[:, q2, 0:n]
                            tt("dve", tmp, cT[:, cc, t0:t0 + n], m_ap, ALU.subtract, ckeys(cc, gi) + [mk], [f"sg2{q2}"])
                            tt("pool", tmp, tmp, r_ap, ALU.mult, [f"sg2{q2}", rk], [f"sg2{q2}"])
                            act(tmp, tmp, AF.Silu, [f"sg2{q2}", "pv"], [f"sg2{q2}"], bias=pvc(PV_LNB + 16 * j + cc),
                                scale=pvc(PV_LNG + 16 * j + cc))
                            tt("dve", hT[:, cc, t0:t0 + n], tmp, szc[:, cc, t0:t0 + n], ALU.mult,
                               [f"sg2{q2}", f"szc{cc}_{gi}"], [f"hT{cc}_{gi}"])
                    for dc in range(16):
                        s = dc % 2
                        load_w(wo2[:, s], wout, 0, KC, dc * 128, 128, f"wo2{s}", f"wo2{s}")
                        for gi, (t0, n) in enumerate(TG):
                            b = 4 + nxt("ldb", 4)
                            for cc in range(KC):
                                mm(banks[b][:, 0:n], wo2[:, s, cc, :], hT[:, cc, t0:t0 + n], cc == 0, cc == KC - 1,
                                   [f"wo2{s}", f"hT{cc}_{gi}"], [BK[b]], inc=(cc == KC - 1))
                            tt("dve", xT[:, dc, t0:t0 + n], xT[:, dc, t0:t0 + n], banks[b][:, 0:n], ALU.add,
                               xkeys(dc, gi) + [BK[b]], xkeys(dc, gi))
                    P.barrier()


        if STAGE >= 1:
            attn_layer(0)
        if STAGE >= 5:
            conv_layer(0)
        if STAGE >= 6:
            attn_layer(1)
        if STAGE >= 7:
            conv_layer(1)

        with nc.sbuf_tensor("xo", [128, 2, D], F32) as xo:
            for t8 in range(9):
                s = nxt("xo", 2)
                n = 128 if t8 < 8 else NS
                gi = 0 if t8 < 4 else (1 if t8 < 8 else 2)
                for k4 in range(4):
                    b = 4 + nxt("ldb", 4)
                    for kk in range(4):
                        kc = k4 * 4 + kk
                        tr(banks[b][0:n, kk * 128:(kk + 1) * 128], xT[:, kc, t8 * 128:t8 * 128 + n], ident,
                           xkeys(kc, gi) + ["cf"], [BK[b]], inc=(kk == 3))
                    cpy("dve" if k4 % 2 == 0 else "act", xo[0:n, s, k4 * 512:(k4 + 1) * 512], banks[b][0:n, :], [BK[b]], [f"xo{s}_{k4}"])
                dst = yp[t8 * 128:(t8 + 1) * 128, :] if t8 < 8 else ys
                dma("sp", dst, xo[0:n, s, :], [f"xo{s}_{k}" for k in range(4)], [], f"xo{s}")
        P.final_waits("sp")

        sems = {}

        def sem_of(k):
            if k not in sems:
                sems[k] = es.enter_context(nc.semaphore(k.replace(":", "_")))
            return sems[k]
        for e in Prog.ENGS:
            sem_of("E:" + e)
        for k in P.dcnt:
            sem_of(k)

        with nc.Block() as block:
            def run(engname):
                def f(eng):
                    for waits, fn, inc in P.q[engname]:
                        for k, v in waits:
                            eng.wait_ge(sems[k], v)
                        if fn is not None:
                            ins = fn(eng)
                            if inc is not None:
                                ins.then_inc(sems[inc[0]], inc[1])
                return f
            block.tensor(run("pe"))
            block.scalar(run("act"))
            block.vector(run("dve"))
            block.gpsimd(run("pool"))
            block.sync(run("sp"))
    return nc


_NC_CACHE = {}


def _host_tables(c):
    pos = c % 4
    r1 = max(pos - 1, 0)
    r2 = max(pos - 2, 0)
    p = np.arange(128)
    idx = np.zeros((128, 112), np.int32)
    for h in range(16):
        idx[:, h] = r1 * 2048 + h * 128 + p
        idx[:, 16 + h] = r1 * 1024 + (h % 8) * 128 + p
        idx[:, 32 + h] = r1 * 512 + (h % 4) * 128 + p
        idx[:, 48 + h] = r2 * 512 + (h % 4) * 128 + p
        idx[:, 64 + h] = (r1 * 16 + h) * 128 + p
        idx[:, 80 + h] = (r1 * 8 + h % 8) * 128 + p
        idx[:, 96 + h] = np.where(p < 64, (r2 * 4 + h % 4) * 64 + p, (r1 * 4 + h % 4) * 64 + (p - 64))
    return idx


def _const_table():
    cf = np.zeros((128, NCF), np.float32)
    k = np.arange(128)[:, None]
    jj = np.arange(256)[None, :]
    dist = (jj - k).astype(np.float32)
    cf[:, CF_D:CF_D + 256] = np.where((dist >= 0) & (dist <= 128), dist, BIG)
    cf[:, CF_DD:CF_DD + 4] = np.where(np.arange(4)[None, :] == k, 0.0, BIG)
    cf[:, CF_ID:CF_ID + 128] = np.eye(128, dtype=np.float32)
    return cf


def _pvec(c, attn_norm, conv_norm, q_gain, k_gain, dw_w, dw_b, ln_g, ln_b):
    pos = c % 4
    pv = np.zeros((128, NPV), np.float32)

    def fm(v):
        return np.ascontiguousarray(v.reshape(16, 128).T)
    for l in range(2):
        pv[:, PV_AN + 16 * l:PV_AN + 16 * l + 16] = fm(attn_norm[l])
        pv[:, PV_CN + 16 * l:PV_CN + 16 * l + 16] = fm(conv_norm[l])
        for g in range(3):
            pv[:, PV_QG + 3 * l + g] = q_gain[l, g]
            pv[:, PV_KG + 3 * l + g] = k_gain[l, g]
        pv[:, PV_DWW + l * 16 * CW:PV_DWW + (l + 1) * 16 * CW] = dw_w[l].T.reshape(16, 128, CW).transpose(1, 0, 2).reshape(128, 16 * CW)
        pv[:, PV_DWB + 16 * l:PV_DWB + 16 * l + 16] = fm(dw_b[l])
        pv[:, PV_LNG + 16 * l:PV_LNG + 16 * l + 16] = fm(ln_g[l])
        pv[:, PV_LNB + 16 * l:PV_LNB + 16 * l + 16] = fm(ln_b[l])
    pv[:, PV_EPS_RMS] = 1e-6
    pv[:, PV_EPS_Q] = 128 * 1e-6
    pv[:, PV_EPS_LN] = 1e-5
    if pos == 0:
        pv[:, PV_LBM:PV_LBM + 3] = NEG
    elif pos == 1:
        pv[0:64, PV_LBM + 2] = NEG
    pv[:, PV_HALO] = 0.0 if pos == 0 else 1.0
    pv[:, PV_ZERO] = 0.0
    return pv


def _make_in_maps(x_prompt, x_sample, cache_kv_w128, cache_kv_w512, cache_kv_w2048, state_conv,
                  attn_norm, attn_w_in, attn_q_gain, attn_k_gain, attn_w_out,
                  conv_norm, conv_w_in, conv_dw_w, conv_dw_b, conv_ln_g, conv_ln_b, conv_w_out):
    f = lambda a: np.ascontiguousarray(np.asarray(a), dtype=np.float32)
    x_prompt, x_sample = f(x_prompt), f(x_sample)
    caches = [f(cache_kv_w128), f(cache_kv_w512), f(cache_kv_w2048)]
    state_conv = f(state_conv)
    attn_w_in, attn_w_out, conv_w_in, conv_w_out = f(attn_w_in), f(attn_w_out), f(conv_w_in), f(conv_w_out)
    attn_norm, conv_norm = f(attn_norm), f(conv_norm)
    attn_q_gain, attn_k_gain = f(attn_q_gain), f(attn_k_gain)
    conv_dw_w, conv_dw_b, conv_ln_g, conv_ln_b = f(conv_dw_w), f(conv_dw_b), f(conv_ln_g), f(conv_ln_b)
    cft = _const_table()
    in_maps = []
    for c in range(NCORES):
        b, pos = c // 4, c % 4
        m = {
            "xp": np.ascontiguousarray(x_prompt[b, pos * NT:(pos + 1) * NT]),
            "xs": np.ascontiguousarray(x_sample[c]),
            "sconv": np.ascontiguousarray(state_conv[:, c]),
            "attn_w_in": attn_w_in, "attn_w_out": attn_w_out, "conv_w_in": conv_w_in, "conv_w_out": conv_w_out,
            "pvec": _pvec(c, attn_norm, conv_norm, attn_q_gain, attn_k_gain, conv_dw_w, conv_dw_b, conv_ln_g, conv_ln_b),
            "cft": cft, "idxt": _host_tables(c),
        }
        for g in range(3):
            m[f"ck{g}"] = np.ascontiguousarray(caches[g][:, c]).reshape(2, GROUPS[g][0], 2 * D)
        in_maps.append(m)
    return in_maps


def _assemble(res):
    y_prompt = np.stack([np.concatenate([res[4 * b + p]["yp"] for p in range(4)], axis=0) for b in range(2)])
    y_sample = np.stack([res[c]["ys"] for c in range(NCORES)])
    kvp = []
    for g in range(3):
        win = GROUPS[g][0]
        per_b = []
        for b in range(2):
            if g < 2:
                a = res[4 * b + 3][f"okv{g}"]
            else:
                a = np.concatenate([res[4 * b + 2]["okv2"], res[4 * b + 3]["okv2"]], axis=1)
            per_b.append(a.reshape(2, win, 2, 16, 128))
        kvp.append(np.stack(per_b, axis=1))
    conv_p = np.stack([res[4 * b + 3]["oconv"] for b in range(2)], axis=1)
    kvs = [np.stack([res[c][f"skv{g}"].reshape(2, GROUPS[g][0], 2, 16, 128) for c in range(NCORES)], axis=1) for g in range(3)]
    conv_s = np.stack([res[c]["sconv_o"] for c in range(NCORES)], axis=1)
    out = (y_prompt, y_sample, kvp[0], kvp[1], kvp[2], conv_p, kvs[0], kvs[1], kvs[2], conv_s)
    return tuple(np.ascontiguousarray(o, dtype=np.float32) for o in out)


def kernel(**inputs):
    if "nc" not in _NC_CACHE:
        _NC_CACHE["nc"] = build_nc()
    nc = _NC_CACHE["nc"]
    in_maps = _make_in_maps(**inputs)
    res = run_bass_kernel_spmd(nc, in_maps, core_ids=list(range(NCORES))).results
    return _assemble(res)
```
